# Optimizing a Trainium2 kernel written in Bass

```python
import jax, jax.numpy as jnp
from jax import lax
import numpy as np

D_MODEL = 1024
BATCH = 1
SEQ = 16384
DEPTH = 1

CHUNK = 64
Q_BLOCK = 128
N_MEM = 256
EPS = 1e-6

GLA_HEADS = 4
GLA_DK = D_MODEL // 16
GLA_DV = D_MODEL // 8
GLA_LOWRANK = 16
GLA_TAU = 16.0
FOX_HEADS = 8
FOX_DH = D_MODEL // 16
MEM_HEADS = 4
MEM_DH = D_MODEL // 8
D_FF = 4 * D_MODEL
N_BRANCH = 3

GLA_K = GLA_HEADS * GLA_DK
GLA_V = GLA_HEADS * GLA_DV
FOX_W = FOX_HEADS * FOX_DH
MEM_W = MEM_HEADS * MEM_DH
PROJ_SIZES = (GLA_K, GLA_K, GLA_V, GLA_V, GLA_LOWRANK, FOX_W, FOX_W, FOX_W, FOX_HEADS, MEM_W, N_BRANCH * D_MODEL)
D_IN = 2 * GLA_K + 2 * GLA_V + GLA_LOWRANK + 3 * FOX_W + FOX_HEADS + MEM_W + N_BRANCH * D_MODEL

kernel_name = "hybrid_gla_fox_memory_gated_block"


def rmsnorm(x, g):
    xf = x.astype(jnp.float32)
    r = lax.rsqrt(jnp.mean(xf * xf, axis=-1, keepdims=True) + EPS)
    return (xf * r).astype(x.dtype) * g


def split_cols(t, sizes):
    offs = np.cumsum(np.array(sizes))[:-1].tolist()
    return jnp.split(t, offs, axis=-1)


def gla_chunk_causal(q, k, v, log_a):
    B, S, H, DK = q.shape
    DV = v.shape[-1]
    N = S // CHUNK
    f32 = jnp.float32
    qc = q.reshape(B, N, CHUNK, H, DK).astype(f32)
    kc = k.reshape(B, N, CHUNK, H, DK).astype(f32)
    vc = v.reshape(B, N, CHUNK, H, DV).astype(f32)
    b = jnp.cumsum(log_a.reshape(B, N, CHUNK, H, DK).astype(f32), axis=2)
    b_last = b[:, :, -1:]
    e_pos = jnp.exp(b)
    e_neg = jnp.exp(-b)
    q_pos = qc * e_pos
    a_causal = jnp.einsum('bnthd,bnshd->bnhts', q_pos, kc * e_neg)
    a_anti = jnp.einsum('bnthd,bnshd->bnhts', qc * e_neg, kc * e_pos)
    t_idx = jnp.arange(CHUNK)
    lower = t_idx[:, None] >= t_idx[None, :]
    attn = jnp.where(lower, a_causal, a_anti)
    o_intra = jnp.einsum('bnhts,bnshv->bnthv', attn, vc)
    chunk_kv = jnp.einsum('bnshd,bnshv->bnhdv', kc * jnp.exp(b_last - b), vc)
    chunk_decay = jnp.exp(b_last[:, :, 0])

    def step(state, inp):
        kv, dec = inp
        return state * dec[..., None] + kv, state

    init = jnp.zeros((B, H, DK, DV), f32)
    _, prev = lax.scan(step, init, (jnp.moveaxis(chunk_kv, 1, 0), jnp.moveaxis(chunk_decay, 1, 0)))
    prev = jnp.moveaxis(prev, 0, 1)
    o_inter = jnp.einsum('bnthd,bnhdv->bnthv', q_pos, prev)
    return (o_intra + o_inter).reshape(B, S, H, DV).astype(v.dtype)


def forgetting_attention(q, k, v, log_f):
    B, S, H, Dh = q.shape
    nb = S // Q_BLOCK
    scale = Dh ** -0.5
    F = jnp.cumsum(log_f.astype(jnp.float32), axis=1).transpose(0, 2, 1)
    kh = k.transpose(0, 2, 1, 3)
    vh = v.transpose(0, 2, 1, 3)
    qb = q.reshape(B, nb, Q_BLOCK, H, Dh).transpose(1, 0, 3, 2, 4)
    Fq = F.reshape(B, H, nb, Q_BLOCK).transpose(2, 0, 1, 3)
    k_pos = jnp.arange(S)

    def block(args):
        qi, Fi, i = args
        s = jnp.einsum('bhqd,bhkd->bhqk', qi, kh).astype(jnp.float32) * scale
        s = s + Fi[..., None] - F[:, :, None, :]
        q_pos = i * Q_BLOCK + jnp.arange(Q_BLOCK)
        mask = k_pos[None, :] <= q_pos[:, None]
        p = jax.nn.softmax(jnp.where(mask, s, -jnp.inf), axis=-1)
        return jnp.einsum('bhqk,bhkd->bhqd', p.astype(vh.dtype), vh)

    out = lax.map(block, (qb, Fq, jnp.arange(nb)))
    return out.transpose(1, 0, 3, 2, 4).reshape(B, S, H, Dh)


def memory_attention(q, mk, mv):
    scale = q.shape[-1] ** -0.5
    s = jnp.einsum('bshd,bmhd->bhsm', q, mk).astype(jnp.float32) * scale
    p = jax.nn.softmax(s, axis=-1)
    return jnp.einsum('bhsm,bmhd->bshd', p.astype(mv.dtype), mv)


def setup_inputs(seed: int = 0) -> dict:
    key = jax.random.key(seed)
    ks = jax.random.split(key, 20)
    f32 = jnp.float32

    def nrm(k, shape, fan_in):
        return jax.random.normal(k, shape, f32) * (fan_in ** -0.5)

    def gain(k, shape):
        return 1.0 + 0.02 * jax.random.normal(k, shape, f32)

    L = DEPTH
    return {
        "x": jax.random.normal(ks[0], (BATCH, SEQ, D_MODEL), f32),
        "mem": jax.random.normal(ks[1], (BATCH, N_MEM, D_MODEL), f32),
        "g_mix": gain(ks[2], (L, D_MODEL)),
        "w_in": nrm(ks[3], (L, D_MODEL, D_IN), D_MODEL),
        "w_alpha_up": nrm(ks[4], (L, GLA_LOWRANK, GLA_K), GLA_LOWRANK),
        "b_alpha": 0.02 * jax.random.normal(ks[5], (L, GLA_K), f32),
        "b_forget": jax.random.uniform(ks[6], (L, FOX_HEADS), f32, 1.0, 5.0),
        "g_gla_head": gain(ks[7], (L, GLA_HEADS, GLA_DV)),
        "g_mem": gain(ks[8], (L, D_MODEL)),
        "w_mem_kv": nrm(ks[9], (L, D_MODEL, 2 * MEM_W), D_MODEL),
        "w_gla_o": nrm(ks[10], (L, GLA_V, D_MODEL), GLA_V),
        "w_fox_o": nrm(ks[11], (L, FOX_W, D_MODEL), FOX_W),
        "w_mem_o": nrm(ks[12], (L, MEM_W, D_MODEL), MEM_W),
        "w_out": nrm(ks[13], (L, D_MODEL, D_MODEL), D_MODEL),
        "g_ffn": gain(ks[14], (L, D_MODEL)),
        "w_ff1": nrm(ks[15], (L, D_MODEL, D_FF), D_MODEL),
        "w_ff2": nrm(ks[16], (L, D_FF, D_MODEL), D_FF),
        "g_final": gain(ks[17], (D_MODEL,)),
    }


def reference(x, mem, g_mix, w_in, w_alpha_up, b_alpha, b_forget, g_gla_head, g_mem, w_mem_kv,
              w_gla_o, w_fox_o, w_mem_o, w_out, g_ffn, w_ff1, w_ff2, g_final):
    B, S, D = x.shape
    M = mem.shape[1]
    h = x
    for l in range(DEPTH):
        u = rmsnorm(h, g_mix[l])
        proj = u @ w_in[l]
        (gq, gk, gv, gg, ga, fq, fk, fv, ff, mq, gates) = split_cols(proj, PROJ_SIZES)

        log_a = jax.nn.log_sigmoid(ga @ w_alpha_up[l] + b_alpha[l]) / GLA_TAU
        o_gla = gla_chunk_causal(
            gq.reshape(B, S, GLA_HEADS, GLA_DK) * (GLA_DK ** -0.5),
            gk.reshape(B, S, GLA_HEADS, GLA_DK),
            gv.reshape(B, S, GLA_HEADS, GLA_DV),
            log_a.reshape(B, S, GLA_HEADS, GLA_DK))
        o_gla = rmsnorm(o_gla, g_gla_head[l]) * jax.nn.silu(gg.reshape(B, S, GLA_HEADS, GLA_DV))
        y_gla = o_gla.reshape(B, S, GLA_V) @ w_gla_o[l]

        log_f = jax.nn.log_sigmoid(ff + b_forget[l])
        o_fox = forgetting_attention(
            fq.reshape(B, S, FOX_HEADS, FOX_DH),
            fk.reshape(B, S, FOX_HEADS, FOX_DH),
            fv.reshape(B, S, FOX_HEADS, FOX_DH),
            log_f)
        y_fox = o_fox.reshape(B, S, FOX_W) @ w_fox_o[l]

        mkv = rmsnorm(mem, g_mem[l]) @ w_mem_kv[l]
        mk, mv = jnp.split(mkv, 2, axis=-1)
        o_mem = memory_attention(
            mq.reshape(B, S, MEM_HEADS, MEM_DH),
            mk.reshape(B, M, MEM_HEADS, MEM_DH),
            mv.reshape(B, M, MEM_HEADS, MEM_DH))
        y_mem = o_mem.reshape(B, S, MEM_W) @ w_mem_o[l]

        gt = jax.nn.sigmoid(gates.reshape(B, S, N_BRANCH, D))
        merged = gt[:, :, 0] * y_gla + gt[:, :, 1] * y_fox + gt[:, :, 2] * y_mem
        h = h + merged @ w_out[l]

        u2 = rmsnorm(h, g_ffn[l])
        h = h + jnp.square(jax.nn.relu(u2 @ w_ff1[l])) @ w_ff2[l]
    return rmsnorm(h, g_final)
```

```python
import numpy as np
import ml_dtypes
from contextlib import ExitStack
import concourse.bass as bass
import concourse.mybir as mybir
from concourse.bass_utils import run_bass_kernel_spmd

F32 = mybir.dt.float32
BF16 = mybir.dt.bfloat16
I32 = mybir.dt.int32
AF = mybir.ActivationFunctionType
ALU = mybir.AluOpType

NCORE = 8
D = 1024
S = 16384
T = S // NCORE
NT = T // 128
NQ = T // 512
EPS = 1e-6
D_IN = 6680
O_GQ, O_GK, O_GV, O_GG, O_GA = 0, 256, 512, 1024, 1536
O_FQ, O_FK, O_FV, O_FF, O_MQ, O_GATES = 1552, 2064, 2576, 3088, 3096, 3608
PAY_R = 193
NDS = 40

C_IDENT, C_TRI, C_SU, C_G4, C_TRIB, C_UPB, C_ONES, C_U2, C_L2, C_SU4 = (
    0, 128, 256, 384, 512, 640, 768, 896, 1408, 1920)
C_SEL = 1952
C_TOT = 2016
B_IDENT, B_ONES, B_MASK = 0, 128, 256
B_TOT = 256 + 4 * 512


def _consts():
    p = np.arange(128)
    cf = np.zeros((128, C_TOT), np.float32)
    cf[:, C_IDENT:C_IDENT + 128] = np.eye(128)
    cf[:, C_TRI:C_TRI + 128] = (p[:, None] <= p[None, :])
    cf[:, C_SU:C_SU + 128] = (p[:, None] < p[None, :])
    cf[:, C_G4:C_G4 + 128] = (p[:, None] < p[None, :]) & ((p[:, None] // 4) == (p[None, :] // 4))
    same = (p[:, None] // 64) == (p[None, :] // 64)
    cf[:, C_TRIB:C_TRIB + 128] = np.where(same & (p[:, None] <= p[None, :]), -1.0 / 16.0, 0.0)
    cf[:, C_UPB:C_UPB + 128] = np.where(same & (p[:, None] > p[None, :]), -1.0 / 16.0, 0.0)
    cf[:, C_ONES:C_ONES + 128] = 1.0
    u2 = (same & (p[None, :] >= p[:, None])).astype(np.float32)
    l2 = (same & (p[None, :] < p[:, None])).astype(np.float32)
    cf[:, C_U2:C_U2 + 512] = np.tile(u2, (1, 4))
    cf[:, C_L2:C_L2 + 512] = np.tile(l2, (1, 4))
    cf[:, C_SU4:C_SU4 + 32] = (p[:, None] < 4 * np.arange(32)[None, :])
    cf[64, C_SEL:C_SEL + 64] = 1.0
    cb = np.zeros((128, B_TOT), np.float32)
    cb[:, B_IDENT:B_IDENT + 128] = np.eye(128)
    cb[:, B_ONES:B_ONES + 128] = 1.0
    cq = np.arange(512)
    for j in range(4):
        cb[:, B_MASK + j * 512:B_MASK + (j + 1) * 512] = np.where(
            (128 * j + p[:, None]) > cq[None, :], -30000.0, 0.0)
    return cf, cb.astype(ml_dtypes.bfloat16)


class Buf:
    __slots__ = ("w", "r")

    def __init__(self):
        self.w = None
        self.r = {}


class KB:
    def __init__(self, nc, es):
        self.nc = nc
        self.eng = {"pe": nc.tensor, "act": nc.scalar, "dve": nc.vector, "pool": nc.gpsimd,
                    "sp": nc.sync}
        self.psem = {e: es.enter_context(nc.semaphore("p_" + e)) for e in ("pe", "act", "dve", "pool")}
        self.cnt = {e: 0 for e in self.psem}
        self.dsem = [es.enter_context(nc.semaphore("d%d" % i)) for i in range(NDS)]
        self.dval = [0] * NDS
        self.dnext = 0
        self.csems = [es.enter_context(nc.semaphore("cc%d" % i)) for i in range(4)]
        self.cidx = 0
        self.seen = {e: {} for e in self.eng}

    def _sem(self, key):
        if isinstance(key, tuple) and key[0] == "c":
            return self.csems[key[1]]
        return self.psem[key] if isinstance(key, str) else self.dsem[key[1]]

    def wait(self, e, tok):
        if tok is None:
            return
        key, val = tok
        if self.seen[e].get(key, 0) >= val:
            return
        self.eng[e].wait_ge(self._sem(key), val)
        self.seen[e][key] = val

    def _deps(self, e, reads, writes):
        for b in reads:
            self.wait(e, b.w)
        for b in writes:
            self.wait(e, b.w)
            for k, v in b.r.items():
                self.wait(e, (k, v))

    def _commit(self, tok, reads, writes):
        for b in writes:
            b.w = tok
            b.r = {}
        for b in reads:
            if b.r.get(tok[0], 0) < tok[1]:
                b.r[tok[0]] = tok[1]

    def op(self, e, fn, reads=(), writes=()):
        self._deps(e, reads, writes)
        ins = fn(self.eng[e])
        self.cnt[e] += 1
        ins.then_inc(self.psem[e], 1)
        tok = (e, self.cnt[e])
        self._commit(tok, reads, writes)
        return tok

    def dma(self, q, out, in_, reads=(), writes=(), **kw):
        self._deps(q, reads, writes)
        i = self.dnext
        self.dnext = (i + 1) % NDS
        if self.dval[i] > 0:
            self.wait(q, (("d", i), self.dval[i]))
        ins = self.eng[q].dma_start(out=out, in_=in_, **kw)
        self.dval[i] += 16
        ins.then_inc(self.dsem[i], 16)
        tok = (("d", i), self.dval[i])
        self._commit(tok, reads, writes)
        return tok

    def collective(self, in_t, out_t, reads=(), writes=()):
        self._deps("pool", reads, writes)
        ins = self.nc.gpsimd.collective_compute(
            "AllGather", ALU.bypass, replica_groups=[list(range(NCORE))],
            ins=[in_t.ap().opt()], outs=[out_t.ap().opt()])
        ins.then_inc(self.csems[self.cidx])
        tok = (("c", self.cidx), 1)
        self.cidx += 1
        self._commit(tok, reads, writes)
        return tok

    def barrier(self):
        for e in self.eng:
            for e2 in self.psem:
                if e2 != e and self.cnt[e2] > 0:
                    self.wait(e, (e2, self.cnt[e2]))
            for i in range(NDS):
                if self.dval[i] > 0:
                    self.wait(e, (("d", i), self.dval[i]))
            for i in range(self.cidx):
                self.wait(e, (("c", i), 1))


def build_nc():
    import os
    STOP = os.environ.get("KSTOP", "")
    nc = bass.Bass("TRN2", target_bir_lowering=False)
    dt = nc.dram_tensor
    x_d = dt("x", [T, D], F32, kind="ExternalInput").ap()
    mem_d = dt("mem", [256 + 1, D], F32, kind="ExternalInput").ap()
    g_mix_d = dt("g_mix", [1, D], F32, kind="ExternalInput").ap()
    w_in_d = dt("w_in", [D + 1, D_IN], F32, kind="ExternalInput").ap()
    w_up_d = dt("w_alpha_up", [16, 256], F32, kind="ExternalInput").ap()
    b_alpha_d = dt("b_alpha", [1, 256], F32, kind="ExternalInput").ap()
    b_forget_d = dt("b_forget", [8, 1], F32, kind="ExternalInput").ap()
    g_gla_d = dt("g_gla_head", [4, 128], F32, kind="ExternalInput").ap()
    g_mem_d = dt("g_mem", [1, D], F32, kind="ExternalInput").ap()
    w_mkv_d = dt("w_mem_kv", [D + 1, 1024], F32, kind="ExternalInput").ap()
    w_o_d = [dt(n, [512 + 1, D], F32, kind="ExternalInput").ap() for n in ("w_gla_o", "w_fox_o", "w_mem_o")]
    w_out_d = dt("w_out", [D + 1, D], F32, kind="ExternalInput").ap()
    g_ffn_d = dt("g_ffn", [1, D], F32, kind="ExternalInput").ap()
    w_ff1_d = dt("w_ff1", [D + 1, 4096], F32, kind="ExternalInput").ap()
    w_ff2_d = dt("w_ff2", [4096 + 1, D], F32, kind="ExternalInput").ap()
    g_fin_d = dt("g_final", [1, D], F32, kind="ExternalInput").ap()
    cf_d = dt("cf", [128 + 1, C_TOT], F32, kind="ExternalInput").ap()
    cb_d = dt("cb", [128, B_TOT], BF16, kind="ExternalInput").ap()
    cid_d = dt("cid", [1, 2], I32, kind="ExternalInput").ap()
    cmask_d = dt("cmask", [128, 8], F32, kind="ExternalInput").ap()
    out_d = dt("out", [T, D], F32, kind="ExternalOutput").ap()

    pay_t = dt("pay", [8 * PAY_R, T], BF16)
    gat_t = dt("gat", [NCORE * 8 * PAY_R, T], BF16)
    payf_t = dt("payf", [8, T], F32)
    gatf_t = dt("gatf", [NCORE * 8, T], F32)
    gs_t = dt("gs", [128, 516], F32)
    gsg_t = dt("gsg", [NCORE * 128, 516], F32)
    pay2_t = dt("pay2", [64, S], BF16)
    gat2_t = dt("gat2", [NCORE * 64, S], BF16)
    cd_t = dt("cd", [128, 128], BF16)
    pay, gat, payf, gatf = pay_t.ap(), gat_t.ap(), payf_t.ap(), gatf_t.ap()
    gs, gsg, pay2, gat2, cd = gs_t.ap(), gsg_t.ap(), pay2_t.ap(), gat2_t.ap(), cd_t.ap()

    es = ExitStack()
    with es:
        kb = KB(nc, es)
        op, dma = kb.op, kb.dma

        def sb(name, shape, dtype, stack=es, side=None):
            return stack.enter_context(nc.sbuf_tensor("s_" + name, shape, dtype, side=side)), Buf()

        PS = []
        for i in range(7):
            PS.append((es.enter_context(nc.psum_tensor("ps%d" % i, [128, 512], F32)), Buf()))
        PST = (es.enter_context(nc.psum_tensor("pst", [128, 1024], BF16)), Buf())

        cf, cf_b = sb("cf_sb", [128, C_TOT], F32)
        cb, cb_b = sb("cb_sb", [128, B_TOT], BF16)
        uT, uT_b = sb("uT", [128, 8, T], BF16)
        es_r = ExitStack()
        omemT, omemT_b = sb("omemT", [128, 4, T], BF16, es_r, "right")
        oglaT, oglaT_b = sb("oglaT", [128, 4, T], BF16, es_r, "right")
        dma("sp", cf[:, :], cf_d[0:128, :], writes=[cf_b])
        dma("sp", cb[:, :], cb_d[:, :], writes=[cb_b])
        ident_f = cf[:, C_IDENT:C_IDENT + 128]
        ones_f = cf[:, C_ONES:C_ONES + 128]
        ident_b = cb[:, B_IDENT:B_IDENT + 128]
        ones_b = cb[:, B_ONES:B_ONES + 128]

        r_cid = nc.sync.alloc_register("r_cid")
        r_off = nc.sync.alloc_register("r_off")
        nc.sync.reg_load(r_cid, cid_d[0:1, 0:1])
        nc.sync.reg_load(r_off, cid_d[0:1, 1:2])
        v_cid = nc.sync.snap(r_cid, min_val=0, max_val=NCORE - 1)
        v_off = nc.sync.snap(r_off, min_val=0, max_val=S - T)

        def load_w(dst, src, dbuf):
            return dma("pool", dst, src, writes=[dbuf])

        def rmsnorm_T(src_rows, ntiles, gvec_d, dstT, dstT_b, ph, tag):
            gm, gm_b = sb(tag + "gm", [128, 8], F32, ph)
            with nc.allow_non_contiguous_dma(reason="tiny gain vector transpose"):
                dma("sp", gm[:, :], gvec_d.rearrange("o (dc p) -> p (o dc)", p=128), writes=[gm_b])
            xt = [sb(tag + "xt%d" % i, [128, D], F32, ph) for i in range(2)]
            xn = [sb(tag + "xn%d" % i, [128, D], BF16, ph) for i in range(2)]
            junk, junk_b = sb(tag + "junk", [128, D], BF16, ph)
            ss, ss_b = sb(tag + "ss", [128, 16], F32, ph)
            rr, rr_b = sb(tag + "rr", [128, 16], F32, ph)
            for ti in range(ntiles):
                xt_t, xt_b = xt[ti % 2]
                xn_t, xn_b = xn[ti % 2]
                dma("sp", xt_t[:, :], src_rows[ti * 128:(ti + 1) * 128, :], writes=[xt_b])
                op("act", lambda e: e.activation(out=junk[:, :], in_=xt_t[:, :], func=AF.Square,
                                                 accum_out=ss[:, ti:ti + 1]),
                   reads=[xt_b], writes=[junk_b, ss_b])
                op("dve", lambda e: e.tensor_scalar(out=rr[:, ti:ti + 1], in0=ss[:, ti:ti + 1],
                                                    scalar1=1.0 / D, scalar2=EPS, op0=ALU.mult, op1=ALU.add),
                   reads=[ss_b], writes=[rr_b])
                op("act", lambda e: e.activation(out=rr[:, ti:ti + 1], in_=rr[:, ti:ti + 1], func=AF.Ln), reads=[rr_b], writes=[rr_b])
                op("act", lambda e: e.activation(out=rr[:, ti:ti + 1], in_=rr[:, ti:ti + 1], func=AF.Exp, scale=-0.5),
                   reads=[rr_b], writes=[rr_b])
                op("dve", lambda e: e.tensor_scalar_mul(out=xn_t[:, :], in0=xt_t[:, :], scalar1=rr[:, ti:ti + 1]),
                   reads=[xt_b, rr_b], writes=[xn_b])

                def tr(e):
                    ins = None
                    for dc in range(8):
                        ins = e.transpose(out=PST[0][:, dc * 128:(dc + 1) * 128],
                                          in_=xn_t[:, dc * 128:(dc + 1) * 128], identity=ident_b)
                    return ins
                op("pe", tr, reads=[xn_b, cb_b], writes=[PST[1]])
                for dc in range(8):
                    if dc % 2 == 0:
                        op("dve", lambda e: e.tensor_scalar_mul(
                            out=dstT[:, dc, ti * 128:(ti + 1) * 128], in0=PST[0][:, dc * 128:(dc + 1) * 128],
                            scalar1=gm[:, dc:dc + 1]), reads=[PST[1], gm_b], writes=[dstT_b])
                    else:
                        op("act", lambda e: e.activation(
                            out=dstT[:, dc, ti * 128:(ti + 1) * 128], in_=PST[0][:, dc * 128:(dc + 1) * 128],
                            func=AF.Copy, scale=gm[:, dc:dc + 1]), reads=[PST[1], gm_b], writes=[dstT_b])

        def proj_T(ps, W, c0, m, rhs_fn, nk=8):
            def f(e):
                ins = None
                for k in range(nk):
                    ins = e.matmul(ps, lhsT=W[:, k, c0:c0 + m], rhs=rhs_fn(k), start=(k == 0), stop=(k == nk - 1))
                return ins
            return f

        with ExitStack() as ph:
            rmsnorm_T(x_d, NT, g_mix_d, uT, uT_b, ph, "a")
        kb.barrier()
        if STOP == "T1":
            return nc

        with ExitStack() as ph:
            Wf, Wf_b = sb("Wf", [128, 8, 1544], BF16, ph)
            for dc in range(8):
                load_w(Wf[:, dc, 0:1536], w_in_d[dc * 128:(dc + 1) * 128, O_FQ:O_FQ + 1536], Wf_b)
                load_w(Wf[:, dc, 1536:1544], w_in_d[dc * 128:(dc + 1) * 128, O_FF:O_FF + 8], Wf_b)
            nbf, nbf_b = sb("nbf", [8, 1], F32, ph)
            dma("sp", nbf[:, :], b_forget_d[:, :], writes=[nbf_b])
            op("dve", lambda e: e.tensor_scalar_mul(out=nbf[:, :], in0=nbf[:, :], scalar1=-1.0),
               reads=[nbf_b], writes=[nbf_b])
            stg = [sb("stg%d" % i, [128, 512], BF16, ph) for i in range(3)]
            lfT, lfT_b = sb("lfT", [8, T], F32, ph)
            etmp, etmp_b = sb("etmp", [8, 512], F32, ph)
            si = 0
            pi = 0
            pay3 = pay.rearrange("(h r) c -> h r c", r=PAY_R)
            payv = pay3[:, 128:193, :].rearrange("h r c -> h (r c)").rearrange("h (p k c) -> p h k c", p=128, k=16)
            stgv = [sb("stgv%d" % i, [128, 8, 65], BF16, ph) for i in range(2)]
            for sv, sv_b in stgv:
                op("dve", lambda e: e.memset(sv[:, :, :], 1.0), writes=[sv_b])
            pay_b = Buf()
            payf_b = Buf()
            for qt in range(NQ):
                ts = slice(qt * 512, (qt + 1) * 512)
                for which in range(2):
                    for hp in range(4):
                        ps, ps_b = PS[pi % 4]
                        pi += 1
                        st, st_b = stg[si % 3]
                        si += 1
                        op("pe", proj_T(ps[:, :], Wf, which * 512 + hp * 128, 128, lambda k: uT[:, k, ts]),
                           reads=[Wf_b, uT_b], writes=[ps_b])
                        op("act", lambda e: e.activation(out=st[:, :], in_=ps[:, :], func=AF.Copy,
                                                         scale=(0.125 if which == 0 else 1.0)),
                           reads=[ps_b], writes=[st_b])
                        for a in range(2):
                            h = 2 * hp + a
                            dma("sp", pay3[h, which * 64:(which + 1) * 64, ts], st[a * 64:(a + 1) * 64, :],
                                reads=[st_b], writes=[pay_b])
                for tt in range(4):
                    tk = slice(qt * 512 + tt * 128, qt * 512 + (tt + 1) * 128)
                    ps, ps_b = PS[pi % 4]
                    pi += 1
                    st, st_b = stgv[tt % 2]

                    def fv(e):
                        ins = None
                        for k in range(8):
                            ins = e.matmul(ps[:, :], lhsT=uT[:, k, tk], rhs=Wf[:, k, 1024:1536],
                                           start=(k == 0), stop=(k == 7))
                        return ins
                    op("pe", fv, reads=[Wf_b, uT_b], writes=[ps_b])
                    op("dve", lambda e: e.tensor_copy(out=st[:, :, 0:64], in_=ps[:, :].rearrange("p (h b) -> p h b", b=64)),
                       reads=[ps_b], writes=[st_b])
                    dma("sp", payv[:, :, qt * 4 + tt, :], st[:, :, :], reads=[st_b], writes=[pay_b])
                ps, ps_b = PS[pi % 4]
                pi += 1
                op("pe", proj_T(ps[0:8, :], Wf, 1536, 8, lambda k: uT[:, k, ts]), reads=[Wf_b, uT_b], writes=[ps_b])
                op("act", lambda e: e.activation(out=etmp[:, :], in_=ps[0:8, :], func=AF.Exp, bias=nbf[:, 0:1], scale=-1.0),
                   reads=[ps_b, nbf_b], writes=[etmp_b])
                op("act", lambda e: e.activation(out=etmp[:, :], in_=etmp[:, :], func=AF.Ln, bias=1.0, scale=1.0),
                   reads=[etmp_b], writes=[etmp_b])
                op("dve", lambda e: e.tensor_scalar_mul(out=lfT[:, ts], in0=etmp[:, :], scalar1=-1.0),
                   reads=[etmp_b], writes=[lfT_b])
            dma("sp", payf[:, :], lfT[:, :], reads=[lfT_b], writes=[payf_b])
            gat_b = Buf()
            gatf_b = Buf()
            kb.collective(pay_t, gat_t, reads=[pay_b], writes=[gat_b])
            kb.collective(payf_t, gatf_t, reads=[payf_b], writes=[gatf_b])
        kb.barrier()
        if STOP == "T2":
            return nc

        with ExitStack() as pm:
            mnT, mnT_b = sb("mnT", [128, 8, 256], BF16, pm)
            rmsnorm_T(mem_d, 2, g_mem_d, mnT, mnT_b, pm, "m")
            Wkv, Wkv_b = sb("Wkv", [128, 8, 1024], BF16, pm)
            Wmq, Wmq_b = sb("Wmq", [128, 8, 512], BF16, pm)
            for dc in range(8):
                load_w(Wkv[:, dc, :], w_mkv_d[dc * 128:(dc + 1) * 128, :], Wkv_b)
                load_w(Wmq[:, dc, :], w_in_d[dc * 128:(dc + 1) * 128, O_MQ:O_MQ + 512], Wmq_b)
            mkT, mkT_b = sb("mkT", [128, 4, 256], BF16, pm)
            mv, mv_b = sb("mv", [128, 2, 512], BF16, pm)
            mq, mq_b = sb("mq", [128, 512], BF16, pm)
            pT = [sb("pT%d" % i, [128, 512], BF16, pm) for i in range(2)]
            rd, rd_b = sb("rd", [128, 512], F32, pm)
            for h in range(4):
                ps, ps_b = PS[h % 4]
                op("pe", proj_T(ps[:, 0:256], Wkv, h * 128, 128, lambda k: mnT[:, k, :]),
                   reads=[Wkv_b, mnT_b], writes=[ps_b])
                op("act", lambda e: e.activation(out=mkT[:, h, :], in_=ps[:, 0:256], func=AF.Copy),
                   reads=[ps_b], writes=[mkT_b])
            for mt in range(2):
                ps, ps_b = PS[mt]

                def fmv(e):
                    ins = None
                    for k in range(8):
                        ins = e.matmul(ps[:, :], lhsT=mnT[:, k, mt * 128:(mt + 1) * 128], rhs=Wkv[:, k, 512:1024],
                                       start=(k == 0), stop=(k == 7))
                    return ins
                op("pe", fmv, reads=[Wkv_b, mnT_b], writes=[ps_b])
                op("act", lambda e: e.activation(out=mv[:, mt, :], in_=ps[:, :], func=AF.Copy),
                   reads=[ps_b], writes=[mv_b])
            for qt in range(NQ):
                ts = slice(qt * 512, (qt + 1) * 512)
                for h in range(4):
                    ps, ps_b = PS[0]
                    op("pe", proj_T(ps[:, :], Wmq, h * 128, 128, lambda k: uT[:, k, ts]),
                       reads=[Wmq_b, uT_b], writes=[ps_b])
                    op("dve", lambda e: e.tensor_scalar_mul(out=mq[:, :], in0=ps[:, :], scalar1=float(128 ** -0.5)),
                       reads=[ps_b], writes=[mq_b])
                    for mt in range(2):
                        pss, pss_b = PS[1 + mt]
                        p_t, p_b = pT[mt]
                        op("pe", lambda e: e.matmul(pss[:, :], lhsT=mkT[:, h, mt * 128:(mt + 1) * 128], rhs=mq[:, :],
                                                    start=True, stop=True), reads=[mkT_b, mq_b], writes=[pss_b])
                        op("act", lambda e: e.activation(out=p_t[:, :], in_=pss[:, :], func=AF.Exp),
                           reads=[pss_b], writes=[p_b])
                    pso, pso_b = PS[3]
                    psd, psd_b = PS[4]

                    def fo(e):
                        ins = None
                        for mt in range(2):
                            e.matmul(pso[:, :], lhsT=mv[:, mt, h * 128:(h + 1) * 128], rhs=pT[mt][0][:, :],
                                     start=(mt == 0), stop=(mt == 1))
                        for mt in range(2):
                            ins = e.matmul(psd[:, :], lhsT=ones_b, rhs=pT[mt][0][:, :], start=(mt == 0), stop=(mt == 1))
                        return ins
                    op("pe", fo, reads=[mv_b, pT[0][1], pT[1][1], cb_b], writes=[pso_b, psd_b])
                    op("dve", lambda e: e.reciprocal(out=rd[:, :], in_=psd[:, :]), reads=[psd_b], writes=[rd_b])
                    op("dve", lambda e: e.tensor_tensor(out=omemT[:, h, ts], in0=pso[:, :], in1=rd[:, :], op=ALU.mult),
                       reads=[pso_b, rd_b], writes=[omemT_b])
        kb.barrier()
        if STOP == "Tmem":
            return nc

        gsg_b = Buf()
        with ExitStack() as ph:
            Wg, Wg_b = sb("Wg", [128, 8, 1552], BF16, ph)
            for dc in range(8):
                load_w(Wg[:, dc, :], w_in_d[dc * 128:(dc + 1) * 128, 0:1552], Wg_b)
            wup, wup_b = sb("wup", [128, 256], BF16, ph)
            op("dve", lambda e: e.memset(wup[:, :], 0.0), writes=[wup_b])
            load_w(wup[0:16, :], w_up_d[:, :], wup_b)
            balp, balp_b = sb("balp", [128, 256], F32, ph)
            dma("sp", balp[:, :], b_alpha_d.partition_broadcast(128), writes=[balp_b])
            ggl, ggl_b = sb("ggl", [128, 4], F32, ph)
            with nc.allow_non_contiguous_dma(reason="tiny gain transpose"):
                dma("sp", ggl[:, :], g_gla_d.rearrange("h p -> p h"), writes=[ggl_b])
            cmask, cmask_b = sb("cmask", [128, 8], F32, ph)
            dma("sp", cmask[:, :], cmask_d[:, :], writes=[cmask_b])

            qpT, qpT_b = sb("qpT", [128, 2, T], BF16, ph)
            attn, attn_b = sb("attn", [128, NT, 512], BF16, ph)
            kdec, kdec_b = sb("kdec", [128, NT, 2, 256], BF16, ph)
            op("dve", lambda e: e.memset(kdec[:, :, :, :], 0.0), writes=[kdec_b])
            vtok, vtok_b = sb("vtok", [128, NT, 512], BF16, ph)
            decs, decs_b = sb("decs", [128, 2, 32], F32, ph)
            qT, qT_b = sb("qT", [128, 2, 512], F32, ph)
            kT, kT_b = sb("kT", [128, 2, 512], F32, ph)
            gaT, gaT_b = sb("gaT", [128, 512], BF16, ph)
            op("dve", lambda e: e.memset(gaT[:, :], 0.0), writes=[gaT_b])
            lsb, lsb_b = sb("lsb", [128, 256], F32, ph)
            Ep, Ep_b = sb("Ep", [128, 256], F32, ph)
            En, En_b = sb("En", [128, 256], F32, ph)
            edl, edl_b = sb("edl", [128, 256], F32, ph)
            qn, qn_b = sb("qn", [128, 256], BF16, ph)
            kp, kp_b = sb("kp", [128, 2, 2, 128], BF16, ph)
            kn, kn_b = sb("kn", [128, 2, 2, 128], BF16, ph)
            op("dve", lambda e: e.memset(kp[:, :, :, :], 0.0), writes=[kp_b])
            op("dve", lambda e: e.memset(kn[:, :, :, :], 0.0), writes=[kn_b])
            t1, t1_b = sb("t1", [128, 512], F32, ph)
            t2, t2_b = sb("t2", [128, 512], F32, ph)
            pi = 0
            for qt in range(NQ):
                ts = slice(qt * 512, (qt + 1) * 512)
                for fc in range(2):
                    ps, ps_b = PS[pi % 4]
                    pi += 1
                    op("pe", proj_T(ps[:, :], Wg, O_GQ + fc * 128, 128, lambda k: uT[:, k, ts]),
                       reads=[Wg_b, uT_b], writes=[ps_b])
                    op("act", lambda e: e.activation(out=qT[:, fc, :], in_=ps[:, :], func=AF.Copy, scale=0.125),
                       reads=[ps_b], writes=[qT_b])
                    ps, ps_b = PS[pi % 4]
                    pi += 1
                    op("pe", proj_T(ps[:, :], Wg, O_GK + fc * 128, 128, lambda k: uT[:, k, ts]),
                       reads=[Wg_b, uT_b], writes=[ps_b])
                    op("dve", lambda e: e.tensor_copy(out=kT[:, fc, :], in_=ps[:, :]), reads=[ps_b], writes=[kT_b])
                ps, ps_b = PS[pi % 4]
                pi += 1
                op("pe", proj_T(ps[0:16, :], Wg, O_GA, 16, lambda k: uT[:, k, ts]), reads=[Wg_b, uT_b], writes=[ps_b])
                op("dve", lambda e: e.tensor_copy(out=gaT[0:16, :], in_=ps[0:16, :]), reads=[ps_b], writes=[gaT_b])
                for tt in range(4):
                    ti = qt * 4 + tt
                    tk = slice(ti * 128, (ti + 1) * 128)
                    lt = slice(tt * 128, (tt + 1) * 128)
                    ps, ps_b = PS[pi % 4]
                    pi += 1
                    op("pe", lambda e: e.matmul(ps[:, 0:256], lhsT=gaT[:, lt], rhs=wup[:, :], start=True, stop=True),
                       reads=[gaT_b, wup_b], writes=[ps_b])
                    op("dve", lambda e: e.tensor_tensor(out=lsb[:, :], in0=ps[:, 0:256], in1=balp[:, :], op=ALU.add),
                       reads=[ps_b, balp_b], writes=[lsb_b])
                    op("act", lambda e: e.activation(out=lsb[:, :], in_=lsb[:, :], func=AF.Exp, scale=-1.0),
                       reads=[lsb_b], writes=[lsb_b])
                    op("act", lambda e: e.activation(out=lsb[:, :], in_=lsb[:, :], func=AF.Ln, bias=1.0, scale=1.0),
                       reads=[lsb_b], writes=[lsb_b])
                    ps, ps_b = PS[pi % 4]
                    pi += 1

                    def gv(e):
                        ins = None
                        for k in range(8):
                            ins = e.matmul(ps[:, :], lhsT=uT[:, k, tk], rhs=Wg[:, k, O_GV:O_GV + 512],
                                           start=(k == 0), stop=(k == 7))
                        return ins
                    op("pe", gv, reads=[Wg_b, uT_b], writes=[ps_b])
                    op("act", lambda e: e.activation(out=vtok[:, ti, :], in_=ps[:, :], func=AF.Copy),
                       reads=[ps_b], writes=[vtok_b])
                    psb, psb_b = PS[pi % 4]
                    pi += 1

                    def fb(e):
                        ins = None
                        for fc in range(2):
                            ins = e.matmul(psb[:, fc * 128:(fc + 1) * 128], lhsT=lsb[:, fc * 128:(fc + 1) * 128],
                                           rhs=cf[:, C_TRIB:C_TRIB + 128], start=True, stop=True)
                        return ins
                    op("pe", fb, reads=[lsb_b, cf_b], writes=[psb_b])
                    op("act", lambda e: e.activation(out=Ep[:, :], in_=psb[:, 0:256], func=AF.Exp),
                       reads=[psb_b], writes=[Ep_b])
                    op("act", lambda e: e.activation(out=En[:, :], in_=psb[:, 0:256], func=AF.Exp, scale=-1.0),
                       reads=[psb_b], writes=[En_b])
                    Ep3 = Ep[:, :].rearrange("p (f t) -> p f t", t=128)
                    En3 = En[:, :].rearrange("p (f t) -> p f t", t=128)
                    op("dve", lambda e: e.tensor_tensor(out=qpT[:, :, tk], in0=qT[:, :, lt], in1=Ep3, op=ALU.mult),
                       reads=[qT_b, Ep_b], writes=[qpT_b])
                    op("dve", lambda e: e.tensor_tensor(out=qn[:, :].rearrange("p (f t) -> p f t", t=128),
                                                        in0=qT[:, :, lt], in1=En3, op=ALU.mult),
                       reads=[qT_b, En_b], writes=[qn_b])
                    for a in range(2):
                        pr = slice(a * 64, (a + 1) * 64)
                        op("dve", lambda e: e.tensor_tensor(out=kp[pr, :, a, :], in0=kT[pr, :, lt], in1=Ep3[pr], op=ALU.mult),
                           reads=[kT_b, Ep_b], writes=[kp_b])
                        op("dve", lambda e: e.tensor_tensor(out=kn[pr, :, a, :], in0=kT[pr, :, lt], in1=En3[pr], op=ALU.mult),
                           reads=[kT_b, En_b], writes=[kn_b])
                    for fc in range(2):
                        op("dve", lambda e: e.tensor_copy(
                            out=decs[:, fc, 2 * ti:2 * ti + 2],
                            in_=Ep[:, fc * 128:(fc + 1) * 128].rearrange("p (j t) -> p j t", t=64)[:, :, 63]),
                           reads=[Ep_b], writes=[decs_b])
                    psd, psd_b = PS[pi % 4]
                    pi += 1
                    op("pe", lambda e: e.matmul(psd[:, 0:256], lhsT=cf[:, C_UPB:C_UPB + 128], rhs=lsb[:, :],
                                                start=True, stop=True), reads=[lsb_b, cf_b], writes=[psd_b])
                    op("act", lambda e: e.activation(out=edl[:, :], in_=psd[:, 0:256], func=AF.Exp),
                       reads=[psd_b], writes=[edl_b])
                    psk, psk_b = PS[pi % 4]
                    pi += 1

                    def gk(e):
                        ins = None
                        for k in range(8):
                            ins = e.matmul(psk[:, 0:256], lhsT=uT[:, k, tk], rhs=Wg[:, k, O_GK:O_GK + 256],
                                           start=(k == 0), stop=(k == 7))
                        return ins
                    op("pe", gk, reads=[Wg_b, uT_b], writes=[psk_b])
                    for j in range(2):
                        pr = slice(j * 64, (j + 1) * 64)
                        op("dve", lambda e: e.tensor_tensor(out=kdec[pr, ti, j, :], in0=psk[pr, 0:256], in1=edl[pr, :], op=ALU.mult),
                           reads=[psk_b, edl_b], writes=[kdec_b])
                    pac, pac_b = PS[4]
                    paa, paa_b = PS[5]

                    def fa(e):
                        ins = None
                        for h in range(4):
                            fc, a = h // 2, h % 2
                            e.matmul(pac[:, h * 128:(h + 1) * 128], lhsT=kn[:, fc, a, :],
                                     rhs=qpT[:, fc, tk], start=True, stop=True)
                            ins = e.matmul(paa[:, h * 128:(h + 1) * 128], lhsT=kp[:, fc, a, :],
                                           rhs=qn[:, fc * 128:(fc + 1) * 128], start=True, stop=True)
                        return ins
                    op("pe", fa, reads=[kn_b, kp_b, qn_b, qpT_b], writes=[pac_b, paa_b])
                    op("dve", lambda e: e.tensor_tensor(out=t1[:, :], in0=pac[:, :], in1=cf[:, C_U2:C_U2 + 512], op=ALU.mult),
                       reads=[pac_b, cf_b], writes=[t1_b])
                    op("dve", lambda e: e.tensor_tensor(out=t2[:, :], in0=paa[:, :], in1=cf[:, C_L2:C_L2 + 512], op=ALU.mult),
                       reads=[paa_b, cf_b], writes=[t2_b])
                    op("dve", lambda e: e.tensor_tensor(out=attn[:, ti, :], in0=t1[:, :], in1=t2[:, :], op=ALU.add),
                       reads=[t1_b, t2_b], writes=[attn_b])

            if STOP == "T3a":
                kb.barrier()
                return nc
            Sst, Sst_b = sb("Sst", [128, 2, 256], F32, ph)
            Pst, Pst_b = sb("Pst", [128, 2], F32, ph)
            Sbf, Sbf_b = sb("Sbf", [128, 2, 2, 128], BF16, ph)
            op("dve", lambda e: e.memset(Sbf[:, :, :, :], 0.0), writes=[Sbf_b])

            def kv_update(n):
                ti, j = n // 2, n % 2
                for fc in range(2):
                    ps, ps_b = PS[(2 * n + fc) % 4]
                    op("pe", lambda e: e.matmul(ps[:, 0:256], lhsT=kdec[:, ti, j, fc * 128:(fc + 1) * 128],
                                                rhs=vtok[:, ti, fc * 256:(fc + 1) * 256],
                                                start=True, stop=True),
                       reads=[kdec_b, vtok_b], writes=[ps_b])
                    op("dve", lambda e: e.scalar_tensor_tensor(out=Sst[:, fc, :], in0=Sst[:, fc, :],
                                                               scalar=decs[:, fc, n:n + 1], in1=ps[:, 0:256],
                                                               op0=ALU.mult, op1=ALU.add),
                       reads=[ps_b, decs_b, Sst_b], writes=[Sst_b])

            op("dve", lambda e: e.memset(Sst[:, :, :], 0.0), writes=[Sst_b])
            op("dve", lambda e: e.memset(Pst[:, :], 1.0), writes=[Pst_b])
            for n in range(32):
                kv_update(n)
                op("dve", lambda e: e.tensor_tensor(out=Pst[:, :], in0=Pst[:, :], in1=decs[:, :, n], op=ALU.mult),
                   reads=[decs_b, Pst_b], writes=[Pst_b])
            gs_b = Buf()
            dma("sp", gs[:, 0:512], Sst[:, :, :].rearrange("p f v -> p (f v)"), reads=[Sst_b], writes=[gs_b])
            dma("sp", gs[:, 512:514], Pst[:, :], reads=[Pst_b], writes=[gs_b])
            kb.collective(gs_t, gsg_t, reads=[gs_b], writes=[gsg_b])

            kb.barrier()

            if STOP == "T3b":
                kb.barrier()
                return nc
            Ain, Ain_b = sb("Ain", [128, 516], F32, ph)
            Pp, Pp_b = sb("Pp", [128, 2], F32, ph)
            op("dve", lambda e: e.memset(Sst[:, :, :], 0.0), writes=[Sst_b])
            for c2 in range(NCORE - 1):
                dma("sp", Ain[:, :], gsg[c2 * 128:(c2 + 1) * 128, :], reads=[gsg_b], writes=[Ain_b])
                op("dve", lambda e: e.tensor_scalar(out=Pp[:, :], in0=Ain[:, 512:514], scalar1=-1.0,
                                                    scalar2=cmask[:, c2:c2 + 1], op0=ALU.add, op1=ALU.mult),
                   reads=[Ain_b, cmask_b], writes=[Pp_b])
                op("dve", lambda e: e.tensor_scalar_add(out=Pp[:, :], in0=Pp[:, :], scalar1=1.0),
                   reads=[Pp_b], writes=[Pp_b])
                op("dve", lambda e: e.tensor_scalar_mul(out=Ain[:, 0:512], in0=Ain[:, 0:512], scalar1=cmask[:, c2:c2 + 1]),
                   reads=[Ain_b, cmask_b], writes=[Ain_b])
                for fc in range(2):
                    op("dve", lambda e: e.scalar_tensor_tensor(out=Sst[:, fc, :], in0=Sst[:, fc, :],
                                                               scalar=Pp[:, fc:fc + 1], in1=Ain[:, fc * 256:(fc + 1) * 256],
                                                               op0=ALU.mult, op1=ALU.add),
                       reads=[Pp_b, Ain_b, Sst_b], writes=[Sst_b])

            if STOP == "T3c":
                kb.barrier()
                return nc
            og, og_b = sb("og", [128, 512], F32, ph)
            osq, osq_b = sb("osq", [128, 512], BF16, ph)
            rstd, rstd_b = sb("rstd", [128, 512], F32, ph)
            sg, sg_b = sb("sg", [128, 512], F32, ph)
            for ti in range(NT):
                tk = slice(ti * 128, (ti + 1) * 128)
                pog, pog_b = PS[4]
                for j in range(2):
                    n = 2 * ti + j
                    for a in range(2):
                        pr = slice(a * 64, (a + 1) * 64)
                        op("act", lambda e: e.activation(out=Sbf[pr, :, a, :], in_=Sst[pr, :, a * 128:(a + 1) * 128], func=AF.Copy),
                           reads=[Sst_b], writes=[Sbf_b])

                    def fo2(e):
                        ins = None
                        for h in range(4):
                            fc, a = h // 2, h % 2
                            cs = slice(h * 128 + j * 64, h * 128 + j * 64 + 64)
                            e.matmul(pog[:, cs], lhsT=vtok[:, ti, h * 128:(h + 1) * 128],
                                     rhs=attn[:, ti, h * 128 + j * 64:h * 128 + j * 64 + 64], start=True, stop=False)
                            ins = e.matmul(pog[:, cs], lhsT=Sbf[:, fc, a, :],
                                           rhs=qpT[:, fc, ti * 128 + j * 64:ti * 128 + j * 64 + 64],
                                           start=False, stop=True)
                        return ins
                    op("pe", fo2, reads=[vtok_b, attn_b, Sbf_b, qpT_b], writes=[pog_b])
                    kv_update(n)
                op("act", lambda e: e.activation(out=og[:, :], in_=pog[:, :], func=AF.Copy), reads=[pog_b], writes=[og_b])
                op("dve", lambda e: e.tensor_tensor(out=osq[:, :], in0=og[:, :], in1=og[:, :], op=ALU.mult),
                   reads=[og_b], writes=[osq_b])
                pss, pss_b = PS[5]
                op("pe", lambda e: e.matmul(pss[:, :], lhsT=ones_b, rhs=osq[:, :], start=True, stop=True),
                   reads=[osq_b, cb_b], writes=[pss_b])
                op("dve", lambda e: e.tensor_scalar(out=rstd[:, :], in0=pss[:, :], scalar1=1.0 / 128.0, scalar2=EPS,
                                                    op0=ALU.mult, op1=ALU.add), reads=[pss_b], writes=[rstd_b])
                op("act", lambda e: e.activation(out=rstd[:, :], in_=rstd[:, :], func=AF.Ln), reads=[rstd_b], writes=[rstd_b])
                op("act", lambda e: e.activation(out=rstd[:, :], in_=rstd[:, :], func=AF.Exp, scale=-0.5),
                   reads=[rstd_b], writes=[rstd_b])
                op("dve", lambda e: e.tensor_tensor(out=og[:, :], in0=og[:, :], in1=rstd[:, :], op=ALU.mult),
                   reads=[og_b, rstd_b], writes=[og_b])
                psg, psg_b = PS[6]

                def fgg(e):
                    ins = None
                    for h in range(4):
                        for k in range(8):
                            ins = e.matmul(psg[:, h * 128:(h + 1) * 128], lhsT=Wg[:, k, O_GG + h * 128:O_GG + (h + 1) * 128],
                                           rhs=uT[:, k, tk], start=(k == 0), stop=(k == 7))
                    return ins
                op("pe", fgg, reads=[Wg_b, uT_b], writes=[psg_b])
                op("act", lambda e: e.activation(out=sg[:, :], in_=psg[:, :], func=AF.Exp, scale=-1.0),
                   reads=[psg_b], writes=[sg_b])
                op("dve", lambda e: e.tensor_scalar_add(out=sg[:, :], in0=sg[:, :], scalar1=1.0), reads=[sg_b], writes=[sg_b])
                op("dve", lambda e: e.reciprocal(out=sg[:, :], in_=sg[:, :]), reads=[sg_b], writes=[sg_b])
                op("dve", lambda e: e.tensor_tensor(out=sg[:, :], in0=psg[:, :], in1=sg[:, :], op=ALU.mult),
                   reads=[psg_b, sg_b], writes=[sg_b])
                op("dve", lambda e: e.tensor_tensor(out=og[:, :], in0=og[:, :], in1=sg[:, :], op=ALU.mult),
                   reads=[og_b, sg_b], writes=[og_b])
                for h in range(4):
                    op("dve", lambda e: e.tensor_scalar_mul(out=oglaT[:, h, tk], in0=og[:, h * 128:(h + 1) * 128],
                                                            scalar1=ggl[:, h:h + 1]),
                       reads=[og_b, ggl_b], writes=[oglaT_b])
        kb.barrier()
        if STOP == "T3":
            return nc

        pay2_b = Buf()
        gat2_b = Buf()
        with ExitStack() as ph:
            Qp, Qp_b = sb("Qp", [128, S], BF16, ph)
            Kp, Kp_b = sb("Kp", [128, S], BF16, ph)
            op("dve", lambda e: e.memset(Qp[:, :], 0.0), writes=[Qp_b])
            op("pool", lambda e: e.memset(Kp[:, :], 0.0), writes=[Kp_b])
            Va, Va_b = sb("Va", [128, 128, 65], BF16, ph)
            L, L_b = sb("L", [128, 128], F32, ph)
            gatq = gat.rearrange("(r h q) c -> h q r c", r=NCORE, h=8)[v_cid]
            dma("sp", Qp[0:64, :].rearrange("q (r c) -> q r c", r=NCORE), gatq[0:64], reads=[gat_b], writes=[Qp_b])
            dma("sp", Kp[0:64, :].rearrange("q (r c) -> q r c", r=NCORE), gatq[64:128], reads=[gat_b], writes=[Kp_b])
            gatv = gat.rearrange("(r h q) c -> h r q c", r=NCORE, h=8)[v_cid][:, 128:193, :]
            gatv = gatv.rearrange("r q c -> r (q c)").rearrange("r (p kc) -> p r kc", p=128)
            dma("sp", Va[:, :, :].rearrange("p (r k) c -> p r (k c)", r=NCORE), gatv, reads=[gat_b], writes=[Va_b])
            dma("sp", L[:, :], gatf.rearrange("(r h) (i p) -> h r i p", h=8, p=128)[v_cid], reads=[gatf_b], writes=[L_b])
            op("dve", lambda e: e.memset(Kp[64:65, :], 1.0), writes=[Kp_b])
            LT, LT_b = sb("LT", [128, 128], F32, ph)
            Xr, Xr_b = sb("Xr", [128, 128], F32, ph)
            Fk, Fk_b = sb("Fk", [128, 128], F32, ph)
            Rrep, Rrep_b = sb("Rrep", [128, 32], F32, ph)
            Cb, Cb_b = sb("Cb", [128, 128], BF16, ph)
            ps, ps_b = PS[0]
            op("pe", lambda e: e.matmul(ps[:, 0:128], lhsT=L[:, :], rhs=ident_f, start=True, stop=True),
               reads=[L_b, cf_b], writes=[ps_b])
            op("dve", lambda e: e.tensor_copy(out=LT[:, :], in_=ps[:, 0:128]), reads=[ps_b], writes=[LT_b])
            ps, ps_b = PS[1]
            op("pe", lambda e: e.matmul(ps[:, 0:128], lhsT=LT[:, :], rhs=ones_f, start=True, stop=True),
               reads=[LT_b, cf_b], writes=[ps_b])
            op("dve", lambda e: e.tensor_copy(out=Xr[:, :], in_=ps[:, 0:128]), reads=[ps_b], writes=[Xr_b])
            ps, ps_b = PS[2]

            def fF(e):
                e.matmul(ps[:, 0:128], lhsT=cf[:, C_TRI:C_TRI + 128], rhs=LT[:, :], start=True, stop=False)
                return e.matmul(ps[:, 0:128], lhsT=Xr[:, :], rhs=cf[:, C_SU:C_SU + 128], start=False, stop=True)
            op("pe", fF, reads=[LT_b, Xr_b, cf_b], writes=[ps_b])
            op("dve", lambda e: e.tensor_copy(out=Fk[:, :], in_=ps[:, 0:128]), reads=[ps_b], writes=[Fk_b])
            ps, ps_b = PS[3]
            op("pe", lambda e: e.matmul(ps[:, 0:32], lhsT=Xr[:, :], rhs=cf[:, C_SU4:C_SU4 + 32], start=True, stop=True),
               reads=[Xr_b, cf_b], writes=[ps_b])
            op("dve", lambda e: e.tensor_copy(out=Rrep[:, :], in_=ps[:, 0:32]), reads=[ps_b], writes=[Rrep_b])
            ps, ps_b = PS[0]

            def fC(e):
                e.matmul(ps[:, 0:128], lhsT=LT[:, :], rhs=cf[:, C_TRI:C_TRI + 128], start=True, stop=False)
                return e.matmul(ps[:, 0:128], lhsT=cf[:, C_G4:C_G4 + 128], rhs=Xr[:, :], start=False, stop=True)
            op("pe", fC, reads=[LT_b, Xr_b, cf_b], writes=[ps_b])
            op("dve", lambda e: e.tensor_copy(out=Cb[:, :], in_=ps[:, 0:128]), reads=[ps_b], writes=[Cb_b])
            cd_b = Buf()
            dma("sp", cd[:, :], Cb[:, :], reads=[Cb_b], writes=[cd_b])
            dma("sp", Qp[64:65, :], cd.rearrange("(o k) p -> o (k p)", o=1), reads=[cd_b], writes=[Qp_b])

            biasq = [sb("biasq%d" % i, [128, 128], F32, ph) for i in range(2)]
            PT = [sb("PT%d" % i, [128, 512], BF16, ph) for i in range(3)]
            rec, rec_b = sb("rec", [128, 512], F32, ph)
            op("dve", lambda e: e.memset(rec[:, :], 0.0), writes=[rec_b])
            bcs, bcs_b = sb("bcs", [64, 512], F32, ph)
            ost = [sb("ost%d" % i, [64, 512], BF16, ph) for i in range(2)]
            step = 0
            for qi in range(S // 512):
                nkt = 4 * qi + 4
                qs = slice(qi * 512, (qi + 1) * 512)
                bq, bq_b = biasq[qi % 2]
                op("dve", lambda e: e.tensor_scalar(out=bq[:, 0:nkt], in0=Fk[:, 0:nkt], scalar1=Rrep[:, qi:qi + 1],
                                                    scalar2=-1.0, op0=ALU.subtract, op1=ALU.mult),
                   reads=[Fk_b, Rrep_b], writes=[bq_b])
                pso, pso_b = PS[3 + (qi % 2)]
                for kt in range(nkt):
                    pss, pss_b = PS[step % 3]
                    p_t, p_b = PT[step % 3]
                    step += 1
                    j = kt - 4 * qi

                    def fs(e):
                        ins = e.matmul(pss[:, :], lhsT=Kp[:, kt * 128:(kt + 1) * 128], rhs=Qp[:, qs],
                                       start=True, stop=(j < 0))
                        if j >= 0:
                            ins = e.matmul(pss[:, :], lhsT=ident_b, rhs=cb[:, B_MASK + j * 512:B_MASK + (j + 1) * 512],
                                           start=False, stop=True)
                        return ins
                    op("pe", fs, reads=[Kp_b, Qp_b, cb_b], writes=[pss_b])
                    op("act", lambda e: e.activation(out=p_t[:, :], in_=pss[:, :], func=AF.Exp, bias=bq[:, kt:kt + 1], scale=1.0),
                       reads=[pss_b, bq_b], writes=[p_b])
                    op("pe", lambda e: e.matmul(pso[0:65, :], lhsT=Va[:, kt, 0:65], rhs=p_t[:, :],
                                                start=(kt == 0), stop=(kt == nkt - 1)),
                       reads=[Va_b, p_b], writes=([pso_b] if kt in (0, nkt - 1) else []))
                op("dve", lambda e: e.reciprocal(out=rec[64:65, :], in_=pso[64:65, :]), reads=[pso_b], writes=[rec_b])
                pbc, pbc_b = PS[5]
                op("pe", lambda e: e.matmul(pbc[0:64, :], lhsT=cf[:, C_SEL:C_SEL + 64], rhs=rec[:, :],
                                            start=True, stop=True), reads=[rec_b, cf_b], writes=[pbc_b])
                op("act", lambda e: e.activation(out=bcs[:, :], in_=pbc[0:64, :], func=AF.Copy), reads=[pbc_b], writes=[bcs_b])
                o_t, o_b = ost[qi % 2]
                op("dve", lambda e: e.tensor_tensor(out=o_t[:, :], in0=pso[0:64, :], in1=bcs[:, :], op=ALU.mult),
                   reads=[pso_b, bcs_b], writes=[o_b])
                dma("sp", pay2[:, qs], o_t[:, :], reads=[o_b], writes=[pay2_b])
            kb.collective(pay2_t, gat2_t, reads=[pay2_b], writes=[gat2_b])
        kb.barrier()
        if STOP == "FOX":
            return nc

        es_m = ExitStack()
        mT, mT_b = sb("mT", [128, 8, T], BF16, es_m)
        with ExitStack() as ph:
            ofoxT, ofoxT_b = sb("ofoxT", [128, 4, T], BF16, ph)
            gat2v = gat2.rearrange("(kc a d) s -> a d kc s", a=2, d=64)
            for a in range(2):
                dma("sp", ofoxT[a * 64:(a + 1) * 64, :, :], gat2v[a][:, :, bass.ds(v_off, T)],
                    reads=[gat2_b], writes=[ofoxT_b])
            oT = [(oglaT, oglaT_b), (ofoxT, ofoxT_b), (omemT, omemT_b)]
            Wgt = [sb("Wgt%d" % i, [128, 8, 1024], BF16, ph) for i in range(2)]
            Wo = [sb("Wo%d" % i, [128, 4, 1024], BF16, ph) for i in range(2)]
            sgm, sgm_b = sb("sgm", [128, 512], BF16, ph)
            tmp, tmp_b = sb("tmpm", [128, 512], BF16, ph)
            pi = 0
            for b in range(3):
                wg_t, wg_b = Wgt[b % 2]
                wo_t, wo_b = Wo[b % 2]
                for dc in range(8):
                    load_w(wg_t[:, dc, :], w_in_d[dc * 128:(dc + 1) * 128, O_GATES + b * 1024:O_GATES + (b + 1) * 1024], wg_b)
                for kc in range(4):
                    load_w(wo_t[:, kc, :], w_o_d[b][kc * 128:(kc + 1) * 128, :], wo_b)
                o_t, o_b = oT[b]
                for qt in range(NQ):
                    ts = slice(qt * 512, (qt + 1) * 512)
                    for fc in range(8):
                        psg, psg_b = PS[pi % 3]
                        psy, psy_b = PS[3 + pi % 3]
                        pi += 1
                        op("pe", proj_T(psg[:, :], wg_t, fc * 128, 128, lambda k: uT[:, k, ts]), reads=[wg_b, uT_b], writes=[psg_b])
                        op("pe", proj_T(psy[:, :], wo_t, fc * 128, 128, lambda k: o_t[:, k, ts], nk=4), reads=[wo_b, o_b], writes=[psy_b])
                        op("act", lambda e: e.activation(out=sgm[:, :], in_=psg[:, :], func=AF.Sigmoid), reads=[psg_b], writes=[sgm_b])
                        if b == 0:
                            op("dve", lambda e: e.tensor_tensor(out=mT[:, fc, ts], in0=psy[:, :], in1=sgm[:, :], op=ALU.mult),
                               reads=[psy_b, sgm_b], writes=[mT_b])
                        else:
                            op("dve", lambda e: e.tensor_tensor(out=tmp[:, :], in0=psy[:, :], in1=sgm[:, :], op=ALU.mult),
                               reads=[psy_b, sgm_b], writes=[tmp_b])
                            op("dve", lambda e: e.tensor_tensor(out=mT[:, fc, ts], in0=mT[:, fc, ts], in1=tmp[:, :], op=ALU.add),
                               reads=[tmp_b, mT_b], writes=[mT_b])
        kb.barrier()
        if STOP == "T5a":
            return nc
        es_r.close()

        es_h = ExitStack()
        hres, hres_b = sb("hres", [128, NT, D], F32, es_h, "right")
        with ExitStack() as ph:
            Wout, Wout_b = sb("Wout", [128, 8, 1024], BF16, ph)
            for kc in range(8):
                load_w(Wout[:, kc, :], w_out_d[kc * 128:(kc + 1) * 128, :], Wout_b)
            gm2, gm2_b = sb("gm2", [128, 8], F32, ph)
            with nc.allow_non_contiguous_dma(reason="tiny gain vector transpose"):
                dma("sp", gm2[:, :], g_ffn_d.rearrange("o (dc p) -> p (o dc)", p=128), writes=[gm2_b])
            xt = [sb("bxt%d" % i, [128, D], F32, ph) for i in range(2)]
            xn = [sb("bxn%d" % i, [128, D], BF16, ph) for i in range(2)]
            junk, junk_b = sb("bjunk", [128, D], BF16, ph)
            ss, ss_b = sb("bss", [128, 16], F32, ph)
            rr, rr_b = sb("brr", [128, 16], F32, ph)
            for ti in range(NT):
                tk = slice(ti * 128, (ti + 1) * 128)
                xt_t, xt_b = xt[ti % 2]
                xn_t, xn_b = xn[ti % 2]
                dma("sp", xt_t[:, :], x_d[tk, :], writes=[xt_b])
                for ch in range(2):
                    ps, ps_b = PS[(2 * ti + ch) % 4]

                    def fh(e):
                        ins = None
                        for k in range(8):
                            ins = e.matmul(ps[:, :], lhsT=mT[:, k, tk], rhs=Wout[:, k, ch * 512:(ch + 1) * 512],
                                           start=(k == 0), stop=(k == 7))
                        return ins
                    op("pe", fh, reads=[mT_b, Wout_b], writes=[ps_b])
                    op("dve", lambda e: e.tensor_tensor(out=hres[:, ti, ch * 512:(ch + 1) * 512], in0=ps[:, :],
                                                        in1=xt_t[:, ch * 512:(ch + 1) * 512], op=ALU.add),
                       reads=[ps_b, xt_b], writes=[hres_b])
                op("act", lambda e: e.activation(out=junk[:, :], in_=hres[:, ti, :], func=AF.Square, accum_out=ss[:, ti:ti + 1]),
                   reads=[hres_b], writes=[junk_b, ss_b])
                op("dve", lambda e: e.tensor_scalar(out=rr[:, ti:ti + 1], in0=ss[:, ti:ti + 1], scalar1=1.0 / D, scalar2=EPS,
                                                    op0=ALU.mult, op1=ALU.add), reads=[ss_b], writes=[rr_b])
                op("act", lambda e: e.activation(out=rr[:, ti:ti + 1], in_=rr[:, ti:ti + 1], func=AF.Ln), reads=[rr_b], writes=[rr_b])
                op("act", lambda e: e.activation(out=rr[:, ti:ti + 1], in_=rr[:, ti:ti + 1], func=AF.Exp, scale=-0.5),
                   reads=[rr_b], writes=[rr_b])
                op("dve", lambda e: e.tensor_scalar_mul(out=xn_t[:, :], in0=hres[:, ti, :], scalar1=rr[:, ti:ti + 1]),
                   reads=[hres_b, rr_b], writes=[xn_b])

                def tr2(e):
                    ins = None
                    for dc in range(8):
                        ins = e.transpose(out=PST[0][:, dc * 128:(dc + 1) * 128], in_=xn_t[:, dc * 128:(dc + 1) * 128],
                                          identity=ident_b)
                    return ins
                op("pe", tr2, reads=[xn_b, cb_b], writes=[PST[1]])
                for dc in range(8):
                    if dc % 2 == 0:
                        op("dve", lambda e: e.tensor_scalar_mul(out=uT[:, dc, tk], in0=PST[0][:, dc * 128:(dc + 1) * 128],
                                                                scalar1=gm2[:, dc:dc + 1]),
                           reads=[PST[1], gm2_b], writes=[uT_b])
                    else:
                        op("act", lambda e: e.activation(out=uT[:, dc, tk], in_=PST[0][:, dc * 128:(dc + 1) * 128],
                                                         func=AF.Copy, scale=gm2[:, dc:dc + 1]),
                           reads=[PST[1], gm2_b], writes=[uT_b])
        kb.barrier()
        if STOP == "T5b":
            return nc
        es_m.close()

        with ExitStack() as ph:
            W1 = [sb("W1_%d" % i, [128, 8, 1024], BF16, ph) for i in range(2)]
            W2 = [sb("W2_%d" % i, [128, 8, 1024], BF16, ph) for i in range(2)]
            aT = [sb("aT%d" % i, [128, 8, 512], BF16, ph) for i in range(2)]
            pi = 0
            ai = 0
            for qf in range(4):
                w1_t, w1_b = W1[qf % 2]
                w2_t, w2_b = W2[qf % 2]
                for dc in range(8):
                    load_w(w1_t[:, dc, :], w_ff1_d[dc * 128:(dc + 1) * 128, qf * 1024:(qf + 1) * 1024], w1_b)
                for fc in range(8):
                    load_w(w2_t[:, fc, :], w_ff2_d[qf * 1024 + fc * 128:qf * 1024 + (fc + 1) * 128, :], w2_b)
                for qt in range(NQ):
                    ts = slice(qt * 512, (qt + 1) * 512)
                    a_t, a_b = aT[ai % 2]
                    ai += 1
                    for fc in range(8):
                        ps, ps_b = PS[pi % 6]
                        pi += 1
                        op("pe", proj_T(ps[:, :], w1_t, fc * 128, 128, lambda k: uT[:, k, ts]), reads=[w1_b, uT_b], writes=[ps_b])
                        op("act", lambda e: e.activation(out=a_t[:, fc, :], in_=ps[:, :], func=AF.Relu), reads=[ps_b], writes=[a_b])
                        op("dve" if fc % 2 == 0 else "pool",
                           lambda e: e.tensor_tensor(out=a_t[:, fc, :], in0=a_t[:, fc, :], in1=a_t[:, fc, :], op=ALU.mult),
                           reads=[a_b], writes=[a_b])
                    for tt in range(4):
                        ti = qt * 4 + tt
                        for ch in range(2):
                            ps, ps_b = PS[pi % 6]
                            pi += 1

                            def f2(e):
                                ins = None
                                for fc in range(8):
                                    ins = e.matmul(ps[:, :], lhsT=a_t[:, fc, tt * 128:(tt + 1) * 128],
                                                   rhs=w2_t[:, fc, ch * 512:(ch + 1) * 512], start=(fc == 0), stop=(fc == 7))
                                return ins
                            op("pe", f2, reads=[a_b, w2_b], writes=[ps_b])
                            op("dve", lambda e: e.tensor_tensor(out=hres[:, ti, ch * 512:(ch + 1) * 512],
                                                                in0=hres[:, ti, ch * 512:(ch + 1) * 512], in1=ps[:, :], op=ALU.add),
                               reads=[ps_b, hres_b], writes=[hres_b])
        kb.barrier()
        if STOP == "T5c":
            return nc

        with ExitStack() as ph:
            gfb, gfb_b = sb("gfb", [128, D], F32, ph)
            dma("sp", gfb[:, :], g_fin_d.partition_broadcast(128), writes=[gfb_b])
            junk, junk_b = sb("djunk", [128, D], BF16, ph)
            ss, ss_b = sb("dss", [128, 16], F32, ph)
            rr, rr_b = sb("drr", [128, 16], F32, ph)
            yo = [sb("yo%d" % i, [128, D], F32, ph) for i in range(2)]
            out_b = Buf()
            for ti in range(NT):
                y_t, y_b = yo[ti % 2]
                op("act", lambda e: e.activation(out=junk[:, :], in_=hres[:, ti, :], func=AF.Square, accum_out=ss[:, ti:ti + 1]),
                   reads=[hres_b], writes=[junk_b, ss_b])
                op("dve", lambda e: e.tensor_scalar(out=rr[:, ti:ti + 1], in0=ss[:, ti:ti + 1], scalar1=1.0 / D, scalar2=EPS,
                                                    op0=ALU.mult, op1=ALU.add), reads=[ss_b], writes=[rr_b])
                op("act", lambda e: e.activation(out=rr[:, ti:ti + 1], in_=rr[:, ti:ti + 1], func=AF.Ln), reads=[rr_b], writes=[rr_b])
                op("act", lambda e: e.activation(out=rr[:, ti:ti + 1], in_=rr[:, ti:ti + 1], func=AF.Exp, scale=-0.5),
                   reads=[rr_b], writes=[rr_b])
                op("dve", lambda e: e.scalar_tensor_tensor(out=y_t[:, :], in0=hres[:, ti, :], scalar=rr[:, ti:ti + 1],
                                                           in1=gfb[:, :], op0=ALU.mult, op1=ALU.mult),
                   reads=[hres_b, rr_b, gfb_b], writes=[y_b])
                dma("sp", out_d[ti * 128:(ti + 1) * 128, :], y_t[:, :], reads=[y_b], writes=[out_b])
        kb.barrier()
        if STOP == "T5d":
            return nc
        es_h.close()
    return nc


_CACHE = {}


def kernel(x, mem, g_mix, w_in, w_alpha_up, b_alpha, b_forget, g_gla_head, g_mem, w_mem_kv,
           w_gla_o, w_fox_o, w_mem_o, w_out, g_ffn, w_ff1, w_ff2, g_final):
    f = lambda a: np.ascontiguousarray(np.asarray(a, dtype=np.float32))
    if "nc" not in _CACHE:
        _CACHE["nc"] = build_nc()
        _CACHE["c"] = _consts()
    nc = _CACHE["nc"]
    cf, cb = _CACHE["c"]
    xs = f(x).reshape(S, D)
    shared = {
        "mem": f(mem).reshape(256, D), "g_mix": f(g_mix).reshape(1, D), "w_in": f(w_in).reshape(D, D_IN),
        "w_alpha_up": f(w_alpha_up).reshape(16, 256), "b_alpha": f(b_alpha).reshape(1, 256),
        "b_forget": f(b_forget).reshape(8, 1), "g_gla_head": f(g_gla_head).reshape(4, 128),
        "g_mem": f(g_mem).reshape(1, D), "w_mem_kv": f(w_mem_kv).reshape(D, 1024),
        "w_gla_o": f(w_gla_o).reshape(512, D), "w_fox_o": f(w_fox_o).reshape(512, D),
        "w_mem_o": f(w_mem_o).reshape(512, D), "w_out": f(w_out).reshape(D, D),
        "g_ffn": f(g_ffn).reshape(1, D), "w_ff1": f(w_ff1).reshape(D, 4096), "w_ff2": f(w_ff2).reshape(4096, D),
        "g_final": f(g_final).reshape(1, D), "cf": cf, "cb": cb,
    }
    big = ("mem", "w_in", "w_mem_kv", "w_gla_o", "w_fox_o", "w_mem_o", "w_out", "w_ff1", "w_ff2", "cf")
    in_maps = []
    for c in range(NCORE):
        m = dict(shared)
        for k in big:
            a = shared[k]
            m[k] = np.concatenate([a, np.full((1, a.shape[1]), float(c), a.dtype)], axis=0)
        m["x"] = xs[c * T:(c + 1) * T]
        m["cid"] = np.array([[c, c * T]], np.int32)
        m["cmask"] = np.tile((np.arange(8) < c).astype(np.float32)[None, :], (128, 1))
        in_maps.append(m)
    res = run_bass_kernel_spmd(nc, in_maps, core_ids=list(range(NCORE)))
    out = np.concatenate([np.asarray(r["out"], dtype=np.float32) for r in res.results], axis=0)
    return out.reshape(1, S, D)
```

```python
import numpy as np
import ml_dtypes
from contextlib import ExitStack
import concourse.bass as bass
import concourse.mybir as mybir
from concourse.bass_utils import run_bass_kernel_spmd

F32 = mybir.dt.float32
BF16 = mybir.dt.bfloat16
I32 = mybir.dt.int32
AF = mybir.ActivationFunctionType
ALU = mybir.AluOpType

NCORE = 8
D = 1024
S = 16384
T = S // NCORE
NT = T // 128
NQ = T // 512
EPS = 1e-6
D_IN = 6680
O_GQ, O_GK, O_GV, O_GG, O_GA = 0, 256, 512, 1024, 1536
O_FQ, O_FK, O_FV, O_FF, O_MQ, O_GATES = 1552, 2064, 2576, 3088, 3096, 3608
PAY_R = 193
NDS = 40

C_IDENT, C_TRI, C_SU, C_G4, C_TRIB, C_UPB, C_ONES, C_U2, C_L2, C_SU4 = (
    0, 128, 256, 384, 512, 640, 768, 896, 1408, 1920)
C_SEL = 1952
C_TOT = 2016
B_IDENT, B_ONES, B_MASK = 0, 128, 256
B_TOT = 256 + 4 * 512


def _consts():
    p = np.arange(128)
    cf = np.zeros((128, C_TOT), np.float32)
    cf[:, C_IDENT:C_IDENT + 128] = np.eye(128)
    cf[:, C_TRI:C_TRI + 128] = (p[:, None] <= p[None, :])
    cf[:, C_SU:C_SU + 128] = (p[:, None] < p[None, :])
    cf[:, C_G4:C_G4 + 128] = (p[:, None] < p[None, :]) & ((p[:, None] // 4) == (p[None, :] // 4))
    same = (p[:, None] // 64) == (p[None, :] // 64)
    cf[:, C_TRIB:C_TRIB + 128] = np.where(same & (p[:, None] <= p[None, :]), -1.0 / 16.0, 0.0)
    cf[:, C_UPB:C_UPB + 128] = np.where(same & (p[:, None] > p[None, :]), -1.0 / 16.0, 0.0)
    cf[:, C_ONES:C_ONES + 128] = 1.0
    u2 = (same & (p[None, :] >= p[:, None])).astype(np.float32)
    l2 = (same & (p[None, :] < p[:, None])).astype(np.float32)
    cf[:, C_U2:C_U2 + 512] = np.tile(u2, (1, 4))
    cf[:, C_L2:C_L2 + 512] = np.tile(l2, (1, 4))
    cf[:, C_SU4:C_SU4 + 32] = (p[:, None] < 4 * np.arange(32)[None, :])
    cf[64, C_SEL:C_SEL + 64] = 1.0
    cb = np.zeros((128, B_TOT), np.float32)
    cb[:, B_IDENT:B_IDENT + 128] = np.eye(128)
    cb[:, B_ONES:B_ONES + 128] = 1.0
    cq = np.arange(512)
    for j in range(4):
        cb[:, B_MASK + j * 512:B_MASK + (j + 1) * 512] = np.where(
            (128 * j + p[:, None]) > cq[None, :], -30000.0, 0.0)
    return cf, cb.astype(ml_dtypes.bfloat16)


class Buf:
    __slots__ = ("w", "r")

    def __init__(self):
        self.w = None
        self.r = {}


class _FirstIns:
    def __init__(self, eng, first):
        self._eng = eng
        self._first = first

    def __getattr__(self, name):
        f = getattr(self._eng, name)

        def call(*a, **k):
            r = f(*a, **k)
            if not self._first:
                self._first.append(r)
            return r
        return call


class KB:
    def __init__(self, nc, es):
        self.nc = nc
        self.eng = {"pe": nc.tensor, "act": nc.scalar, "dve": nc.vector, "pool": nc.gpsimd,
                    "sp": nc.sync}
        self.psem = {e: es.enter_context(nc.semaphore("p_" + e)) for e in ("pe", "act", "dve", "pool")}
        self.cnt = {e: 0 for e in self.psem}
        self.dsem = [es.enter_context(nc.semaphore("d%d" % i)) for i in range(NDS)]
        self.dval = [0] * NDS
        self.dnext = 0
        self.csems = [es.enter_context(nc.semaphore("cc%d" % i)) for i in range(4)]
        self.cidx = 0
        self.seen = {e: {} for e in self.eng}

    def _sem(self, key):
        if isinstance(key, tuple) and key[0] == "c":
            return self.csems[key[1]]
        return self.psem[key] if isinstance(key, str) else self.dsem[key[1]]

    def wait(self, e, tok):
        if tok is None:
            return
        key, val = tok
        if self.seen[e].get(key, 0) >= val:
            return
        self.eng[e].wait_ge(self._sem(key), val)
        self.seen[e][key] = val

    def _deps(self, e, reads, writes):
        for b in reads:
            self.wait(e, b.w)
        for b in writes:
            self.wait(e, b.w)
            for k, v in b.r.items():
                self.wait(e, (k, v))

    def _commit(self, tok, reads, writes):
        for b in writes:
            b.w = tok
            b.r = {}
        for b in reads:
            if b.r.get(tok[0], 0) < tok[1]:
                b.r[tok[0]] = tok[1]

    def _need(self, e, reads, writes):
        need = {}

        def add(tok):
            if tok is None:
                return
            key, val = tok
            if self.seen[e].get(key, 0) >= val or need.get(key, 0) >= val:
                return
            need[key] = val
        for b in reads:
            add(b.w)
        for b in writes:
            add(b.w)
            for k, v in b.r.items():
                add((k, v))
        return list(need.items())

    def op(self, e, fn, reads=(), writes=()):
        need = self._need(e, reads, writes)
        for tok in need[:-1]:
            self.wait(e, tok)
        first = []
        ins = fn(_FirstIns(self.eng[e], first))
        if need:
            key, val = need[-1]
            first[0]._wait_ge(self._sem(key), val)
            self.seen[e][key] = val
        self.cnt[e] += 1
        ins.then_inc(self.psem[e], 1)
        tok = (e, self.cnt[e])
        self._commit(tok, reads, writes)
        return tok

    def dma(self, q, out, in_, reads=(), writes=(), **kw):
        self._deps(q, reads, writes)
        i = self.dnext
        self.dnext = (i + 1) % NDS
        if self.dval[i] > 0:
            self.wait(q, (("d", i), self.dval[i]))
        ins = self.eng[q].dma_start(out=out, in_=in_, **kw)
        self.dval[i] += 16
        ins.then_inc(self.dsem[i], 16)
        tok = (("d", i), self.dval[i])
        self._commit(tok, reads, writes)
        return tok

    def collective(self, in_t, out_t, reads=(), writes=()):
        self._deps("pool", reads, writes)
        ins = self.nc.gpsimd.collective_compute(
            "AllGather", ALU.bypass, replica_groups=[list(range(NCORE))],
            ins=[in_t.ap().opt()], outs=[out_t.ap().opt()])
        ins.then_inc(self.csems[self.cidx])
        tok = (("c", self.cidx), 1)
        self.cidx += 1
        self._commit(tok, reads, writes)
        return tok

    def barrier(self):
        for e in self.eng:
            for e2 in self.psem:
                if e2 != e and self.cnt[e2] > 0:
                    self.wait(e, (e2, self.cnt[e2]))
            for i in range(NDS):
                if self.dval[i] > 0:
                    self.wait(e, (("d", i), self.dval[i]))
            for i in range(self.cidx):
                self.wait(e, (("c", i), 1))


def build_nc():
    import os
    STOP = os.environ.get("KSTOP", "")
    nc = bass.Bass("TRN2", target_bir_lowering=False)
    dt = nc.dram_tensor
    x_d = dt("x", [T, D], F32, kind="ExternalInput").ap()
    mem_d = dt("mem", [256 + 1, D], F32, kind="ExternalInput").ap()
    g_mix_d = dt("g_mix", [1, D], F32, kind="ExternalInput").ap()
    w_in_d = dt("w_in", [D + 1, D_IN], F32, kind="ExternalInput").ap()
    w_up_d = dt("w_alpha_up", [16, 256], F32, kind="ExternalInput").ap()
    b_alpha_d = dt("b_alpha", [1, 256], F32, kind="ExternalInput").ap()
    b_forget_d = dt("b_forget", [8, 1], F32, kind="ExternalInput").ap()
    g_gla_d = dt("g_gla_head", [4, 128], F32, kind="ExternalInput").ap()
    g_mem_d = dt("g_mem", [1, D], F32, kind="ExternalInput").ap()
    w_mkv_d = dt("w_mem_kv", [D + 1, 1024], F32, kind="ExternalInput").ap()
    w_o_d = [dt(n, [512 + 1, D], F32, kind="ExternalInput").ap() for n in ("w_gla_o", "w_fox_o", "w_mem_o")]
    w_out_d = dt("w_out", [D + 1, D], F32, kind="ExternalInput").ap()
    g_ffn_d = dt("g_ffn", [1, D], F32, kind="ExternalInput").ap()
    w_ff1_d = dt("w_ff1", [D + 1, 4096], F32, kind="ExternalInput").ap()
    w_ff2_d = dt("w_ff2", [4096 + 1, D], F32, kind="ExternalInput").ap()
    g_fin_d = dt("g_final", [1, D], F32, kind="ExternalInput").ap()
    cf_d = dt("cf", [128 + 1, C_TOT], F32, kind="ExternalInput").ap()
    cb_d = dt("cb", [128, B_TOT], BF16, kind="ExternalInput").ap()
    cid_d = dt("cid", [1, 2], I32, kind="ExternalInput").ap()
    cmask_d = dt("cmask", [128, 8], F32, kind="ExternalInput").ap()
    out_d = dt("out", [T, D], F32, kind="ExternalOutput").ap()

    pay_t = dt("pay", [8 * PAY_R, T], BF16)
    gat_t = dt("gat", [NCORE * 8 * PAY_R, T], BF16)
    payf_t = dt("payf", [8, T], F32)
    gatf_t = dt("gatf", [NCORE * 8, T], F32)
    gs_t = dt("gs", [128, 516], F32)
    gsg_t = dt("gsg", [NCORE * 128, 516], F32)
    pay2_t = dt("pay2", [64, S], BF16)
    gat2_t = dt("gat2", [NCORE * 64, S], BF16)
    cd_t = dt("cd", [128, 128], BF16)
    pay, gat, payf, gatf = pay_t.ap(), gat_t.ap(), payf_t.ap(), gatf_t.ap()
    gs, gsg, pay2, gat2, cd = gs_t.ap(), gsg_t.ap(), pay2_t.ap(), gat2_t.ap(), cd_t.ap()

    es = ExitStack()
    with es:
        kb = KB(nc, es)
        op, dma = kb.op, kb.dma

        def sb(name, shape, dtype, stack=es, side=None):
            return stack.enter_context(nc.sbuf_tensor("s_" + name, shape, dtype, side=side)), Buf()

        PS = []
        for i in range(7):
            PS.append((es.enter_context(nc.psum_tensor("ps%d" % i, [128, 512], F32)), Buf()))
        PST = (es.enter_context(nc.psum_tensor("pst", [128, 1024], BF16)), Buf())

        cf, cf_b = sb("cf_sb", [128, C_TOT], F32)
        cb, cb_b = sb("cb_sb", [128, B_TOT], BF16)
        uT, uT_b = sb("uT", [128, 8, T], BF16)
        es_r = ExitStack()
        omemT, omemT_b = sb("omemT", [128, 4, T], BF16, es_r, "right")
        oglaT, oglaT_b = sb("oglaT", [128, 4, T], BF16, es_r, "right")
        dma("sp", cf[:, :], cf_d[0:128, :], writes=[cf_b])
        dma("sp", cb[:, :], cb_d[:, :], writes=[cb_b])
        ident_f = cf[:, C_IDENT:C_IDENT + 128]
        ones_f = cf[:, C_ONES:C_ONES + 128]
        ident_b = cb[:, B_IDENT:B_IDENT + 128]
        ones_b = cb[:, B_ONES:B_ONES + 128]

        r_cid = nc.sync.alloc_register("r_cid")
        r_off = nc.sync.alloc_register("r_off")
        nc.sync.reg_load(r_cid, cid_d[0:1, 0:1])
        nc.sync.reg_load(r_off, cid_d[0:1, 1:2])
        v_cid = nc.sync.snap(r_cid, min_val=0, max_val=NCORE - 1)
        v_off = nc.sync.snap(r_off, min_val=0, max_val=S - T)

        def load_w(dst, src, dbuf):
            return dma("pool", dst, src, writes=[dbuf])

        def rmsnorm_T(src_rows, ntiles, gvec_d, dstT, dstT_b, ph, tag):
            gm, gm_b = sb(tag + "gm", [128, 8], F32, ph)
            with nc.allow_non_contiguous_dma(reason="tiny gain vector transpose"):
                dma("sp", gm[:, :], gvec_d.rearrange("o (dc p) -> p (o dc)", p=128), writes=[gm_b])
            xt = [sb(tag + "xt%d" % i, [128, D], F32, ph) for i in range(2)]
            xn = [sb(tag + "xn%d" % i, [128, D], BF16, ph) for i in range(2)]
            junk, junk_b = sb(tag + "junk", [128, D], BF16, ph)
            ss, ss_b = sb(tag + "ss", [128, 16], F32, ph)
            rr, rr_b = sb(tag + "rr", [128, 16], F32, ph)
            for ti in range(ntiles):
                xt_t, xt_b = xt[ti % 2]
                xn_t, xn_b = xn[ti % 2]
                dma("sp", xt_t[:, :], src_rows[ti * 128:(ti + 1) * 128, :], writes=[xt_b])
                op("act", lambda e: e.activation(out=junk[:, :], in_=xt_t[:, :], func=AF.Square,
                                                 accum_out=ss[:, ti:ti + 1]),
                   reads=[xt_b], writes=[junk_b, ss_b])
                op("dve", lambda e: e.tensor_scalar(out=rr[:, ti:ti + 1], in0=ss[:, ti:ti + 1],
                                                    scalar1=1.0 / D, scalar2=EPS, op0=ALU.mult, op1=ALU.add),
                   reads=[ss_b], writes=[rr_b])
                op("act", lambda e: e.activation(out=rr[:, ti:ti + 1], in_=rr[:, ti:ti + 1], func=AF.Ln), reads=[rr_b], writes=[rr_b])
                op("act", lambda e: e.activation(out=rr[:, ti:ti + 1], in_=rr[:, ti:ti + 1], func=AF.Exp, scale=-0.5),
                   reads=[rr_b], writes=[rr_b])
                op("dve", lambda e: e.tensor_scalar_mul(out=xn_t[:, :], in0=xt_t[:, :], scalar1=rr[:, ti:ti + 1]),
                   reads=[xt_b, rr_b], writes=[xn_b])

                def tr(e):
                    ins = None
                    for dc in range(8):
                        ins = e.transpose(out=PST[0][:, dc * 128:(dc + 1) * 128],
                                          in_=xn_t[:, dc * 128:(dc + 1) * 128], identity=ident_b)
                    return ins
                op("pe", tr, reads=[xn_b, cb_b], writes=[PST[1]])
                for dc in range(8):
                    if dc % 2 == 0:
                        op("dve", lambda e: e.tensor_scalar_mul(
                            out=dstT[:, dc, ti * 128:(ti + 1) * 128], in0=PST[0][:, dc * 128:(dc + 1) * 128],
                            scalar1=gm[:, dc:dc + 1]), reads=[PST[1], gm_b], writes=[dstT_b])
                    else:
                        op("act", lambda e: e.activation(
                            out=dstT[:, dc, ti * 128:(ti + 1) * 128], in_=PST[0][:, dc * 128:(dc + 1) * 128],
                            func=AF.Copy, scale=gm[:, dc:dc + 1]), reads=[PST[1], gm_b], writes=[dstT_b])

        def proj_T(ps, W, c0, m, rhs_fn, nk=8):
            def f(e):
                ins = None
                for k in range(nk):
                    ins = e.matmul(ps, lhsT=W[:, k, c0:c0 + m], rhs=rhs_fn(k), start=(k == 0), stop=(k == nk - 1))
                return ins
            return f

        with ExitStack() as ph:
            rmsnorm_T(x_d, NT, g_mix_d, uT, uT_b, ph, "a")
        kb.barrier()
        if STOP == "T1":
            return nc

        with ExitStack() as ph:
            Wf, Wf_b = sb("Wf", [128, 8, 1544], BF16, ph)
            for dc in range(8):
                load_w(Wf[:, dc, 0:1536], w_in_d[dc * 128:(dc + 1) * 128, O_FQ:O_FQ + 1536], Wf_b)
                load_w(Wf[:, dc, 1536:1544], w_in_d[dc * 128:(dc + 1) * 128, O_FF:O_FF + 8], Wf_b)
            nbf, nbf_b = sb("nbf", [8, 1], F32, ph)
            dma("sp", nbf[:, :], b_forget_d[:, :], writes=[nbf_b])
            op("dve", lambda e: e.tensor_scalar_mul(out=nbf[:, :], in0=nbf[:, :], scalar1=-1.0),
               reads=[nbf_b], writes=[nbf_b])
            stg = [sb("stg%d" % i, [128, 512], BF16, ph) for i in range(3)]
            lfT, lfT_b = sb("lfT", [8, T], F32, ph)
            etmp, etmp_b = sb("etmp", [8, 512], F32, ph)
            si = 0
            pi = 0
            pay3 = pay.rearrange("(h r) c -> h r c", r=PAY_R)
            payv = pay3[:, 128:193, :].rearrange("h r c -> h (r c)").rearrange("h (p k c) -> p h k c", p=128, k=16)
            stgv = [sb("stgv%d" % i, [128, 8, 65], BF16, ph) for i in range(2)]
            for sv, sv_b in stgv:
                op("dve", lambda e: e.memset(sv[:, :, :], 1.0), writes=[sv_b])
            pay_b = Buf()
            payf_b = Buf()
            for qt in range(NQ):
                ts = slice(qt * 512, (qt + 1) * 512)
                for which in range(2):
                    for hp in range(4):
                        ps, ps_b = PS[pi % 4]
                        pi += 1
                        st, st_b = stg[si % 3]
                        si += 1
                        op("pe", proj_T(ps[:, :], Wf, which * 512 + hp * 128, 128, lambda k: uT[:, k, ts]),
                           reads=[Wf_b, uT_b], writes=[ps_b])
                        op("act", lambda e: e.activation(out=st[:, :], in_=ps[:, :], func=AF.Copy,
                                                         scale=(0.125 if which == 0 else 1.0)),
                           reads=[ps_b], writes=[st_b])
                        for a in range(2):
                            h = 2 * hp + a
                            dma("sp", pay3[h, which * 64:(which + 1) * 64, ts], st[a * 64:(a + 1) * 64, :],
                                reads=[st_b], writes=[pay_b])
                for tt in range(4):
                    tk = slice(qt * 512 + tt * 128, qt * 512 + (tt + 1) * 128)
                    ps, ps_b = PS[pi % 4]
                    pi += 1
                    st, st_b = stgv[tt % 2]

                    def fv(e):
                        ins = None
                        for k in range(8):
                            ins = e.matmul(ps[:, :], lhsT=uT[:, k, tk], rhs=Wf[:, k, 1024:1536],
                                           start=(k == 0), stop=(k == 7))
                        return ins
                    op("pe", fv, reads=[Wf_b, uT_b], writes=[ps_b])
                    op("dve", lambda e: e.tensor_copy(out=st[:, :, 0:64], in_=ps[:, :].rearrange("p (h b) -> p h b", b=64)),
                       reads=[ps_b], writes=[st_b])
                    dma("sp", payv[:, :, qt * 4 + tt, :], st[:, :, :], reads=[st_b], writes=[pay_b])
                ps, ps_b = PS[pi % 4]
                pi += 1
                op("pe", proj_T(ps[0:8, :], Wf, 1536, 8, lambda k: uT[:, k, ts]), reads=[Wf_b, uT_b], writes=[ps_b])
                op("act", lambda e: e.activation(out=etmp[:, :], in_=ps[0:8, :], func=AF.Exp, bias=nbf[:, 0:1], scale=-1.0),
                   reads=[ps_b, nbf_b], writes=[etmp_b])
                op("act", lambda e: e.activation(out=etmp[:, :], in_=etmp[:, :], func=AF.Ln, bias=1.0, scale=1.0),
                   reads=[etmp_b], writes=[etmp_b])
                op("dve", lambda e: e.tensor_scalar_mul(out=lfT[:, ts], in0=etmp[:, :], scalar1=-1.0),
                   reads=[etmp_b], writes=[lfT_b])
            dma("sp", payf[:, :], lfT[:, :], reads=[lfT_b], writes=[payf_b])
            gat_b = Buf()
            gatf_b = Buf()
            kb.collective(pay_t, gat_t, reads=[pay_b], writes=[gat_b])
            kb.collective(payf_t, gatf_t, reads=[payf_b], writes=[gatf_b])
        kb.barrier()
        if STOP == "T2":
            return nc

        with ExitStack() as pm:
            mnT, mnT_b = sb("mnT", [128, 8, 256], BF16, pm)
            rmsnorm_T(mem_d, 2, g_mem_d, mnT, mnT_b, pm, "m")
            Wkv, Wkv_b = sb("Wkv", [128, 8, 1024], BF16, pm)
            Wmq, Wmq_b = sb("Wmq", [128, 8, 512], BF16, pm)
            for dc in range(8):
                load_w(Wkv[:, dc, :], w_mkv_d[dc * 128:(dc + 1) * 128, :], Wkv_b)
                load_w(Wmq[:, dc, :], w_in_d[dc * 128:(dc + 1) * 128, O_MQ:O_MQ + 512], Wmq_b)
            mkT, mkT_b = sb("mkT", [128, 4, 256], BF16, pm)
            mv, mv_b = sb("mv", [128, 2, 512], BF16, pm)
            mq, mq_b = sb("mq", [128, 512], BF16, pm)
            pT = [sb("pT%d" % i, [128, 512], BF16, pm) for i in range(2)]
            rd, rd_b = sb("rd", [128, 512], F32, pm)
            for h in range(4):
                ps, ps_b = PS[h % 4]
                op("pe", proj_T(ps[:, 0:256], Wkv, h * 128, 128, lambda k: mnT[:, k, :]),
                   reads=[Wkv_b, mnT_b], writes=[ps_b])
                op("act", lambda e: e.activation(out=mkT[:, h, :], in_=ps[:, 0:256], func=AF.Copy),
                   reads=[ps_b], writes=[mkT_b])
            for mt in range(2):
                ps, ps_b = PS[mt]

                def fmv(e):
                    ins = None
                    for k in range(8):
                        ins = e.matmul(ps[:, :], lhsT=mnT[:, k, mt * 128:(mt + 1) * 128], rhs=Wkv[:, k, 512:1024],
                                       start=(k == 0), stop=(k == 7))
                    return ins
                op("pe", fmv, reads=[Wkv_b, mnT_b], writes=[ps_b])
                op("act", lambda e: e.activation(out=mv[:, mt, :], in_=ps[:, :], func=AF.Copy),
                   reads=[ps_b], writes=[mv_b])
            for qt in range(NQ):
                ts = slice(qt * 512, (qt + 1) * 512)
                for h in range(4):
                    ps, ps_b = PS[0]
                    op("pe", proj_T(ps[:, :], Wmq, h * 128, 128, lambda k: uT[:, k, ts]),
                       reads=[Wmq_b, uT_b], writes=[ps_b])
                    op("dve", lambda e: e.tensor_scalar_mul(out=mq[:, :], in0=ps[:, :], scalar1=float(128 ** -0.5)),
                       reads=[ps_b], writes=[mq_b])
                    for mt in range(2):
                        pss, pss_b = PS[1 + mt]
                        p_t, p_b = pT[mt]
                        op("pe", lambda e: e.matmul(pss[:, :], lhsT=mkT[:, h, mt * 128:(mt + 1) * 128], rhs=mq[:, :],
                                                    start=True, stop=True), reads=[mkT_b, mq_b], writes=[pss_b])
                        op("act", lambda e: e.activation(out=p_t[:, :], in_=pss[:, :], func=AF.Exp),
                           reads=[pss_b], writes=[p_b])
                    pso, pso_b = PS[3]
                    psd, psd_b = PS[4]

                    def fo(e):
                        ins = None
                        for mt in range(2):
                            e.matmul(pso[:, :], lhsT=mv[:, mt, h * 128:(h + 1) * 128], rhs=pT[mt][0][:, :],
                                     start=(mt == 0), stop=(mt == 1))
                        for mt in range(2):
                            ins = e.matmul(psd[:, :], lhsT=ones_b, rhs=pT[mt][0][:, :], start=(mt == 0), stop=(mt == 1))
                        return ins
                    op("pe", fo, reads=[mv_b, pT[0][1], pT[1][1], cb_b], writes=[pso_b, psd_b])
                    op("dve", lambda e: e.reciprocal(out=rd[:, :], in_=psd[:, :]), reads=[psd_b], writes=[rd_b])
                    op("dve", lambda e: e.tensor_tensor(out=omemT[:, h, ts], in0=pso[:, :], in1=rd[:, :], op=ALU.mult),
                       reads=[pso_b, rd_b], writes=[omemT_b])
        kb.barrier()
        if STOP == "Tmem":
            return nc

        gsg_b = Buf()
        with ExitStack() as ph:
            Wg, Wg_b = sb("Wg", [128, 8, 1552], BF16, ph)
            for dc in range(8):
                load_w(Wg[:, dc, :], w_in_d[dc * 128:(dc + 1) * 128, 0:1552], Wg_b)
            wup, wup_b = sb("wup", [128, 256], BF16, ph)
            op("dve", lambda e: e.memset(wup[:, :], 0.0), writes=[wup_b])
            load_w(wup[0:16, :], w_up_d[:, :], wup_b)
            balp, balp_b = sb("balp", [128, 256], F32, ph)
            dma("sp", balp[:, :], b_alpha_d.partition_broadcast(128), writes=[balp_b])
            ggl, ggl_b = sb("ggl", [128, 4], F32, ph)
            with nc.allow_non_contiguous_dma(reason="tiny gain transpose"):
                dma("sp", ggl[:, :], g_gla_d.rearrange("h p -> p h"), writes=[ggl_b])
            cmask, cmask_b = sb("cmask", [128, 8], F32, ph)
            dma("sp", cmask[:, :], cmask_d[:, :], writes=[cmask_b])

            qpT, qpT_b = sb("qpT", [128, 2, T], BF16, ph)
            attn, attn_b = sb("attn", [128, NT, 512], BF16, ph)
            kdec, kdec_b = sb("kdec", [128, NT, 2, 256], BF16, ph)
            op("dve", lambda e: e.memset(kdec[:, :, :, :], 0.0), writes=[kdec_b])
            vtok, vtok_b = sb("vtok", [128, NT, 512], BF16, ph)
            decs, decs_b = sb("decs", [128, 2, 32], F32, ph)
            qT, qT_b = sb("qT", [128, 2, 512], F32, ph)
            kT, kT_b = sb("kT", [128, 2, 512], F32, ph)
            gaT, gaT_b = sb("gaT", [128, 512], BF16, ph)
            op("dve", lambda e: e.memset(gaT[:, :], 0.0), writes=[gaT_b])
            lsb, lsb_b = sb("lsb", [128, 256], F32, ph)
            Ep, Ep_b = sb("Ep", [128, 256], F32, ph)
            En, En_b = sb("En", [128, 256], F32, ph)
            edl, edl_b = sb("edl", [128, 256], F32, ph)
            qn, qn_b = sb("qn", [128, 256], BF16, ph)
            kp, kp_b = sb("kp", [128, 2, 2, 128], BF16, ph)
            kn, kn_b = sb("kn", [128, 2, 2, 128], BF16, ph)
            op("dve", lambda e: e.memset(kp[:, :, :, :], 0.0), writes=[kp_b])
            op("dve", lambda e: e.memset(kn[:, :, :, :], 0.0), writes=[kn_b])
            t1, t1_b = sb("t1", [128, 512], F32, ph)
            t2, t2_b = sb("t2", [128, 512], F32, ph)
            pi = 0
            for qt in range(NQ):
                ts = slice(qt * 512, (qt + 1) * 512)
                for fc in range(2):
                    ps, ps_b = PS[pi % 4]
                    pi += 1
                    op("pe", proj_T(ps[:, :], Wg, O_GQ + fc * 128, 128, lambda k: uT[:, k, ts]),
                       reads=[Wg_b, uT_b], writes=[ps_b])
                    op("act", lambda e: e.activation(out=qT[:, fc, :], in_=ps[:, :], func=AF.Copy, scale=0.125),
                       reads=[ps_b], writes=[qT_b])
                    ps, ps_b = PS[pi % 4]
                    pi += 1
                    op("pe", proj_T(ps[:, :], Wg, O_GK + fc * 128, 128, lambda k: uT[:, k, ts]),
                       reads=[Wg_b, uT_b], writes=[ps_b])
                    op("dve", lambda e: e.tensor_copy(out=kT[:, fc, :], in_=ps[:, :]), reads=[ps_b], writes=[kT_b])
                ps, ps_b = PS[pi % 4]
                pi += 1
                op("pe", proj_T(ps[0:16, :], Wg, O_GA, 16, lambda k: uT[:, k, ts]), reads=[Wg_b, uT_b], writes=[ps_b])
                op("dve", lambda e: e.tensor_copy(out=gaT[0:16, :], in_=ps[0:16, :]), reads=[ps_b], writes=[gaT_b])
                for tt in range(4):
                    ti = qt * 4 + tt
                    tk = slice(ti * 128, (ti + 1) * 128)
                    lt = slice(tt * 128, (tt + 1) * 128)
                    ps, ps_b = PS[pi % 4]
                    pi += 1
                    op("pe", lambda e: e.matmul(ps[:, 0:256], lhsT=gaT[:, lt], rhs=wup[:, :], start=True, stop=True),
                       reads=[gaT_b, wup_b], writes=[ps_b])
                    op("dve", lambda e: e.tensor_tensor(out=lsb[:, :], in0=ps[:, 0:256], in1=balp[:, :], op=ALU.add),
                       reads=[ps_b, balp_b], writes=[lsb_b])
                    op("act", lambda e: e.activation(out=lsb[:, :], in_=lsb[:, :], func=AF.Exp, scale=-1.0),
                       reads=[lsb_b], writes=[lsb_b])
                    op("act", lambda e: e.activation(out=lsb[:, :], in_=lsb[:, :], func=AF.Ln, bias=1.0, scale=1.0),
                       reads=[lsb_b], writes=[lsb_b])
                    ps, ps_b = PS[pi % 4]
                    pi += 1

                    def gv(e):
                        ins = None
                        for k in range(8):
                            ins = e.matmul(ps[:, :], lhsT=uT[:, k, tk], rhs=Wg[:, k, O_GV:O_GV + 512],
                                           start=(k == 0), stop=(k == 7))
                        return ins
                    op("pe", gv, reads=[Wg_b, uT_b], writes=[ps_b])
                    op("act", lambda e: e.activation(out=vtok[:, ti, :], in_=ps[:, :], func=AF.Copy),
                       reads=[ps_b], writes=[vtok_b])
                    psb, psb_b = PS[pi % 4]
                    pi += 1

                    def fb(e):
                        ins = None
                        for fc in range(2):
                            ins = e.matmul(psb[:, fc * 128:(fc + 1) * 128], lhsT=lsb[:, fc * 128:(fc + 1) * 128],
                                           rhs=cf[:, C_TRIB:C_TRIB + 128], start=True, stop=True)
                        return ins
                    op("pe", fb, reads=[lsb_b, cf_b], writes=[psb_b])
                    op("act", lambda e: e.activation(out=Ep[:, :], in_=psb[:, 0:256], func=AF.Exp),
                       reads=[psb_b], writes=[Ep_b])
                    op("act", lambda e: e.activation(out=En[:, :], in_=psb[:, 0:256], func=AF.Exp, scale=-1.0),
                       reads=[psb_b], writes=[En_b])
                    Ep3 = Ep[:, :].rearrange("p (f t) -> p f t", t=128)
                    En3 = En[:, :].rearrange("p (f t) -> p f t", t=128)
                    op("dve", lambda e: e.tensor_tensor(out=qpT[:, :, tk], in0=qT[:, :, lt], in1=Ep3, op=ALU.mult),
                       reads=[qT_b, Ep_b], writes=[qpT_b])
                    op("dve", lambda e: e.tensor_tensor(out=qn[:, :].rearrange("p (f t) -> p f t", t=128),
                                                        in0=qT[:, :, lt], in1=En3, op=ALU.mult),
                       reads=[qT_b, En_b], writes=[qn_b])
                    for a in range(2):
                        pr = slice(a * 64, (a + 1) * 64)
                        op("dve", lambda e: e.tensor_tensor(out=kp[pr, :, a, :], in0=kT[pr, :, lt], in1=Ep3[pr], op=ALU.mult),
                           reads=[kT_b, Ep_b], writes=[kp_b])
                        op("dve", lambda e: e.tensor_tensor(out=kn[pr, :, a, :], in0=kT[pr, :, lt], in1=En3[pr], op=ALU.mult),
                           reads=[kT_b, En_b], writes=[kn_b])
                    for fc in range(2):
                        op("dve", lambda e: e.tensor_copy(
                            out=decs[:, fc, 2 * ti:2 * ti + 2],
                            in_=Ep[:, fc * 128:(fc + 1) * 128].rearrange("p (j t) -> p j t", t=64)[:, :, 63]),
                           reads=[Ep_b], writes=[decs_b])
                    psd, psd_b = PS[pi % 4]
                    pi += 1
                    op("pe", lambda e: e.matmul(psd[:, 0:256], lhsT=cf[:, C_UPB:C_UPB + 128], rhs=lsb[:, :],
                                                start=True, stop=True), reads=[lsb_b, cf_b], writes=[psd_b])
                    op("act", lambda e: e.activation(out=edl[:, :], in_=psd[:, 0:256], func=AF.Exp),
                       reads=[psd_b], writes=[edl_b])
                    psk, psk_b = PS[pi % 4]
                    pi += 1

                    def gk(e):
                        ins = None
                        for k in range(8):
                            ins = e.matmul(psk[:, 0:256], lhsT=uT[:, k, tk], rhs=Wg[:, k, O_GK:O_GK + 256],
                                           start=(k == 0), stop=(k == 7))
                        return ins
                    op("pe", gk, reads=[Wg_b, uT_b], writes=[psk_b])
                    for j in range(2):
                        pr = slice(j * 64, (j + 1) * 64)
                        op("dve", lambda e: e.tensor_tensor(out=kdec[pr, ti, j, :], in0=psk[pr, 0:256], in1=edl[pr, :], op=ALU.mult),
                           reads=[psk_b, edl_b], writes=[kdec_b])
                    pac, pac_b = PS[4]
                    paa, paa_b = PS[5]

                    def fa(e):
                        ins = None
                        for h in range(4):
                            fc, a = h // 2, h % 2
                            e.matmul(pac[:, h * 128:(h + 1) * 128], lhsT=kn[:, fc, a, :],
                                     rhs=qpT[:, fc, tk], start=True, stop=True)
                            ins = e.matmul(paa[:, h * 128:(h + 1) * 128], lhsT=kp[:, fc, a, :],
                                           rhs=qn[:, fc * 128:(fc + 1) * 128], start=True, stop=True)
                        return ins
                    op("pe", fa, reads=[kn_b, kp_b, qn_b, qpT_b], writes=[pac_b, paa_b])
                    op("dve", lambda e: e.tensor_tensor(out=t1[:, :], in0=pac[:, :], in1=cf[:, C_U2:C_U2 + 512], op=ALU.mult),
                       reads=[pac_b, cf_b], writes=[t1_b])
                    op("dve", lambda e: e.tensor_tensor(out=t2[:, :], in0=paa[:, :], in1=cf[:, C_L2:C_L2 + 512], op=ALU.mult),
                       reads=[paa_b, cf_b], writes=[t2_b])
                    op("dve", lambda e: e.tensor_tensor(out=attn[:, ti, :], in0=t1[:, :], in1=t2[:, :], op=ALU.add),
                       reads=[t1_b, t2_b], writes=[attn_b])

            if STOP == "T3a":
                kb.barrier()
                return nc
            Sst, Sst_b = sb("Sst", [128, 2, 256], F32, ph)
            Pst, Pst_b = sb("Pst", [128, 2], F32, ph)
            Sbf, Sbf_b = sb("Sbf", [128, 2, 2, 128], BF16, ph)
            op("dve", lambda e: e.memset(Sbf[:, :, :, :], 0.0), writes=[Sbf_b])

            def kv_update(n):
                ti, j = n // 2, n % 2
                for fc in range(2):
                    ps, ps_b = PS[(2 * n + fc) % 4]
                    op("pe", lambda e: e.matmul(ps[:, 0:256], lhsT=kdec[:, ti, j, fc * 128:(fc + 1) * 128],
                                                rhs=vtok[:, ti, fc * 256:(fc + 1) * 256],
                                                start=True, stop=True),
                       reads=[kdec_b, vtok_b], writes=[ps_b])
                    op("dve", lambda e: e.scalar_tensor_tensor(out=Sst[:, fc, :], in0=Sst[:, fc, :],
                                                               scalar=decs[:, fc, n:n + 1], in1=ps[:, 0:256],
                                                               op0=ALU.mult, op1=ALU.add),
                       reads=[ps_b, decs_b, Sst_b], writes=[Sst_b])

            op("dve", lambda e: e.memset(Sst[:, :, :], 0.0), writes=[Sst_b])
            op("dve", lambda e: e.memset(Pst[:, :], 1.0), writes=[Pst_b])
            for n in range(32):
                kv_update(n)
                op("dve", lambda e: e.tensor_tensor(out=Pst[:, :], in0=Pst[:, :], in1=decs[:, :, n], op=ALU.mult),
                   reads=[decs_b, Pst_b], writes=[Pst_b])
            gs_b = Buf()
            dma("sp", gs[:, 0:512], Sst[:, :, :].rearrange("p f v -> p (f v)"), reads=[Sst_b], writes=[gs_b])
            dma("sp", gs[:, 512:514], Pst[:, :], reads=[Pst_b], writes=[gs_b])
            kb.collective(gs_t, gsg_t, reads=[gs_b], writes=[gsg_b])

            kb.barrier()

            if STOP == "T3b":
                kb.barrier()
                return nc
            Ain, Ain_b = sb("Ain", [128, 516], F32, ph)
            Pp, Pp_b = sb("Pp", [128, 2], F32, ph)
            op("dve", lambda e: e.memset(Sst[:, :, :], 0.0), writes=[Sst_b])
            for c2 in range(NCORE - 1):
                dma("sp", Ain[:, :], gsg[c2 * 128:(c2 + 1) * 128, :], reads=[gsg_b], writes=[Ain_b])
                op("dve", lambda e: e.tensor_scalar(out=Pp[:, :], in0=Ain[:, 512:514], scalar1=-1.0,
                                                    scalar2=cmask[:, c2:c2 + 1], op0=ALU.add, op1=ALU.mult),
                   reads=[Ain_b, cmask_b], writes=[Pp_b])
                op("dve", lambda e: e.tensor_scalar_add(out=Pp[:, :], in0=Pp[:, :], scalar1=1.0),
                   reads=[Pp_b], writes=[Pp_b])
                op("dve", lambda e: e.tensor_scalar_mul(out=Ain[:, 0:512], in0=Ain[:, 0:512], scalar1=cmask[:, c2:c2 + 1]),
                   reads=[Ain_b, cmask_b], writes=[Ain_b])
                for fc in range(2):
                    op("dve", lambda e: e.scalar_tensor_tensor(out=Sst[:, fc, :], in0=Sst[:, fc, :],
                                                               scalar=Pp[:, fc:fc + 1], in1=Ain[:, fc * 256:(fc + 1) * 256],
                                                               op0=ALU.mult, op1=ALU.add),
                       reads=[Pp_b, Ain_b, Sst_b], writes=[Sst_b])

            if STOP == "T3c":
                kb.barrier()
                return nc
            og, og_b = sb("og", [128, 512], F32, ph)
            osq, osq_b = sb("osq", [128, 512], BF16, ph)
            rstd, rstd_b = sb("rstd", [128, 512], F32, ph)
            sg, sg_b = sb("sg", [128, 512], F32, ph)
            for ti in range(NT):
                tk = slice(ti * 128, (ti + 1) * 128)
                pog, pog_b = PS[4]
                for j in range(2):
                    n = 2 * ti + j
                    for a in range(2):
                        pr = slice(a * 64, (a + 1) * 64)
                        op("act", lambda e: e.activation(out=Sbf[pr, :, a, :], in_=Sst[pr, :, a * 128:(a + 1) * 128], func=AF.Copy),
                           reads=[Sst_b], writes=[Sbf_b])

                    def fo2(e):
                        ins = None
                        for h in range(4):
                            fc, a = h // 2, h % 2
                            cs = slice(h * 128 + j * 64, h * 128 + j * 64 + 64)
                            e.matmul(pog[:, cs], lhsT=vtok[:, ti, h * 128:(h + 1) * 128],
                                     rhs=attn[:, ti, h * 128 + j * 64:h * 128 + j * 64 + 64], start=True, stop=False)
                            ins = e.matmul(pog[:, cs], lhsT=Sbf[:, fc, a, :],
                                           rhs=qpT[:, fc, ti * 128 + j * 64:ti * 128 + j * 64 + 64],
                                           start=False, stop=True)
                        return ins
                    op("pe", fo2, reads=[vtok_b, attn_b, Sbf_b, qpT_b], writes=[pog_b])
                    kv_update(n)
                op("act", lambda e: e.activation(out=og[:, :], in_=pog[:, :], func=AF.Copy), reads=[pog_b], writes=[og_b])
                op("dve", lambda e: e.tensor_tensor(out=osq[:, :], in0=og[:, :], in1=og[:, :], op=ALU.mult),
                   reads=[og_b], writes=[osq_b])
                pss, pss_b = PS[5]
                op("pe", lambda e: e.matmul(pss[:, :], lhsT=ones_b, rhs=osq[:, :], start=True, stop=True),
                   reads=[osq_b, cb_b], writes=[pss_b])
                op("dve", lambda e: e.tensor_scalar(out=rstd[:, :], in0=pss[:, :], scalar1=1.0 / 128.0, scalar2=EPS,
                                                    op0=ALU.mult, op1=ALU.add), reads=[pss_b], writes=[rstd_b])
                op("act", lambda e: e.activation(out=rstd[:, :], in_=rstd[:, :], func=AF.Ln), reads=[rstd_b], writes=[rstd_b])
                op("act", lambda e: e.activation(out=rstd[:, :], in_=rstd[:, :], func=AF.Exp, scale=-0.5),
                   reads=[rstd_b], writes=[rstd_b])
                op("dve", lambda e: e.tensor_tensor(out=og[:, :], in0=og[:, :], in1=rstd[:, :], op=ALU.mult),
                   reads=[og_b, rstd_b], writes=[og_b])
                psg, psg_b = PS[6]

                def fgg(e):
                    ins = None
                    for h in range(4):
                        for k in range(8):
                            ins = e.matmul(psg[:, h * 128:(h + 1) * 128], lhsT=Wg[:, k, O_GG + h * 128:O_GG + (h + 1) * 128],
                                           rhs=uT[:, k, tk], start=(k == 0), stop=(k == 7))
                    return ins
                op("pe", fgg, reads=[Wg_b, uT_b], writes=[psg_b])
                op("act", lambda e: e.activation(out=sg[:, :], in_=psg[:, :], func=AF.Exp, scale=-1.0),
                   reads=[psg_b], writes=[sg_b])
                op("dve", lambda e: e.tensor_scalar_add(out=sg[:, :], in0=sg[:, :], scalar1=1.0), reads=[sg_b], writes=[sg_b])
                op("dve", lambda e: e.reciprocal(out=sg[:, :], in_=sg[:, :]), reads=[sg_b], writes=[sg_b])
                op("dve", lambda e: e.tensor_tensor(out=sg[:, :], in0=psg[:, :], in1=sg[:, :], op=ALU.mult),
                   reads=[psg_b, sg_b], writes=[sg_b])
                op("dve", lambda e: e.tensor_tensor(out=og[:, :], in0=og[:, :], in1=sg[:, :], op=ALU.mult),
                   reads=[og_b, sg_b], writes=[og_b])
                for h in range(4):
                    op("dve", lambda e: e.tensor_scalar_mul(out=oglaT[:, h, tk], in0=og[:, h * 128:(h + 1) * 128],
                                                            scalar1=ggl[:, h:h + 1]),
                       reads=[og_b, ggl_b], writes=[oglaT_b])
        kb.barrier()
        if STOP == "T3":
            return nc

        pay2_b = Buf()
        gat2_b = Buf()
        with ExitStack() as ph:
            Qp, Qp_b = sb("Qp", [128, S], BF16, ph)
            Kp, Kp_b = sb("Kp", [128, S], BF16, ph)
            op("dve", lambda e: e.memset(Qp[:, :], 0.0), writes=[Qp_b])
            op("pool", lambda e: e.memset(Kp[:, :], 0.0), writes=[Kp_b])
            Va, Va_b = sb("Va", [128, 128, 65], BF16, ph)
            L, L_b = sb("L", [128, 128], F32, ph)
            gatq = gat.rearrange("(r h q) c -> h q r c", r=NCORE, h=8)[v_cid]
            dma("sp", Qp[0:64, :].rearrange("q (r c) -> q r c", r=NCORE), gatq[0:64], reads=[gat_b], writes=[Qp_b])
            dma("sp", Kp[0:64, :].rearrange("q (r c) -> q r c", r=NCORE), gatq[64:128], reads=[gat_b], writes=[Kp_b])
            gatv = gat.rearrange("(r h q) c -> h r q c", r=NCORE, h=8)[v_cid][:, 128:193, :]
            gatv = gatv.rearrange("r q c -> r (q c)").rearrange("r (p kc) -> p r kc", p=128)
            dma("sp", Va[:, :, :].rearrange("p (r k) c -> p r (k c)", r=NCORE), gatv, reads=[gat_b], writes=[Va_b])
            dma("sp", L[:, :], gatf.rearrange("(r h) (i p) -> h r i p", h=8, p=128)[v_cid], reads=[gatf_b], writes=[L_b])
            op("dve", lambda e: e.memset(Kp[64:65, :], 1.0), writes=[Kp_b])
            LT, LT_b = sb("LT", [128, 128], F32, ph)
            Xr, Xr_b = sb("Xr", [128, 128], F32, ph)
            Fk, Fk_b = sb("Fk", [128, 128], F32, ph)
            Rrep, Rrep_b = sb("Rrep", [128, 32], F32, ph)
            Cb, Cb_b = sb("Cb", [128, 128], BF16, ph)
            ps, ps_b = PS[0]
            op("pe", lambda e: e.matmul(ps[:, 0:128], lhsT=L[:, :], rhs=ident_f, start=True, stop=True),
               reads=[L_b, cf_b], writes=[ps_b])
            op("dve", lambda e: e.tensor_copy(out=LT[:, :], in_=ps[:, 0:128]), reads=[ps_b], writes=[LT_b])
            ps, ps_b = PS[1]
            op("pe", lambda e: e.matmul(ps[:, 0:128], lhsT=LT[:, :], rhs=ones_f, start=True, stop=True),
               reads=[LT_b, cf_b], writes=[ps_b])
            op("dve", lambda e: e.tensor_copy(out=Xr[:, :], in_=ps[:, 0:128]), reads=[ps_b], writes=[Xr_b])
            ps, ps_b = PS[2]

            def fF(e):
                e.matmul(ps[:, 0:128], lhsT=cf[:, C_TRI:C_TRI + 128], rhs=LT[:, :], start=True, stop=False)
                return e.matmul(ps[:, 0:128], lhsT=Xr[:, :], rhs=cf[:, C_SU:C_SU + 128], start=False, stop=True)
            op("pe", fF, reads=[LT_b, Xr_b, cf_b], writes=[ps_b])
            op("dve", lambda e: e.tensor_copy(out=Fk[:, :], in_=ps[:, 0:128]), reads=[ps_b], writes=[Fk_b])
            ps, ps_b = PS[3]
            op("pe", lambda e: e.matmul(ps[:, 0:32], lhsT=Xr[:, :], rhs=cf[:, C_SU4:C_SU4 + 32], start=True, stop=True),
               reads=[Xr_b, cf_b], writes=[ps_b])
            op("dve", lambda e: e.tensor_copy(out=Rrep[:, :], in_=ps[:, 0:32]), reads=[ps_b], writes=[Rrep_b])
            ps, ps_b = PS[0]

            def fC(e):
                e.matmul(ps[:, 0:128], lhsT=LT[:, :], rhs=cf[:, C_TRI:C_TRI + 128], start=True, stop=False)
                return e.matmul(ps[:, 0:128], lhsT=cf[:, C_G4:C_G4 + 128], rhs=Xr[:, :], start=False, stop=True)
            op("pe", fC, reads=[LT_b, Xr_b, cf_b], writes=[ps_b])
            op("dve", lambda e: e.tensor_copy(out=Cb[:, :], in_=ps[:, 0:128]), reads=[ps_b], writes=[Cb_b])
            cd_b = Buf()
            dma("sp", cd[:, :], Cb[:, :], reads=[Cb_b], writes=[cd_b])
            dma("sp", Qp[64:65, :], cd.rearrange("(o k) p -> o (k p)", o=1), reads=[cd_b], writes=[Qp_b])

            biasq = [sb("biasq%d" % i, [128, 128], F32, ph) for i in range(2)]
            PT = [sb("PT%d" % i, [128, 512], BF16, ph) for i in range(3)]
            rec, rec_b = sb("rec", [128, 512], F32, ph)
            op("dve", lambda e: e.memset(rec[:, :], 0.0), writes=[rec_b])
            bcs, bcs_b = sb("bcs", [64, 512], F32, ph)
            ost = [sb("ost%d" % i, [64, 512], BF16, ph) for i in range(2)]
            step = 0
            for qi in range(S // 512):
                nkt = 4 * qi + 4
                qs = slice(qi * 512, (qi + 1) * 512)
                bq, bq_b = biasq[qi % 2]
                op("dve", lambda e: e.tensor_scalar(out=bq[:, 0:nkt], in0=Fk[:, 0:nkt], scalar1=Rrep[:, qi:qi + 1],
                                                    scalar2=-1.0, op0=ALU.subtract, op1=ALU.mult),
                   reads=[Fk_b, Rrep_b], writes=[bq_b])
                pso, pso_b = PS[3 + (qi % 2)]
                for kt in range(nkt):
                    pss, pss_b = PS[step % 3]
                    p_t, p_b = PT[step % 3]
                    step += 1
                    j = kt - 4 * qi

                    def fs(e):
                        ins = e.matmul(pss[:, :], lhsT=Kp[:, kt * 128:(kt + 1) * 128], rhs=Qp[:, qs],
                                       start=True, stop=(j < 0))
                        if j >= 0:
                            ins = e.matmul(pss[:, :], lhsT=ident_b, rhs=cb[:, B_MASK + j * 512:B_MASK + (j + 1) * 512],
                                           start=False, stop=True)
                        return ins
                    op("pe", fs, reads=[Kp_b, Qp_b, cb_b], writes=[pss_b])
                    op("act", lambda e: e.activation(out=p_t[:, :], in_=pss[:, :], func=AF.Exp, bias=bq[:, kt:kt + 1], scale=1.0),
                       reads=[pss_b, bq_b], writes=[p_b])
                    op("pe", lambda e: e.matmul(pso[0:65, :], lhsT=Va[:, kt, 0:65], rhs=p_t[:, :],
                                                start=(kt == 0), stop=(kt == nkt - 1)),
                       reads=[Va_b, p_b], writes=([pso_b] if kt in (0, nkt - 1) else []))
                op("dve", lambda e: e.reciprocal(out=rec[64:65, :], in_=pso[64:65, :]), reads=[pso_b], writes=[rec_b])
                pbc, pbc_b = PS[5]
                op("pe", lambda e: e.matmul(pbc[0:64, :], lhsT=cf[:, C_SEL:C_SEL + 64], rhs=rec[:, :],
                                            start=True, stop=True), reads=[rec_b, cf_b], writes=[pbc_b])
                op("act", lambda e: e.activation(out=bcs[:, :], in_=pbc[0:64, :], func=AF.Copy), reads=[pbc_b], writes=[bcs_b])
                o_t, o_b = ost[qi % 2]
                op("dve", lambda e: e.tensor_tensor(out=o_t[:, :], in0=pso[0:64, :], in1=bcs[:, :], op=ALU.mult),
                   reads=[pso_b, bcs_b], writes=[o_b])
                dma("sp", pay2[:, qs], o_t[:, :], reads=[o_b], writes=[pay2_b])
            kb.collective(pay2_t, gat2_t, reads=[pay2_b], writes=[gat2_b])
        kb.barrier()
        if STOP == "FOX":
            return nc

        es_m = ExitStack()
        mT, mT_b = sb("mT", [128, 8, T], BF16, es_m)
        with ExitStack() as ph:
            ofoxT, ofoxT_b = sb("ofoxT", [128, 4, T], BF16, ph)
            gat2v = gat2.rearrange("(kc a d) s -> a d kc s", a=2, d=64)
            for a in range(2):
                dma("sp", ofoxT[a * 64:(a + 1) * 64, :, :], gat2v[a][:, :, bass.ds(v_off, T)],
                    reads=[gat2_b], writes=[ofoxT_b])
            oT = [(oglaT, oglaT_b), (ofoxT, ofoxT_b), (omemT, omemT_b)]
            Wgt = [sb("Wgt%d" % i, [128, 8, 1024], BF16, ph) for i in range(2)]
            Wo = [sb("Wo%d" % i, [128, 4, 1024], BF16, ph) for i in range(2)]
            sgm, sgm_b = sb("sgm", [128, 512], BF16, ph)
            tmp, tmp_b = sb("tmpm", [128, 512], BF16, ph)
            pi = 0
            for b in range(3):
                wg_t, wg_b = Wgt[b % 2]
                wo_t, wo_b = Wo[b % 2]
                for dc in range(8):
                    load_w(wg_t[:, dc, :], w_in_d[dc * 128:(dc + 1) * 128, O_GATES + b * 1024:O_GATES + (b + 1) * 1024], wg_b)
                for kc in range(4):
                    load_w(wo_t[:, kc, :], w_o_d[b][kc * 128:(kc + 1) * 128, :], wo_b)
                o_t, o_b = oT[b]
                for qt in range(NQ):
                    ts = slice(qt * 512, (qt + 1) * 512)
                    for fc in range(8):
                        psg, psg_b = PS[pi % 3]
                        psy, psy_b = PS[3 + pi % 3]
                        pi += 1
                        op("pe", proj_T(psg[:, :], wg_t, fc * 128, 128, lambda k: uT[:, k, ts]), reads=[wg_b, uT_b], writes=[psg_b])
                        op("pe", proj_T(psy[:, :], wo_t, fc * 128, 128, lambda k: o_t[:, k, ts], nk=4), reads=[wo_b, o_b], writes=[psy_b])
                        op("act", lambda e: e.activation(out=sgm[:, :], in_=psg[:, :], func=AF.Sigmoid), reads=[psg_b], writes=[sgm_b])
                        if b == 0:
                            op("dve", lambda e: e.tensor_tensor(out=mT[:, fc, ts], in0=psy[:, :], in1=sgm[:, :], op=ALU.mult),
                               reads=[psy_b, sgm_b], writes=[mT_b])
                        else:
                            op("dve", lambda e: e.tensor_tensor(out=tmp[:, :], in0=psy[:, :], in1=sgm[:, :], op=ALU.mult),
                               reads=[psy_b, sgm_b], writes=[tmp_b])
                            op("dve", lambda e: e.tensor_tensor(out=mT[:, fc, ts], in0=mT[:, fc, ts], in1=tmp[:, :], op=ALU.add),
                               reads=[tmp_b, mT_b], writes=[mT_b])
        kb.barrier()
        if STOP == "T5a":
            return nc
        es_r.close()

        es_h = ExitStack()
        hres, hres_b = sb("hres", [128, NT, D], F32, es_h, "right")
        with ExitStack() as ph:
            Wout, Wout_b = sb("Wout", [128, 8, 1024], BF16, ph)
            for kc in range(8):
                load_w(Wout[:, kc, :], w_out_d[kc * 128:(kc + 1) * 128, :], Wout_b)
            gm2, gm2_b = sb("gm2", [128, 8], F32, ph)
            with nc.allow_non_contiguous_dma(reason="tiny gain vector transpose"):
                dma("sp", gm2[:, :], g_ffn_d.rearrange("o (dc p) -> p (o dc)", p=128), writes=[gm2_b])
            xt = [sb("bxt%d" % i, [128, D], F32, ph) for i in range(2)]
            xn = [sb("bxn%d" % i, [128, D], BF16, ph) for i in range(2)]
            junk, junk_b = sb("bjunk", [128, D], BF16, ph)
            ss, ss_b = sb("bss", [128, 16], F32, ph)
            rr, rr_b = sb("brr", [128, 16], F32, ph)
            for ti in range(NT):
                tk = slice(ti * 128, (ti + 1) * 128)
                xt_t, xt_b = xt[ti % 2]
                xn_t, xn_b = xn[ti % 2]
                dma("sp", xt_t[:, :], x_d[tk, :], writes=[xt_b])
                for ch in range(2):
                    ps, ps_b = PS[(2 * ti + ch) % 4]

                    def fh(e):
                        ins = None
                        for k in range(8):
                            ins = e.matmul(ps[:, :], lhsT=mT[:, k, tk], rhs=Wout[:, k, ch * 512:(ch + 1) * 512],
                                           start=(k == 0), stop=(k == 7))
                        return ins
                    op("pe", fh, reads=[mT_b, Wout_b], writes=[ps_b])
                    op("dve", lambda e: e.tensor_tensor(out=hres[:, ti, ch * 512:(ch + 1) * 512], in0=ps[:, :],
                                                        in1=xt_t[:, ch * 512:(ch + 1) * 512], op=ALU.add),
                       reads=[ps_b, xt_b], writes=[hres_b])
                op("act", lambda e: e.activation(out=junk[:, :], in_=hres[:, ti, :], func=AF.Square, accum_out=ss[:, ti:ti + 1]),
                   reads=[hres_b], writes=[junk_b, ss_b])
                op("dve", lambda e: e.tensor_scalar(out=rr[:, ti:ti + 1], in0=ss[:, ti:ti + 1], scalar1=1.0 / D, scalar2=EPS,
                                                    op0=ALU.mult, op1=ALU.add), reads=[ss_b], writes=[rr_b])
                op("act", lambda e: e.activation(out=rr[:, ti:ti + 1], in_=rr[:, ti:ti + 1], func=AF.Ln), reads=[rr_b], writes=[rr_b])
                op("act", lambda e: e.activation(out=rr[:, ti:ti + 1], in_=rr[:, ti:ti + 1], func=AF.Exp, scale=-0.5),
                   reads=[rr_b], writes=[rr_b])
                op("dve", lambda e: e.tensor_scalar_mul(out=xn_t[:, :], in0=hres[:, ti, :], scalar1=rr[:, ti:ti + 1]),
                   reads=[hres_b, rr_b], writes=[xn_b])

                def tr2(e):
                    ins = None
                    for dc in range(8):
                        ins = e.transpose(out=PST[0][:, dc * 128:(dc + 1) * 128], in_=xn_t[:, dc * 128:(dc + 1) * 128],
                                          identity=ident_b)
                    return ins
                op("pe", tr2, reads=[xn_b, cb_b], writes=[PST[1]])
                for dc in range(8):
                    if dc % 2 == 0:
                        op("dve", lambda e: e.tensor_scalar_mul(out=uT[:, dc, tk], in0=PST[0][:, dc * 128:(dc + 1) * 128],
                                                                scalar1=gm2[:, dc:dc + 1]),
                           reads=[PST[1], gm2_b], writes=[uT_b])
                    else:
                        op("act", lambda e: e.activation(out=uT[:, dc, tk], in_=PST[0][:, dc * 128:(dc + 1) * 128],
                                                         func=AF.Copy, scale=gm2[:, dc:dc + 1]),
                           reads=[PST[1], gm2_b], writes=[uT_b])
        kb.barrier()
        if STOP == "T5b":
            return nc
        es_m.close()

        with ExitStack() as ph:
            W1 = [sb("W1_%d" % i, [128, 8, 1024], BF16, ph) for i in range(2)]
            W2 = [sb("W2_%d" % i, [128, 8, 1024], BF16, ph) for i in range(2)]
            aT = [sb("aT%d" % i, [128, 8, 512], BF16, ph) for i in range(2)]
            pi = 0
            ai = 0
            for qf in range(4):
                w1_t, w1_b = W1[qf % 2]
                w2_t, w2_b = W2[qf % 2]
                for dc in range(8):
                    load_w(w1_t[:, dc, :], w_ff1_d[dc * 128:(dc + 1) * 128, qf * 1024:(qf + 1) * 1024], w1_b)
                for fc in range(8):
                    load_w(w2_t[:, fc, :], w_ff2_d[qf * 1024 + fc * 128:qf * 1024 + (fc + 1) * 128, :], w2_b)
                for qt in range(NQ):
                    ts = slice(qt * 512, (qt + 1) * 512)
                    a_t, a_b = aT[ai % 2]
                    ai += 1
                    for fc in range(8):
                        ps, ps_b = PS[pi % 6]
                        pi += 1
                        op("pe", proj_T(ps[:, :], w1_t, fc * 128, 128, lambda k: uT[:, k, ts]), reads=[w1_b, uT_b], writes=[ps_b])
                        op("act", lambda e: e.activation(out=a_t[:, fc, :], in_=ps[:, :], func=AF.Relu), reads=[ps_b], writes=[a_b])
                        op("dve" if fc % 2 == 0 else "pool",
                           lambda e: e.tensor_tensor(out=a_t[:, fc, :], in0=a_t[:, fc, :], in1=a_t[:, fc, :], op=ALU.mult),
                           reads=[a_b], writes=[a_b])
                    for tt in range(4):
                        ti = qt * 4 + tt
                        for ch in range(2):
                            ps, ps_b = PS[pi % 6]
                            pi += 1

                            def f2(e):
                                ins = None
                                for fc in range(8):
                                    ins = e.matmul(ps[:, :], lhsT=a_t[:, fc, tt * 128:(tt + 1) * 128],
                                                   rhs=w2_t[:, fc, ch * 512:(ch + 1) * 512], start=(fc == 0), stop=(fc == 7))
                                return ins
                            op("pe", f2, reads=[a_b, w2_b], writes=[ps_b])
                            op("dve", lambda e: e.tensor_tensor(out=hres[:, ti, ch * 512:(ch + 1) * 512],
                                                                in0=hres[:, ti, ch * 512:(ch + 1) * 512], in1=ps[:, :], op=ALU.add),
                               reads=[ps_b, hres_b], writes=[hres_b])
        kb.barrier()
        if STOP == "T5c":
            return nc

        with ExitStack() as ph:
            gfb, gfb_b = sb("gfb", [128, D], F32, ph)
            dma("sp", gfb[:, :], g_fin_d.partition_broadcast(128), writes=[gfb_b])
            junk, junk_b = sb("djunk", [128, D], BF16, ph)
            ss, ss_b = sb("dss", [128, 16], F32, ph)
            rr, rr_b = sb("drr", [128, 16], F32, ph)
            yo = [sb("yo%d" % i, [128, D], F32, ph) for i in range(2)]
            out_b = Buf()
            for ti in range(NT):
                y_t, y_b = yo[ti % 2]
                op("act", lambda e: e.activation(out=junk[:, :], in_=hres[:, ti, :], func=AF.Square, accum_out=ss[:, ti:ti + 1]),
                   reads=[hres_b], writes=[junk_b, ss_b])
                op("dve", lambda e: e.tensor_scalar(out=rr[:, ti:ti + 1], in0=ss[:, ti:ti + 1], scalar1=1.0 / D, scalar2=EPS,
                                                    op0=ALU.mult, op1=ALU.add), reads=[ss_b], writes=[rr_b])
                op("act", lambda e: e.activation(out=rr[:, ti:ti + 1], in_=rr[:, ti:ti + 1], func=AF.Ln), reads=[rr_b], writes=[rr_b])
                op("act", lambda e: e.activation(out=rr[:, ti:ti + 1], in_=rr[:, ti:ti + 1], func=AF.Exp, scale=-0.5),
                   reads=[rr_b], writes=[rr_b])
                op("dve", lambda e: e.scalar_tensor_tensor(out=y_t[:, :], in0=hres[:, ti, :], scalar=rr[:, ti:ti + 1],
                                                           in1=gfb[:, :], op0=ALU.mult, op1=ALU.mult),
                   reads=[hres_b, rr_b, gfb_b], writes=[y_b])
                dma("sp", out_d[ti * 128:(ti + 1) * 128, :], y_t[:, :], reads=[y_b], writes=[out_b])
        kb.barrier()
        if STOP == "T5d":
            return nc
        es_h.close()
    return nc


_CACHE = {}


def kernel(x, mem, g_mix, w_in, w_alpha_up, b_alpha, b_forget, g_gla_head, g_mem, w_mem_kv,
           w_gla_o, w_fox_o, w_mem_o, w_out, g_ffn, w_ff1, w_ff2, g_final):
    f = lambda a: np.ascontiguousarray(np.asarray(a, dtype=np.float32))
    if "nc" not in _CACHE:
        _CACHE["nc"] = build_nc()
        _CACHE["c"] = _consts()
    nc = _CACHE["nc"]
    cf, cb = _CACHE["c"]
    xs = f(x).reshape(S, D)
    shared = {
        "mem": f(mem).reshape(256, D), "g_mix": f(g_mix).reshape(1, D), "w_in": f(w_in).reshape(D, D_IN),
        "w_alpha_up": f(w_alpha_up).reshape(16, 256), "b_alpha": f(b_alpha).reshape(1, 256),
        "b_forget": f(b_forget).reshape(8, 1), "g_gla_head": f(g_gla_head).reshape(4, 128),
        "g_mem": f(g_mem).reshape(1, D), "w_mem_kv": f(w_mem_kv).reshape(D, 1024),
        "w_gla_o": f(w_gla_o).reshape(512, D), "w_fox_o": f(w_fox_o).reshape(512, D),
        "w_mem_o": f(w_mem_o).reshape(512, D), "w_out": f(w_out).reshape(D, D),
        "g_ffn": f(g_ffn).reshape(1, D), "w_ff1": f(w_ff1).reshape(D, 4096), "w_ff2": f(w_ff2).reshape(4096, D),
        "g_final": f(g_final).reshape(1, D), "cf": cf, "cb": cb,
    }
    big = ("mem", "w_in", "w_mem_kv", "w_gla_o", "w_fox_o", "w_mem_o", "w_out", "w_ff1", "w_ff2", "cf")
    in_maps = []
    for c in range(NCORE):
        m = dict(shared)
        for k in big:
            a = shared[k]
            m[k] = np.concatenate([a, np.full((1, a.shape[1]), float(c), a.dtype)], axis=0)
        m["x"] = xs[c * T:(c + 1) * T]
        m["cid"] = np.array([[c, c * T]], np.int32)
        m["cmask"] = np.tile((np.arange(8) < c).astype(np.float32)[None, :], (128, 1))
        in_maps.append(m)
    res = run_bass_kernel_spmd(nc, in_maps, core_ids=list(range(NCORE)))
    out = np.concatenate([np.asarray(r["out"], dtype=np.float32) for r in res.results], axis=0)
    return out.reshape(1, S, D)
```

```python
import numpy as np
import ml_dtypes
from contextlib import ExitStack
import concourse.bass as bass
import concourse.mybir as mybir
from concourse.bass_utils import run_bass_kernel_spmd

F32 = mybir.dt.float32
BF16 = mybir.dt.bfloat16
I32 = mybir.dt.int32
AF = mybir.ActivationFunctionType
ALU = mybir.AluOpType

NCORE = 8
D = 1024
S = 16384
T = S // NCORE
NT = T // 128
NQ = T // 512
EPS = 1e-6
D_IN = 6680
O_GQ, O_GK, O_GV, O_GG, O_GA = 0, 256, 512, 1024, 1536
O_FQ, O_FK, O_FV, O_FF, O_MQ, O_GATES = 1552, 2064, 2576, 3088, 3096, 3608
PAY_R = 193
NDS = 40

C_IDENT, C_TRI, C_SU, C_G4, C_TRIB, C_UPB, C_ONES, C_U2, C_L2, C_SU4 = (
    0, 128, 256, 384, 512, 640, 768, 896, 1408, 1920)
C_SEL = 1952
C_TOT = 2016
B_IDENT, B_ONES, B_MASK = 0, 128, 256
B_TOT = 256 + 4 * 512


def _consts():
    p = np.arange(128)
    cf = np.zeros((128, C_TOT), np.float32)
    cf[:, C_IDENT:C_IDENT + 128] = np.eye(128)
    cf[:, C_TRI:C_TRI + 128] = (p[:, None] <= p[None, :])
    cf[:, C_SU:C_SU + 128] = (p[:, None] < p[None, :])
    cf[:, C_G4:C_G4 + 128] = (p[:, None] < p[None, :]) & ((p[:, None] // 4) == (p[None, :] // 4))
    same = (p[:, None] // 64) == (p[None, :] // 64)
    cf[:, C_TRIB:C_TRIB + 128] = np.where(same & (p[:, None] <= p[None, :]), -1.0 / 16.0, 0.0)
    cf[:, C_UPB:C_UPB + 128] = np.where(same & (p[:, None] > p[None, :]), -1.0 / 16.0, 0.0)
    cf[:, C_ONES:C_ONES + 128] = 1.0
    u2 = (same & (p[None, :] >= p[:, None])).astype(np.float32)
    l2 = (same & (p[None, :] < p[:, None])).astype(np.float32)
    cf[:, C_U2:C_U2 + 512] = np.tile(u2, (1, 4))
    cf[:, C_L2:C_L2 + 512] = np.tile(l2, (1, 4))
    cf[:, C_SU4:C_SU4 + 32] = (p[:, None] < 4 * np.arange(32)[None, :])
    cf[64, C_SEL:C_SEL + 64] = 1.0
    cb = np.zeros((128, B_TOT), np.float32)
    cb[:, B_IDENT:B_IDENT + 128] = np.eye(128)
    cb[:, B_ONES:B_ONES + 128] = 1.0
    cq = np.arange(512)
    for j in range(4):
        cb[:, B_MASK + j * 512:B_MASK + (j + 1) * 512] = np.where(
            (128 * j + p[:, None]) > cq[None, :], -30000.0, 0.0)
    return cf, cb.astype(ml_dtypes.bfloat16)


class Buf:
    __slots__ = ("w", "r")

    def __init__(self):
        self.w = None
        self.r = {}


class _FirstIns:
    def __init__(self, eng, first):
        self._eng = eng
        self._first = first

    def __getattr__(self, name):
        f = getattr(self._eng, name)

        def call(*a, **k):
            r = f(*a, **k)
            if not self._first:
                self._first.append(r)
            return r
        return call


class KB:
    def __init__(self, nc, es):
        self.nc = nc
        self.eng = {"pe": nc.tensor, "act": nc.scalar, "dve": nc.vector, "pool": nc.gpsimd,
                    "sp": nc.sync}
        self.psem = {e: es.enter_context(nc.semaphore("p_" + e)) for e in ("pe", "act", "dve", "pool")}
        self.cnt = {e: 0 for e in self.psem}
        self.dsem = [es.enter_context(nc.semaphore("d%d" % i)) for i in range(NDS)]
        self.dval = [0] * NDS
        self.dnext = 0
        self.csems = [es.enter_context(nc.semaphore("cc%d" % i)) for i in range(4)]
        self.cidx = 0
        self.seen = {e: {} for e in self.eng}

    def _sem(self, key):
        if isinstance(key, tuple) and key[0] == "c":
            return self.csems[key[1]]
        return self.psem[key] if isinstance(key, str) else self.dsem[key[1]]

    def wait(self, e, tok):
        if tok is None:
            return
        key, val = tok
        if self.seen[e].get(key, 0) >= val:
            return
        self.eng[e].wait_ge(self._sem(key), val)
        self.seen[e][key] = val

    def _deps(self, e, reads, writes):
        for b in reads:
            self.wait(e, b.w)
        for b in writes:
            self.wait(e, b.w)
            for k, v in b.r.items():
                self.wait(e, (k, v))

    def _commit(self, tok, reads, writes):
        for b in writes:
            b.w = tok
            b.r = {}
        for b in reads:
            if b.r.get(tok[0], 0) < tok[1]:
                b.r[tok[0]] = tok[1]

    def _need(self, e, reads, writes):
        need = {}

        def add(tok):
            if tok is None:
                return
            key, val = tok
            if self.seen[e].get(key, 0) >= val or need.get(key, 0) >= val:
                return
            need[key] = val
        for b in reads:
            add(b.w)
        for b in writes:
            add(b.w)
            for k, v in b.r.items():
                add((k, v))
        return list(need.items())

    def op(self, e, fn, reads=(), writes=()):
        need = self._need(e, reads, writes)
        for tok in need[:-1]:
            self.wait(e, tok)
        first = []
        ins = fn(_FirstIns(self.eng[e], first))
        if need:
            key, val = need[-1]
            first[0]._wait_ge(self._sem(key), val)
            self.seen[e][key] = val
        self.cnt[e] += 1
        ins.then_inc(self.psem[e], 1)
        tok = (e, self.cnt[e])
        self._commit(tok, reads, writes)
        return tok

    def dma(self, q, out, in_, reads=(), writes=(), **kw):
        self._deps(q, reads, writes)
        i = self.dnext
        self.dnext = (i + 1) % NDS
        if self.dval[i] > 0:
            self.wait(q, (("d", i), self.dval[i]))
        ins = self.eng[q].dma_start(out=out, in_=in_, **kw)
        self.dval[i] += 16
        ins.then_inc(self.dsem[i], 16)
        tok = (("d", i), self.dval[i])
        self._commit(tok, reads, writes)
        return tok

    def collective(self, in_t, out_t, reads=(), writes=()):
        self._deps("pool", reads, writes)
        ins = self.nc.gpsimd.collective_compute(
            "AllGather", ALU.bypass, replica_groups=[list(range(NCORE))],
            ins=[in_t.ap().opt()], outs=[out_t.ap().opt()])
        ins.then_inc(self.csems[self.cidx])
        tok = (("c", self.cidx), 1)
        self.cidx += 1
        self._commit(tok, reads, writes)
        return tok

    def barrier(self, final=False):
        for e in self.eng:
            for e2 in self.psem:
                if e2 != e and self.cnt[e2] > 0:
                    self.wait(e, (e2, self.cnt[e2]))
            for i in range(NDS):
                if self.dval[i] > 0:
                    self.wait(e, (("d", i), self.dval[i]))
            if final:
                for i in range(self.cidx):
                    self.wait(e, (("c", i), 1))


def build_nc():
    import os
    STOP = os.environ.get("KSTOP", "")
    nc = bass.Bass("TRN2", target_bir_lowering=False)
    dt = nc.dram_tensor
    x_d = dt("x", [T, D], F32, kind="ExternalInput").ap()
    mem_d = dt("mem", [256 + 1, D], F32, kind="ExternalInput").ap()
    g_mix_d = dt("g_mix", [1, D], F32, kind="ExternalInput").ap()
    w_in_d = dt("w_in", [D + 1, D_IN], F32, kind="ExternalInput").ap()
    w_up_d = dt("w_alpha_up", [16, 256], F32, kind="ExternalInput").ap()
    b_alpha_d = dt("b_alpha", [1, 256], F32, kind="ExternalInput").ap()
    b_forget_d = dt("b_forget", [8, 1], F32, kind="ExternalInput").ap()
    g_gla_d = dt("g_gla_head", [4, 128], F32, kind="ExternalInput").ap()
    g_mem_d = dt("g_mem", [1, D], F32, kind="ExternalInput").ap()
    w_mkv_d = dt("w_mem_kv", [D + 1, 1024], F32, kind="ExternalInput").ap()
    w_o_d = [dt(n, [512 + 1, D], F32, kind="ExternalInput").ap() for n in ("w_gla_o", "w_fox_o", "w_mem_o")]
    w_out_d = dt("w_out", [D + 1, D], F32, kind="ExternalInput").ap()
    g_ffn_d = dt("g_ffn", [1, D], F32, kind="ExternalInput").ap()
    w_ff1_d = dt("w_ff1", [D + 1, 4096], F32, kind="ExternalInput").ap()
    w_ff2_d = dt("w_ff2", [4096 + 1, D], F32, kind="ExternalInput").ap()
    g_fin_d = dt("g_final", [1, D], F32, kind="ExternalInput").ap()
    cf_d = dt("cf", [128 + 1, C_TOT], F32, kind="ExternalInput").ap()
    cb_d = dt("cb", [128, B_TOT], BF16, kind="ExternalInput").ap()
    cid_d = dt("cid", [1, 2], I32, kind="ExternalInput").ap()
    cmask_d = dt("cmask", [128, 8], F32, kind="ExternalInput").ap()
    out_d = dt("out", [T, D], F32, kind="ExternalOutput").ap()

    pay_t = dt("pay", [8 * PAY_R, T], BF16)
    gat_t = dt("gat", [NCORE * 8 * PAY_R, T], BF16)
    payf_t = dt("payf", [8, T], F32)
    gatf_t = dt("gatf", [NCORE * 8, T], F32)
    gs_t = dt("gs", [128, 514], F32)
    gsg_t = dt("gsg", [NCORE * 128, 514], F32)
    pay2_t = dt("pay2", [64, S], BF16)
    gat2_t = dt("gat2", [NCORE * 64, S], BF16)
    cd_t = dt("cd", [128, 128], BF16)
    pay, gat, payf, gatf = pay_t.ap(), gat_t.ap(), payf_t.ap(), gatf_t.ap()
    gs, gsg, pay2, gat2, cd = gs_t.ap(), gsg_t.ap(), pay2_t.ap(), gat2_t.ap(), cd_t.ap()

    es = ExitStack()
    with es:
        kb = KB(nc, es)
        op, dma = kb.op, kb.dma

        def sb(name, shape, dtype, stack=es, side=None):
            return stack.enter_context(nc.sbuf_tensor("s_" + name, shape, dtype, side=side)), Buf()

        PS = []
        for i in range(7):
            PS.append((es.enter_context(nc.psum_tensor("ps%d" % i, [128, 512], F32)), Buf()))
        PST = (es.enter_context(nc.psum_tensor("pst", [128, 1024], BF16)), Buf())

        cf, cf_b = sb("cf_sb", [128, C_TOT], F32)
        cb, cb_b = sb("cb_sb", [128, B_TOT], BF16)
        uT, uT_b = sb("uT", [128, 8, T], BF16)
        es_r = ExitStack()
        omemT, omemT_b = sb("omemT", [128, 4, T], BF16, es_r, "right")
        oglaT, oglaT_b = sb("oglaT", [128, 4, T], BF16, es_r, "right")
        dma("sp", cf[:, :], cf_d[0:128, :], writes=[cf_b])
        dma("sp", cb[:, :], cb_d[:, :], writes=[cb_b])
        ident_f = cf[:, C_IDENT:C_IDENT + 128]
        ones_f = cf[:, C_ONES:C_ONES + 128]
        ident_b = cb[:, B_IDENT:B_IDENT + 128]
        ones_b = cb[:, B_ONES:B_ONES + 128]

        r_cid = nc.sync.alloc_register("r_cid")
        r_off = nc.sync.alloc_register("r_off")
        nc.sync.reg_load(r_cid, cid_d[0:1, 0:1])
        nc.sync.reg_load(r_off, cid_d[0:1, 1:2])
        v_cid = nc.sync.snap(r_cid, min_val=0, max_val=NCORE - 1)
        v_off = nc.sync.snap(r_off, min_val=0, max_val=S - T)

        wst = [sb("wst%d" % i, [128, 1024], F32) for i in range(3)]
        wsi = [0]

        def load_w(dst, src, dbuf):
            p, n = src.shape[0], src.shape[-1]
            tok = None
            for c0 in range(0, n, 1024):
                c1 = min(n, c0 + 1024)
                st, st_b = wst[wsi[0] % 3]
                wsi[0] += 1
                dma("sp", st[0:p, 0:c1 - c0], src[:, c0:c1], writes=[st_b])
                tok = op("pool", lambda e: e.tensor_copy(out=dst[:, c0:c1], in_=st[0:p, 0:c1 - c0]),
                         reads=[st_b], writes=[dbuf])
            return tok

        def rmsnorm_T(src_rows, ntiles, gvec_d, dstT, dstT_b, ph, tag):
            gm, gm_b = sb(tag + "gm", [128, 8], F32, ph)
            with nc.allow_non_contiguous_dma(reason="tiny gain vector transpose"):
                dma("sp", gm[:, :], gvec_d.rearrange("o (dc p) -> p (o dc)", p=128), writes=[gm_b])
            xt = [sb(tag + "xt%d" % i, [128, D], F32, ph) for i in range(2)]
            xn = [sb(tag + "xn%d" % i, [128, D], BF16, ph) for i in range(2)]
            junk, junk_b = sb(tag + "junk", [128, D], BF16, ph)
            ss, ss_b = sb(tag + "ss", [128, 16], F32, ph)
            rr, rr_b = sb(tag + "rr", [128, 16], F32, ph)
            for ti in range(ntiles):
                xt_t, xt_b = xt[ti % 2]
                xn_t, xn_b = xn[ti % 2]
                dma("sp", xt_t[:, :], src_rows[ti * 128:(ti + 1) * 128, :], writes=[xt_b])
                op("act", lambda e: e.activation(out=junk[:, :], in_=xt_t[:, :], func=AF.Square,
                                                 accum_out=ss[:, ti:ti + 1]),
                   reads=[xt_b], writes=[junk_b, ss_b])
                op("dve", lambda e: e.tensor_scalar(out=rr[:, ti:ti + 1], in0=ss[:, ti:ti + 1],
                                                    scalar1=1.0 / D, scalar2=EPS, op0=ALU.mult, op1=ALU.add),
                   reads=[ss_b], writes=[rr_b])
                op("act", lambda e: e.activation(out=rr[:, ti:ti + 1], in_=rr[:, ti:ti + 1], func=AF.Ln), reads=[rr_b], writes=[rr_b])
                op("act", lambda e: e.activation(out=rr[:, ti:ti + 1], in_=rr[:, ti:ti + 1], func=AF.Exp, scale=-0.5),
                   reads=[rr_b], writes=[rr_b])
                op("dve", lambda e: e.tensor_scalar_mul(out=xn_t[:, :], in0=xt_t[:, :], scalar1=rr[:, ti:ti + 1]),
                   reads=[xt_b, rr_b], writes=[xn_b])

                def tr(e):
                    ins = None
                    for dc in range(8):
                        ins = e.transpose(out=PST[0][:, dc * 128:(dc + 1) * 128],
                                          in_=xn_t[:, dc * 128:(dc + 1) * 128], identity=ident_b)
                    return ins
                op("pe", tr, reads=[xn_b, cb_b], writes=[PST[1]])
                for dc in range(8):
                    if dc % 2 == 0:
                        op("dve", lambda e: e.tensor_scalar_mul(
                            out=dstT[:, dc, ti * 128:(ti + 1) * 128], in0=PST[0][:, dc * 128:(dc + 1) * 128],
                            scalar1=gm[:, dc:dc + 1]), reads=[PST[1], gm_b], writes=[dstT_b])
                    else:
                        op("act", lambda e: e.activation(
                            out=dstT[:, dc, ti * 128:(ti + 1) * 128], in_=PST[0][:, dc * 128:(dc + 1) * 128],
                            func=AF.Copy, scale=gm[:, dc:dc + 1]), reads=[PST[1], gm_b], writes=[dstT_b])

        def proj_T(ps, W, c0, m, rhs_fn, nk=8):
            def f(e):
                ins = None
                for k in range(nk):
                    ins = e.matmul(ps, lhsT=W[:, k, c0:c0 + m], rhs=rhs_fn(k), start=(k == 0), stop=(k == nk - 1))
                return ins
            return f

        with ExitStack() as ph:
            rmsnorm_T(x_d, NT, g_mix_d, uT, uT_b, ph, "a")
        kb.barrier()
        if STOP == "T1":
            return nc

        es_w = ExitStack()
        Wg, Wg_b = sb("Wg", [128, 8, 1552], BF16, es_w)
        wup, wup_b = sb("wup", [128, 256], BF16, es_w)
        es_w2 = ExitStack()
        Wkv, Wkv_b = sb("Wkv", [128, 8, 1024], BF16, es_w2)
        Wmq, Wmq_b = sb("Wmq", [128, 8, 512], BF16, es_w2)
        with ExitStack() as ph:
            Wf, Wf_b = sb("Wf", [128, 8, 1544], BF16, ph)
            for dc in range(8):
                load_w(Wf[:, dc, 0:1536], w_in_d[dc * 128:(dc + 1) * 128, O_FQ:O_FQ + 1536], Wf_b)
                load_w(Wf[:, dc, 1536:1544], w_in_d[dc * 128:(dc + 1) * 128, O_FF:O_FF + 8], Wf_b)
            for dc in range(8):
                load_w(Wkv[:, dc, :], w_mkv_d[dc * 128:(dc + 1) * 128, :], Wkv_b)
                load_w(Wmq[:, dc, :], w_in_d[dc * 128:(dc + 1) * 128, O_MQ:O_MQ + 512], Wmq_b)
            for dc in range(8):
                load_w(Wg[:, dc, :], w_in_d[dc * 128:(dc + 1) * 128, 0:1552], Wg_b)
            op("dve", lambda e: e.memset(wup[:, :], 0.0), writes=[wup_b])
            load_w(wup[0:16, :], w_up_d[:, :], wup_b)
            nbf, nbf_b = sb("nbf", [8, 1], F32, ph)
            dma("sp", nbf[:, :], b_forget_d[:, :], writes=[nbf_b])
            op("dve", lambda e: e.tensor_scalar_mul(out=nbf[:, :], in0=nbf[:, :], scalar1=-1.0),
               reads=[nbf_b], writes=[nbf_b])
            stg = [sb("stg%d" % i, [128, 512], BF16, ph) for i in range(3)]
            lfT, lfT_b = sb("lfT", [8, T], F32, ph)
            etmp, etmp_b = sb("etmp", [8, 512], F32, ph)
            si = 0
            pi = 0
            pay3 = pay.rearrange("(h r) c -> h r c", r=PAY_R)
            payv = pay3[:, 128:193, :].rearrange("h r c -> h (r c)").rearrange("h (p k c) -> p h k c", p=128, k=16)
            stgv = [sb("stgv%d" % i, [128, 8, 65], BF16, ph) for i in range(2)]
            for sv, sv_b in stgv:
                op("dve", lambda e: e.memset(sv[:, :, :], 1.0), writes=[sv_b])
            pay_b = Buf()
            payf_b = Buf()
            for qt in range(NQ):
                ts = slice(qt * 512, (qt + 1) * 512)
                for which in range(2):
                    for hp in range(4):
                        ps, ps_b = PS[pi % 4]
                        pi += 1
                        st, st_b = stg[si % 3]
                        si += 1
                        op("pe", proj_T(ps[:, :], Wf, which * 512 + hp * 128, 128, lambda k: uT[:, k, ts]),
                           reads=[Wf_b, uT_b], writes=[ps_b])
                        op("act", lambda e: e.activation(out=st[:, :], in_=ps[:, :], func=AF.Copy,
                                                         scale=(0.125 if which == 0 else 1.0)),
                           reads=[ps_b], writes=[st_b])
                        for a in range(2):
                            h = 2 * hp + a
                            dma("sp", pay3[h, which * 64:(which + 1) * 64, ts], st[a * 64:(a + 1) * 64, :],
                                reads=[st_b], writes=[pay_b])
                for tt in range(4):
                    tk = slice(qt * 512 + tt * 128, qt * 512 + (tt + 1) * 128)
                    ps, ps_b = PS[pi % 4]
                    pi += 1
                    st, st_b = stgv[tt % 2]

                    def fv(e):
                        ins = None
                        for k in range(8):
                            ins = e.matmul(ps[:, :], lhsT=uT[:, k, tk], rhs=Wf[:, k, 1024:1536],
                                           start=(k == 0), stop=(k == 7))
                        return ins
                    op("pe", fv, reads=[Wf_b, uT_b], writes=[ps_b])
                    op("dve", lambda e: e.tensor_copy(out=st[:, :, 0:64], in_=ps[:, :].rearrange("p (h b) -> p h b", b=64)),
                       reads=[ps_b], writes=[st_b])
                    dma("sp", payv[:, :, qt * 4 + tt, :], st[:, :, :], reads=[st_b], writes=[pay_b])
                ps, ps_b = PS[pi % 4]
                pi += 1
                op("pe", proj_T(ps[0:8, :], Wf, 1536, 8, lambda k: uT[:, k, ts]), reads=[Wf_b, uT_b], writes=[ps_b])
                op("act", lambda e: e.activation(out=etmp[:, :], in_=ps[0:8, :], func=AF.Exp, bias=nbf[:, 0:1], scale=-1.0),
                   reads=[ps_b, nbf_b], writes=[etmp_b])
                op("act", lambda e: e.activation(out=etmp[:, :], in_=etmp[:, :], func=AF.Ln, bias=1.0, scale=1.0),
                   reads=[etmp_b], writes=[etmp_b])
                op("dve", lambda e: e.tensor_scalar_mul(out=lfT[:, ts], in0=etmp[:, :], scalar1=-1.0),
                   reads=[etmp_b], writes=[lfT_b])
            dma("sp", payf[:, :], lfT[:, :], reads=[lfT_b], writes=[payf_b])
            gat_b = Buf()
            gatf_b = Buf()
            kb.collective(pay_t, gat_t, reads=[pay_b], writes=[gat_b])
            kb.collective(payf_t, gatf_t, reads=[payf_b], writes=[gatf_b])
        kb.barrier()
        if STOP == "T2":
            return nc

        with ExitStack() as pm:
            mnT, mnT_b = sb("mnT", [128, 8, 256], BF16, pm)
            rmsnorm_T(mem_d, 2, g_mem_d, mnT, mnT_b, pm, "m")
            mkT, mkT_b = sb("mkT", [128, 4, 256], BF16, pm)
            mv, mv_b = sb("mv", [128, 2, 512], BF16, pm)
            mq, mq_b = sb("mq", [128, 512], BF16, pm)
            pT = [sb("pT%d" % i, [128, 512], BF16, pm) for i in range(2)]
            rd, rd_b = sb("rd", [128, 512], F32, pm)
            for h in range(4):
                ps, ps_b = PS[h % 4]
                op("pe", proj_T(ps[:, 0:256], Wkv, h * 128, 128, lambda k: mnT[:, k, :]),
                   reads=[Wkv_b, mnT_b], writes=[ps_b])
                op("act", lambda e: e.activation(out=mkT[:, h, :], in_=ps[:, 0:256], func=AF.Copy),
                   reads=[ps_b], writes=[mkT_b])
            for mt in range(2):
                ps, ps_b = PS[mt]

                def fmv(e):
                    ins = None
                    for k in range(8):
                        ins = e.matmul(ps[:, :], lhsT=mnT[:, k, mt * 128:(mt + 1) * 128], rhs=Wkv[:, k, 512:1024],
                                       start=(k == 0), stop=(k == 7))
                    return ins
                op("pe", fmv, reads=[Wkv_b, mnT_b], writes=[ps_b])
                op("act", lambda e: e.activation(out=mv[:, mt, :], in_=ps[:, :], func=AF.Copy),
                   reads=[ps_b], writes=[mv_b])
            for qt in range(NQ):
                ts = slice(qt * 512, (qt + 1) * 512)
                for h in range(4):
                    ps, ps_b = PS[0]
                    op("pe", proj_T(ps[:, :], Wmq, h * 128, 128, lambda k: uT[:, k, ts]),
                       reads=[Wmq_b, uT_b], writes=[ps_b])
                    op("dve", lambda e: e.tensor_scalar_mul(out=mq[:, :], in0=ps[:, :], scalar1=float(128 ** -0.5)),
                       reads=[ps_b], writes=[mq_b])
                    for mt in range(2):
                        pss, pss_b = PS[1 + mt]
                        p_t, p_b = pT[mt]
                        op("pe", lambda e: e.matmul(pss[:, :], lhsT=mkT[:, h, mt * 128:(mt + 1) * 128], rhs=mq[:, :],
                                                    start=True, stop=True), reads=[mkT_b, mq_b], writes=[pss_b])
                        op("act", lambda e: e.activation(out=p_t[:, :], in_=pss[:, :], func=AF.Exp),
                           reads=[pss_b], writes=[p_b])
                    pso, pso_b = PS[3]
                    psd, psd_b = PS[4]

                    def fo(e):
                        ins = None
                        for mt in range(2):
                            e.matmul(pso[:, :], lhsT=mv[:, mt, h * 128:(h + 1) * 128], rhs=pT[mt][0][:, :],
                                     start=(mt == 0), stop=(mt == 1))
                        for mt in range(2):
                            ins = e.matmul(psd[:, :], lhsT=ones_b, rhs=pT[mt][0][:, :], start=(mt == 0), stop=(mt == 1))
                        return ins
                    op("pe", fo, reads=[mv_b, pT[0][1], pT[1][1], cb_b], writes=[pso_b, psd_b])
                    op("dve", lambda e: e.reciprocal(out=rd[:, :], in_=psd[:, :]), reads=[psd_b], writes=[rd_b])
                    op("dve", lambda e: e.tensor_tensor(out=omemT[:, h, ts], in0=pso[:, :], in1=rd[:, :], op=ALU.mult),
                       reads=[pso_b, rd_b], writes=[omemT_b])
        kb.barrier()
        if STOP == "Tmem":
            return nc
        es_w2.close()

        gsg_b = Buf()
        with ExitStack() as ph:
            balp, balp_b = sb("balp", [128, 256], F32, ph)
            dma("sp", balp[:, :], b_alpha_d.partition_broadcast(128), writes=[balp_b])
            ggl, ggl_b = sb("ggl", [128, 4], F32, ph)
            with nc.allow_non_contiguous_dma(reason="tiny gain transpose"):
                dma("sp", ggl[:, :], g_gla_d.rearrange("h p -> p h"), writes=[ggl_b])
            cmask, cmask_b = sb("cmask", [128, 8], F32, ph)
            dma("sp", cmask[:, :], cmask_d[:, :], writes=[cmask_b])

            qpT, qpT_b = sb("qpT", [128, 2, T], BF16, ph)
            attn, attn_b = sb("attn", [128, NT, 512], BF16, ph)
            kdec, kdec_b = sb("kdec", [128, NT, 2, 256], BF16, ph)
            op("dve", lambda e: e.memset(kdec[:, :, :, :], 0.0), writes=[kdec_b])
            vtok, vtok_b = sb("vtok", [128, NT, 512], BF16, ph)
            decs, decs_b = sb("decs", [128, 2, 32], F32, ph)
            qT, qT_b = sb("qT", [128, 2, 512], F32, ph)
            kT, kT_b = sb("kT", [128, 2, 512], F32, ph)
            gaT, gaT_b = sb("gaT", [128, 512], BF16, ph)
            op("dve", lambda e: e.memset(gaT[:, :], 0.0), writes=[gaT_b])
            lsb, lsb_b = sb("lsb", [128, 256], F32, ph)
            Ep, Ep_b = sb("Ep", [128, 256], F32, ph)
            En, En_b = sb("En", [128, 256], F32, ph)
            edl, edl_b = sb("edl", [128, 256], F32, ph)
            qn, qn_b = sb("qn", [128, 256], BF16, ph)
            kp, kp_b = sb("kp", [128, 2, 2, 128], BF16, ph)
            kn, kn_b = sb("kn", [128, 2, 2, 128], BF16, ph)
            op("dve", lambda e: e.memset(kp[:, :, :, :], 0.0), writes=[kp_b])
            op("dve", lambda e: e.memset(kn[:, :, :, :], 0.0), writes=[kn_b])
            t1, t1_b = sb("t1", [128, 512], F32, ph)
            t2, t2_b = sb("t2", [128, 512], F32, ph)
            pi = 0
            for qt in range(NQ):
                ts = slice(qt * 512, (qt + 1) * 512)
                for fc in range(2):
                    ps, ps_b = PS[pi % 4]
                    pi += 1
                    op("pe", proj_T(ps[:, :], Wg, O_GQ + fc * 128, 128, lambda k: uT[:, k, ts]),
                       reads=[Wg_b, uT_b], writes=[ps_b])
                    op("act", lambda e: e.activation(out=qT[:, fc, :], in_=ps[:, :], func=AF.Copy, scale=0.125),
                       reads=[ps_b], writes=[qT_b])
                    ps, ps_b = PS[pi % 4]
                    pi += 1
                    op("pe", proj_T(ps[:, :], Wg, O_GK + fc * 128, 128, lambda k: uT[:, k, ts]),
                       reads=[Wg_b, uT_b], writes=[ps_b])
                    op("dve", lambda e: e.tensor_copy(out=kT[:, fc, :], in_=ps[:, :]), reads=[ps_b], writes=[kT_b])
                ps, ps_b = PS[pi % 4]
                pi += 1
                op("pe", proj_T(ps[0:16, :], Wg, O_GA, 16, lambda k: uT[:, k, ts]), reads=[Wg_b, uT_b], writes=[ps_b])
                op("dve", lambda e: e.tensor_copy(out=gaT[0:16, :], in_=ps[0:16, :]), reads=[ps_b], writes=[gaT_b])
                for tt in range(4):
                    ti = qt * 4 + tt
                    tk = slice(ti * 128, (ti + 1) * 128)
                    lt = slice(tt * 128, (tt + 1) * 128)
                    ps, ps_b = PS[pi % 4]
                    pi += 1
                    op("pe", lambda e: e.matmul(ps[:, 0:256], lhsT=gaT[:, lt], rhs=wup[:, :], start=True, stop=True),
                       reads=[gaT_b, wup_b], writes=[ps_b])
                    op("dve", lambda e: e.tensor_tensor(out=lsb[:, :], in0=ps[:, 0:256], in1=balp[:, :], op=ALU.add),
                       reads=[ps_b, balp_b], writes=[lsb_b])
                    op("act", lambda e: e.activation(out=lsb[:, :], in_=lsb[:, :], func=AF.Exp, scale=-1.0),
                       reads=[lsb_b], writes=[lsb_b])
                    op("act", lambda e: e.activation(out=lsb[:, :], in_=lsb[:, :], func=AF.Ln, bias=1.0, scale=1.0),
                       reads=[lsb_b], writes=[lsb_b])
                    ps, ps_b = PS[pi % 4]
                    pi += 1

                    def gv(e):
                        ins = None
                        for k in range(8):
                            ins = e.matmul(ps[:, :], lhsT=uT[:, k, tk], rhs=Wg[:, k, O_GV:O_GV + 512],
                                           start=(k == 0), stop=(k == 7))
                        return ins
                    op("pe", gv, reads=[Wg_b, uT_b], writes=[ps_b])
                    op("act", lambda e: e.activation(out=vtok[:, ti, :], in_=ps[:, :], func=AF.Copy),
                       reads=[ps_b], writes=[vtok_b])
                    psb, psb_b = PS[pi % 4]
                    pi += 1

                    def fb(e):
                        ins = None
                        for fc in range(2):
                            ins = e.matmul(psb[:, fc * 128:(fc + 1) * 128], lhsT=lsb[:, fc * 128:(fc + 1) * 128],
                                           rhs=cf[:, C_TRIB:C_TRIB + 128], start=True, stop=True)
                        return ins
                    op("pe", fb, reads=[lsb_b, cf_b], writes=[psb_b])
                    op("act", lambda e: e.activation(out=Ep[:, :], in_=psb[:, 0:256], func=AF.Exp),
                       reads=[psb_b], writes=[Ep_b])
                    op("act", lambda e: e.activation(out=En[:, :], in_=psb[:, 0:256], func=AF.Exp, scale=-1.0),
                       reads=[psb_b], writes=[En_b])
                    Ep3 = Ep[:, :].rearrange("p (f t) -> p f t", t=128)
                    En3 = En[:, :].rearrange("p (f t) -> p f t", t=128)
                    op("dve", lambda e: e.tensor_tensor(out=qpT[:, :, tk], in0=qT[:, :, lt], in1=Ep3, op=ALU.mult),
                       reads=[qT_b, Ep_b], writes=[qpT_b])
                    op("dve", lambda e: e.tensor_tensor(out=qn[:, :].rearrange("p (f t) -> p f t", t=128),
                                                        in0=qT[:, :, lt], in1=En3, op=ALU.mult),
                       reads=[qT_b, En_b], writes=[qn_b])
                    for a in range(2):
                        pr = slice(a * 64, (a + 1) * 64)
                        op("dve", lambda e: e.tensor_tensor(out=kp[pr, :, a, :], in0=kT[pr, :, lt], in1=Ep3[pr], op=ALU.mult),
                           reads=[kT_b, Ep_b], writes=[kp_b])
                        op("dve", lambda e: e.tensor_tensor(out=kn[pr, :, a, :], in0=kT[pr, :, lt], in1=En3[pr], op=ALU.mult),
                           reads=[kT_b, En_b], writes=[kn_b])
                    for fc in range(2):
                        op("dve", lambda e: e.tensor_copy(
                            out=decs[:, fc, 2 * ti:2 * ti + 2],
                            in_=Ep[:, fc * 128:(fc + 1) * 128].rearrange("p (j t) -> p j t", t=64)[:, :, 63]),
                           reads=[Ep_b], writes=[decs_b])
                    psd, psd_b = PS[pi % 4]
                    pi += 1
                    op("pe", lambda e: e.matmul(psd[:, 0:256], lhsT=cf[:, C_UPB:C_UPB + 128], rhs=lsb[:, :],
                                                start=True, stop=True), reads=[lsb_b, cf_b], writes=[psd_b])
                    op("act", lambda e: e.activation(out=edl[:, :], in_=psd[:, 0:256], func=AF.Exp),
                       reads=[psd_b], writes=[edl_b])
                    psk, psk_b = PS[pi % 4]
                    pi += 1

                    def gk(e):
                        ins = None
                        for k in range(8):
                            ins = e.matmul(psk[:, 0:256], lhsT=uT[:, k, tk], rhs=Wg[:, k, O_GK:O_GK + 256],
                                           start=(k == 0), stop=(k == 7))
                        return ins
                    op("pe", gk, reads=[Wg_b, uT_b], writes=[psk_b])
                    for j in range(2):
                        pr = slice(j * 64, (j + 1) * 64)
                        op("dve", lambda e: e.tensor_tensor(out=kdec[pr, ti, j, :], in0=psk[pr, 0:256], in1=edl[pr, :], op=ALU.mult),
                           reads=[psk_b, edl_b], writes=[kdec_b])
                    pac, pac_b = PS[4]
                    paa, paa_b = PS[5]

                    def fa(e):
                        ins = None
                        for h in range(4):
                            fc, a = h // 2, h % 2
                            e.matmul(pac[:, h * 128:(h + 1) * 128], lhsT=kn[:, fc, a, :],
                                     rhs=qpT[:, fc, tk], start=True, stop=True)
                            ins = e.matmul(paa[:, h * 128:(h + 1) * 128], lhsT=kp[:, fc, a, :],
                                           rhs=qn[:, fc * 128:(fc + 1) * 128], start=True, stop=True)
                        return ins
                    op("pe", fa, reads=[kn_b, kp_b, qn_b, qpT_b], writes=[pac_b, paa_b])
                    op("dve", lambda e: e.tensor_tensor(out=t1[:, :], in0=pac[:, :], in1=cf[:, C_U2:C_U2 + 512], op=ALU.mult),
                       reads=[pac_b, cf_b], writes=[t1_b])
                    op("dve", lambda e: e.tensor_tensor(out=t2[:, :], in0=paa[:, :], in1=cf[:, C_L2:C_L2 + 512], op=ALU.mult),
                       reads=[paa_b, cf_b], writes=[t2_b])
                    op("dve", lambda e: e.tensor_tensor(out=attn[:, ti, :], in0=t1[:, :], in1=t2[:, :], op=ALU.add),
                       reads=[t1_b, t2_b], writes=[attn_b])

            if STOP == "T3a":
                kb.barrier()
                return nc
            Sst, Sst_b = sb("Sst", [128, 2, 256], F32, ph)
            Pst, Pst_b = sb("Pst", [128, 2], F32, ph)
            Sbf, Sbf_b = sb("Sbf", [128, 2, 2, 128], BF16, ph)
            op("dve", lambda e: e.memset(Sbf[:, :, :, :], 0.0), writes=[Sbf_b])

            def kv_update(n):
                ti, j = n // 2, n % 2
                for fc in range(2):
                    ps, ps_b = PS[(2 * n + fc) % 4]
                    op("pe", lambda e: e.matmul(ps[:, 0:256], lhsT=kdec[:, ti, j, fc * 128:(fc + 1) * 128],
                                                rhs=vtok[:, ti, fc * 256:(fc + 1) * 256],
                                                start=True, stop=True),
                       reads=[kdec_b, vtok_b], writes=[ps_b])
                    op("dve", lambda e: e.scalar_tensor_tensor(out=Sst[:, fc, :], in0=Sst[:, fc, :],
                                                               scalar=decs[:, fc, n:n + 1], in1=ps[:, 0:256],
                                                               op0=ALU.mult, op1=ALU.add),
                       reads=[ps_b, decs_b, Sst_b], writes=[Sst_b])

            op("dve", lambda e: e.memset(Sst[:, :, :], 0.0), writes=[Sst_b])
            op("dve", lambda e: e.memset(Pst[:, :], 1.0), writes=[Pst_b])
            for n in range(32):
                kv_update(n)
                op("dve", lambda e: e.tensor_tensor(out=Pst[:, :], in0=Pst[:, :], in1=decs[:, :, n], op=ALU.mult),
                   reads=[decs_b, Pst_b], writes=[Pst_b])
            gs_b = Buf()
            dma("sp", gs[:, 0:512], Sst[:, :, :].rearrange("p f v -> p (f v)"), reads=[Sst_b], writes=[gs_b])
            dma("sp", gs[:, 512:514], Pst[:, :], reads=[Pst_b], writes=[gs_b])
            kb.collective(gs_t, gsg_t, reads=[gs_b], writes=[gsg_b])

            kb.barrier()

            if STOP == "T3b":
                kb.barrier()
                return nc
            Ain, Ain_b = sb("Ain", [128, 514], F32, ph)
            Pp, Pp_b = sb("Pp", [128, 2], F32, ph)
            op("dve", lambda e: e.memset(Sst[:, :, :], 0.0), writes=[Sst_b])
            for c2 in range(NCORE - 1):
                dma("sp", Ain[:, :], gsg[c2 * 128:(c2 + 1) * 128, :], reads=[gsg_b], writes=[Ain_b])
                op("dve", lambda e: e.tensor_scalar(out=Pp[:, :], in0=Ain[:, 512:514], scalar1=-1.0,
                                                    scalar2=cmask[:, c2:c2 + 1], op0=ALU.add, op1=ALU.mult),
                   reads=[Ain_b, cmask_b], writes=[Pp_b])
                op("dve", lambda e: e.tensor_scalar_add(out=Pp[:, :], in0=Pp[:, :], scalar1=1.0),
                   reads=[Pp_b], writes=[Pp_b])
                op("dve", lambda e: e.tensor_scalar_mul(out=Ain[:, 0:512], in0=Ain[:, 0:512], scalar1=cmask[:, c2:c2 + 1]),
                   reads=[Ain_b, cmask_b], writes=[Ain_b])
                for fc in range(2):
                    op("dve", lambda e: e.scalar_tensor_tensor(out=Sst[:, fc, :], in0=Sst[:, fc, :],
                                                               scalar=Pp[:, fc:fc + 1], in1=Ain[:, fc * 256:(fc + 1) * 256],
                                                               op0=ALU.mult, op1=ALU.add),
                       reads=[Pp_b, Ain_b, Sst_b], writes=[Sst_b])

            if STOP == "T3c":
                kb.barrier()
                return nc
            og, og_b = sb("og", [128, 512], F32, ph)
            osq, osq_b = sb("osq", [128, 512], BF16, ph)
            rstd, rstd_b = sb("rstd", [128, 512], F32, ph)
            sg, sg_b = sb("sg", [128, 512], F32, ph)
            for ti in range(NT):
                tk = slice(ti * 128, (ti + 1) * 128)
                pog, pog_b = PS[4]
                for j in range(2):
                    n = 2 * ti + j
                    for a in range(2):
                        pr = slice(a * 64, (a + 1) * 64)
                        op("act", lambda e: e.activation(out=Sbf[pr, :, a, :], in_=Sst[pr, :, a * 128:(a + 1) * 128], func=AF.Copy),
                           reads=[Sst_b], writes=[Sbf_b])

                    def fo2(e):
                        ins = None
                        for h in range(4):
                            fc, a = h // 2, h % 2
                            cs = slice(h * 128 + j * 64, h * 128 + j * 64 + 64)
                            e.matmul(pog[:, cs], lhsT=vtok[:, ti, h * 128:(h + 1) * 128],
                                     rhs=attn[:, ti, h * 128 + j * 64:h * 128 + j * 64 + 64], start=True, stop=False)
                            ins = e.matmul(pog[:, cs], lhsT=Sbf[:, fc, a, :],
                                           rhs=qpT[:, fc, ti * 128 + j * 64:ti * 128 + j * 64 + 64],
                                           start=False, stop=True)
                        return ins
                    op("pe", fo2, reads=[vtok_b, attn_b, Sbf_b, qpT_b], writes=[pog_b])
                    kv_update(n)
                op("act", lambda e: e.activation(out=og[:, :], in_=pog[:, :], func=AF.Copy), reads=[pog_b], writes=[og_b])
                op("dve", lambda e: e.tensor_tensor(out=osq[:, :], in0=og[:, :], in1=og[:, :], op=ALU.mult),
                   reads=[og_b], writes=[osq_b])
                pss, pss_b = PS[5]
                op("pe", lambda e: e.matmul(pss[:, :], lhsT=ones_b, rhs=osq[:, :], start=True, stop=True),
                   reads=[osq_b, cb_b], writes=[pss_b])
                op("dve", lambda e: e.tensor_scalar(out=rstd[:, :], in0=pss[:, :], scalar1=1.0 / 128.0, scalar2=EPS,
                                                    op0=ALU.mult, op1=ALU.add), reads=[pss_b], writes=[rstd_b])
                op("act", lambda e: e.activation(out=rstd[:, :], in_=rstd[:, :], func=AF.Ln), reads=[rstd_b], writes=[rstd_b])
                op("act", lambda e: e.activation(out=rstd[:, :], in_=rstd[:, :], func=AF.Exp, scale=-0.5),
                   reads=[rstd_b], writes=[rstd_b])
                op("dve", lambda e: e.tensor_tensor(out=og[:, :], in0=og[:, :], in1=rstd[:, :], op=ALU.mult),
                   reads=[og_b, rstd_b], writes=[og_b])
                psg, psg_b = PS[6]

                def fgg(e):
                    ins = None
                    for h in range(4):
                        for k in range(8):
                            ins = e.matmul(psg[:, h * 128:(h + 1) * 128], lhsT=Wg[:, k, O_GG + h * 128:O_GG + (h + 1) * 128],
                                           rhs=uT[:, k, tk], start=(k == 0), stop=(k == 7))
                    return ins
                op("pe", fgg, reads=[Wg_b, uT_b], writes=[psg_b])
                op("act", lambda e: e.activation(out=sg[:, :], in_=psg[:, :], func=AF.Exp, scale=-1.0),
                   reads=[psg_b], writes=[sg_b])
                op("dve", lambda e: e.tensor_scalar_add(out=sg[:, :], in0=sg[:, :], scalar1=1.0), reads=[sg_b], writes=[sg_b])
                op("dve", lambda e: e.reciprocal(out=sg[:, :], in_=sg[:, :]), reads=[sg_b], writes=[sg_b])
                op("dve", lambda e: e.tensor_tensor(out=sg[:, :], in0=psg[:, :], in1=sg[:, :], op=ALU.mult),
                   reads=[psg_b, sg_b], writes=[sg_b])
                op("dve", lambda e: e.tensor_tensor(out=og[:, :], in0=og[:, :], in1=sg[:, :], op=ALU.mult),
                   reads=[og_b, sg_b], writes=[og_b])
                for h in range(4):
                    op("dve", lambda e: e.tensor_scalar_mul(out=oglaT[:, h, tk], in0=og[:, h * 128:(h + 1) * 128],
                                                            scalar1=ggl[:, h:h + 1]),
                       reads=[og_b, ggl_b], writes=[oglaT_b])
        kb.barrier()
        if STOP == "T3":
            return nc
        es_w.close()

        pay2_b = Buf()
        gat2_b = Buf()
        with ExitStack() as ph:
            Qp, Qp_b = sb("Qp", [128, S], BF16, ph)
            Kp, Kp_b = sb("Kp", [128, S], BF16, ph)
            op("dve", lambda e: e.memset(Qp[:, :], 0.0), writes=[Qp_b])
            op("pool", lambda e: e.memset(Kp[:, :], 0.0), writes=[Kp_b])
            Va, Va_b = sb("Va", [128, 128, 65], BF16, ph)
            L, L_b = sb("L", [128, 128], F32, ph)
            gatq = gat.rearrange("(r h q) c -> h q r c", r=NCORE, h=8)[v_cid]
            dma("sp", Qp[0:64, :].rearrange("q (r c) -> q r c", r=NCORE), gatq[0:64], reads=[gat_b], writes=[Qp_b])
            dma("sp", Kp[0:64, :].rearrange("q (r c) -> q r c", r=NCORE), gatq[64:128], reads=[gat_b], writes=[Kp_b])
            gatv = gat.rearrange("(r h q) c -> h r q c", r=NCORE, h=8)[v_cid][:, 128:193, :]
            gatv = gatv.rearrange("r q c -> r (q c)").rearrange("r (p kc) -> p r kc", p=128)
            dma("sp", Va[:, :, :].rearrange("p (r k) c -> p r (k c)", r=NCORE), gatv, reads=[gat_b], writes=[Va_b])
            dma("sp", L[:, :], gatf.rearrange("(r h) (i p) -> h r i p", h=8, p=128)[v_cid], reads=[gatf_b], writes=[L_b])
            op("dve", lambda e: e.memset(Kp[64:65, :], 1.0), writes=[Kp_b])
            LT, LT_b = sb("LT", [128, 128], F32, ph)
            Xr, Xr_b = sb("Xr", [128, 128], F32, ph)
            Fk, Fk_b = sb("Fk", [128, 128], F32, ph)
            Rrep, Rrep_b = sb("Rrep", [128, 32], F32, ph)
            Cb, Cb_b = sb("Cb", [128, 128], BF16, ph)
            ps, ps_b = PS[0]
            op("pe", lambda e: e.matmul(ps[:, 0:128], lhsT=L[:, :], rhs=ident_f, start=True, stop=True),
               reads=[L_b, cf_b], writes=[ps_b])
            op("dve", lambda e: e.tensor_copy(out=LT[:, :], in_=ps[:, 0:128]), reads=[ps_b], writes=[LT_b])
            ps, ps_b = PS[1]
            op("pe", lambda e: e.matmul(ps[:, 0:128], lhsT=LT[:, :], rhs=ones_f, start=True, stop=True),
               reads=[LT_b, cf_b], writes=[ps_b])
            op("dve", lambda e: e.tensor_copy(out=Xr[:, :], in_=ps[:, 0:128]), reads=[ps_b], writes=[Xr_b])
            ps, ps_b = PS[2]

            def fF(e):
                e.matmul(ps[:, 0:128], lhsT=cf[:, C_TRI:C_TRI + 128], rhs=LT[:, :], start=True, stop=False)
                return e.matmul(ps[:, 0:128], lhsT=Xr[:, :], rhs=cf[:, C_SU:C_SU + 128], start=False, stop=True)
            op("pe", fF, reads=[LT_b, Xr_b, cf_b], writes=[ps_b])
            op("dve", lambda e: e.tensor_copy(out=Fk[:, :], in_=ps[:, 0:128]), reads=[ps_b], writes=[Fk_b])
            ps, ps_b = PS[3]
            op("pe", lambda e: e.matmul(ps[:, 0:32], lhsT=Xr[:, :], rhs=cf[:, C_SU4:C_SU4 + 32], start=True, stop=True),
               reads=[Xr_b, cf_b], writes=[ps_b])
            op("dve", lambda e: e.tensor_copy(out=Rrep[:, :], in_=ps[:, 0:32]), reads=[ps_b], writes=[Rrep_b])
            ps, ps_b = PS[0]

            def fC(e):
                e.matmul(ps[:, 0:128], lhsT=LT[:, :], rhs=cf[:, C_TRI:C_TRI + 128], start=True, stop=False)
                return e.matmul(ps[:, 0:128], lhsT=cf[:, C_G4:C_G4 + 128], rhs=Xr[:, :], start=False, stop=True)
            op("pe", fC, reads=[LT_b, Xr_b, cf_b], writes=[ps_b])
            op("dve", lambda e: e.tensor_copy(out=Cb[:, :], in_=ps[:, 0:128]), reads=[ps_b], writes=[Cb_b])
            cd_b = Buf()
            dma("sp", cd[:, :], Cb[:, :], reads=[Cb_b], writes=[cd_b])
            dma("sp", Qp[64:65, :], cd.rearrange("(o k) p -> o (k p)", o=1), reads=[cd_b], writes=[Qp_b])

            biasq = [sb("biasq%d" % i, [128, 128], F32, ph) for i in range(2)]
            PT = [sb("PT%d" % i, [128, 512], BF16, ph) for i in range(4)]
            rec, rec_b = sb("rec", [128, 512], F32, ph)
            op("dve", lambda e: e.memset(rec[:, :], 0.0), writes=[rec_b])
            bcs, bcs_b = sb("bcs", [64, 512], F32, ph)
            ost = [sb("ost%d" % i, [64, 512], BF16, ph) for i in range(2)]
            steps = [(qi, kt) for qi in range(S // 512) for kt in range(4 * qi + 4)]
            SB = [PS[0], PS[1], PS[2], PS[6]]
            LOOK = 3

            def emit_s(sidx):
                qi, kt = steps[sidx]
                nkt = 4 * qi + 4
                qs = slice(qi * 512, (qi + 1) * 512)
                bq, bq_b = biasq[qi % 2]
                if kt == 0:
                    op("dve", lambda e: e.tensor_scalar(out=bq[:, 0:nkt], in0=Fk[:, 0:nkt], scalar1=Rrep[:, qi:qi + 1],
                                                        scalar2=-1.0, op0=ALU.subtract, op1=ALU.mult),
                       reads=[Fk_b, Rrep_b], writes=[bq_b])
                pss, pss_b = SB[sidx % 4]
                j = kt - 4 * qi

                def fs(e):
                    ins = e.matmul(pss[:, :], lhsT=Kp[:, kt * 128:(kt + 1) * 128], rhs=Qp[:, qs],
                                   start=True, stop=(j < 0))
                    if j >= 0:
                        ins = e.matmul(pss[:, :], lhsT=ident_b, rhs=cb[:, B_MASK + j * 512:B_MASK + (j + 1) * 512],
                                       start=False, stop=True)
                    return ins
                op("pe", fs, reads=[Kp_b, Qp_b, cb_b], writes=[pss_b])

            def emit_rest(sidx):
                qi, kt = steps[sidx]
                nkt = 4 * qi + 4
                qs = slice(qi * 512, (qi + 1) * 512)
                bq, bq_b = biasq[qi % 2]
                pss, pss_b = SB[sidx % 4]
                p_t, p_b = PT[sidx % 4]
                pso, pso_b = PS[3 + (qi % 2)]
                op("act", lambda e: e.activation(out=p_t[:, :], in_=pss[:, :], func=AF.Exp, bias=bq[:, kt:kt + 1], scale=1.0),
                   reads=[pss_b, bq_b], writes=[p_b])
                op("pe", lambda e: e.matmul(pso[0:65, :], lhsT=Va[:, kt, 0:65], rhs=p_t[:, :],
                                            start=(kt == 0), stop=(kt == nkt - 1)),
                   reads=[Va_b, p_b], writes=([pso_b] if kt in (0, nkt - 1) else []))
                if kt != nkt - 1:
                    return
                op("dve", lambda e: e.reciprocal(out=rec[64:65, :], in_=pso[64:65, :]), reads=[pso_b], writes=[rec_b])
                pbc, pbc_b = PS[5]
                op("pe", lambda e: e.matmul(pbc[0:64, :], lhsT=cf[:, C_SEL:C_SEL + 64], rhs=rec[:, :],
                                            start=True, stop=True), reads=[rec_b, cf_b], writes=[pbc_b])
                op("act", lambda e: e.activation(out=bcs[:, :], in_=pbc[0:64, :], func=AF.Copy), reads=[pbc_b], writes=[bcs_b])
                o_t, o_b = ost[qi % 2]
                op("dve", lambda e: e.tensor_tensor(out=o_t[:, :], in0=pso[0:64, :], in1=bcs[:, :], op=ALU.mult),
                   reads=[pso_b, bcs_b], writes=[o_b])
                dma("sp", pay2[:, qs], o_t[:, :], reads=[o_b], writes=[pay2_b])

            for sidx in range(LOOK):
                emit_s(sidx)
            for sidx in range(len(steps)):
                if sidx + LOOK < len(steps):
                    emit_s(sidx + LOOK)
                emit_rest(sidx)
            kb.collective(pay2_t, gat2_t, reads=[pay2_b], writes=[gat2_b])
        kb.barrier()
        if STOP == "FOX":
            return nc

        es_m = ExitStack()
        mT, mT_b = sb("mT", [128, 8, T], BF16, es_m)
        with ExitStack() as ph:
            ofoxT, ofoxT_b = sb("ofoxT", [128, 4, T], BF16, ph)
            gat2v = gat2.rearrange("(kc a d) s -> a d kc s", a=2, d=64)
            for a in range(2):
                dma("sp", ofoxT[a * 64:(a + 1) * 64, :, :], gat2v[a][:, :, bass.ds(v_off, T)],
                    reads=[gat2_b], writes=[ofoxT_b])
            oT = [(oglaT, oglaT_b), (ofoxT, ofoxT_b), (omemT, omemT_b)]
            Wgt = [sb("Wgt%d" % i, [128, 8, 1024], BF16, ph) for i in range(2)]
            Wo = [sb("Wo%d" % i, [128, 4, 1024], BF16, ph) for i in range(2)]
            sgm, sgm_b = sb("sgm", [128, 512], BF16, ph)
            tmp, tmp_b = sb("tmpm", [128, 512], BF16, ph)
            pi = 0
            for b in range(3):
                wg_t, wg_b = Wgt[b % 2]
                wo_t, wo_b = Wo[b % 2]
                for dc in range(8):
                    load_w(wg_t[:, dc, :], w_in_d[dc * 128:(dc + 1) * 128, O_GATES + b * 1024:O_GATES + (b + 1) * 1024], wg_b)
                for kc in range(4):
                    load_w(wo_t[:, kc, :], w_o_d[b][kc * 128:(kc + 1) * 128, :], wo_b)
                o_t, o_b = oT[b]
                for qt in range(NQ):
                    ts = slice(qt * 512, (qt + 1) * 512)
                    for fc in range(8):
                        psg, psg_b = PS[pi % 3]
                        psy, psy_b = PS[3 + pi % 3]
                        pi += 1
                        op("pe", proj_T(psg[:, :], wg_t, fc * 128, 128, lambda k: uT[:, k, ts]), reads=[wg_b, uT_b], writes=[psg_b])
                        op("pe", proj_T(psy[:, :], wo_t, fc * 128, 128, lambda k: o_t[:, k, ts], nk=4), reads=[wo_b, o_b], writes=[psy_b])
                        op("act", lambda e: e.activation(out=sgm[:, :], in_=psg[:, :], func=AF.Sigmoid), reads=[psg_b], writes=[sgm_b])
                        if b == 0:
                            op("dve", lambda e: e.tensor_tensor(out=mT[:, fc, ts], in0=psy[:, :], in1=sgm[:, :], op=ALU.mult),
                               reads=[psy_b, sgm_b], writes=[mT_b])
                        else:
                            op("dve", lambda e: e.tensor_tensor(out=tmp[:, :], in0=psy[:, :], in1=sgm[:, :], op=ALU.mult),
                               reads=[psy_b, sgm_b], writes=[tmp_b])
                            op("dve", lambda e: e.tensor_tensor(out=mT[:, fc, ts], in0=mT[:, fc, ts], in1=tmp[:, :], op=ALU.add),
                               reads=[tmp_b, mT_b], writes=[mT_b])
        kb.barrier()
        if STOP == "T5a":
            return nc
        es_r.close()

        es_h = ExitStack()
        hres, hres_b = sb("hres", [128, NT, D], F32, es_h, "right")
        with ExitStack() as ph:
            Wout, Wout_b = sb("Wout", [128, 8, 1024], BF16, ph)
            for kc in range(8):
                load_w(Wout[:, kc, :], w_out_d[kc * 128:(kc + 1) * 128, :], Wout_b)
            gm2, gm2_b = sb("gm2", [128, 8], F32, ph)
            with nc.allow_non_contiguous_dma(reason="tiny gain vector transpose"):
                dma("sp", gm2[:, :], g_ffn_d.rearrange("o (dc p) -> p (o dc)", p=128), writes=[gm2_b])
            xt = [sb("bxt%d" % i, [128, D], F32, ph) for i in range(2)]
            xn = [sb("bxn%d" % i, [128, D], BF16, ph) for i in range(2)]
            junk, junk_b = sb("bjunk", [128, D], BF16, ph)
            ss, ss_b = sb("bss", [128, 16], F32, ph)
            rr, rr_b = sb("brr", [128, 16], F32, ph)
            for ti in range(NT):
                tk = slice(ti * 128, (ti + 1) * 128)
                xt_t, xt_b = xt[ti % 2]
                xn_t, xn_b = xn[ti % 2]
                dma("sp", xt_t[:, :], x_d[tk, :], writes=[xt_b])
                for ch in range(2):
                    ps, ps_b = PS[(2 * ti + ch) % 4]

                    def fh(e):
                        ins = None
                        for k in range(8):
                            ins = e.matmul(ps[:, :], lhsT=mT[:, k, tk], rhs=Wout[:, k, ch * 512:(ch + 1) * 512],
                                           start=(k == 0), stop=(k == 7))
                        return ins
                    op("pe", fh, reads=[mT_b, Wout_b], writes=[ps_b])
                    op("dve", lambda e: e.tensor_tensor(out=hres[:, ti, ch * 512:(ch + 1) * 512], in0=ps[:, :],
                                                        in1=xt_t[:, ch * 512:(ch + 1) * 512], op=ALU.add),
                       reads=[ps_b, xt_b], writes=[hres_b])
                op("act", lambda e: e.activation(out=junk[:, :], in_=hres[:, ti, :], func=AF.Square, accum_out=ss[:, ti:ti + 1]),
                   reads=[hres_b], writes=[junk_b, ss_b])
                op("dve", lambda e: e.tensor_scalar(out=rr[:, ti:ti + 1], in0=ss[:, ti:ti + 1], scalar1=1.0 / D, scalar2=EPS,
                                                    op0=ALU.mult, op1=ALU.add), reads=[ss_b], writes=[rr_b])
                op("act", lambda e: e.activation(out=rr[:, ti:ti + 1], in_=rr[:, ti:ti + 1], func=AF.Ln), reads=[rr_b], writes=[rr_b])
                op("act", lambda e: e.activation(out=rr[:, ti:ti + 1], in_=rr[:, ti:ti + 1], func=AF.Exp, scale=-0.5),
                   reads=[rr_b], writes=[rr_b])
                op("dve", lambda e: e.tensor_scalar_mul(out=xn_t[:, :], in0=hres[:, ti, :], scalar1=rr[:, ti:ti + 1]),
                   reads=[hres_b, rr_b], writes=[xn_b])

                def tr2(e):
                    ins = None
                    for dc in range(8):
                        ins = e.transpose(out=PST[0][:, dc * 128:(dc + 1) * 128], in_=xn_t[:, dc * 128:(dc + 1) * 128],
                                          identity=ident_b)
                    return ins
                op("pe", tr2, reads=[xn_b, cb_b], writes=[PST[1]])
                for dc in range(8):
                    if dc % 2 == 0:
                        op("dve", lambda e: e.tensor_scalar_mul(out=uT[:, dc, tk], in0=PST[0][:, dc * 128:(dc + 1) * 128],
                                                                scalar1=gm2[:, dc:dc + 1]),
                           reads=[PST[1], gm2_b], writes=[uT_b])
                    else:
                        op("act", lambda e: e.activation(out=uT[:, dc, tk], in_=PST[0][:, dc * 128:(dc + 1) * 128],
                                                         func=AF.Copy, scale=gm2[:, dc:dc + 1]),
                           reads=[PST[1], gm2_b], writes=[uT_b])
        kb.barrier()
        if STOP == "T5b":
            return nc
        es_m.close()

        with ExitStack() as ph:
            W1 = [sb("W1_%d" % i, [128, 8, 1024], BF16, ph) for i in range(2)]
            W2 = [sb("W2_%d" % i, [128, 8, 1024], BF16, ph) for i in range(2)]
            aT = [sb("aT%d" % i, [128, 8, 512], BF16, ph) for i in range(2)]
            pi = 0
            ai = 0
            for qf in range(4):
                w1_t, w1_b = W1[qf % 2]
                w2_t, w2_b = W2[qf % 2]
                for dc in range(8):
                    load_w(w1_t[:, dc, :], w_ff1_d[dc * 128:(dc + 1) * 128, qf * 1024:(qf + 1) * 1024], w1_b)
                for fc in range(8):
                    load_w(w2_t[:, fc, :], w_ff2_d[qf * 1024 + fc * 128:qf * 1024 + (fc + 1) * 128, :], w2_b)
                for qt in range(NQ):
                    ts = slice(qt * 512, (qt + 1) * 512)
                    a_t, a_b = aT[ai % 2]
                    ai += 1
                    for fc in range(8):
                        ps, ps_b = PS[pi % 6]
                        pi += 1
                        op("pe", proj_T(ps[:, :], w1_t, fc * 128, 128, lambda k: uT[:, k, ts]), reads=[w1_b, uT_b], writes=[ps_b])
                        op("act", lambda e: e.activation(out=a_t[:, fc, :], in_=ps[:, :], func=AF.Relu), reads=[ps_b], writes=[a_b])
                        op("dve",
                           lambda e: e.tensor_tensor(out=a_t[:, fc, :], in0=a_t[:, fc, :], in1=a_t[:, fc, :], op=ALU.mult),
                           reads=[a_b], writes=[a_b])
                    for tt in range(4):
                        ti = qt * 4 + tt
                        for ch in range(2):
                            ps, ps_b = PS[pi % 6]
                            pi += 1

                            def f2(e):
                                ins = None
                                for fc in range(8):
                                    ins = e.matmul(ps[:, :], lhsT=a_t[:, fc, tt * 128:(tt + 1) * 128],
                                                   rhs=w2_t[:, fc, ch * 512:(ch + 1) * 512], start=(fc == 0), stop=(fc == 7))
                                return ins
                            op("pe", f2, reads=[a_b, w2_b], writes=[ps_b])
                            op("dve", lambda e: e.tensor_tensor(out=hres[:, ti, ch * 512:(ch + 1) * 512],
                                                                in0=hres[:, ti, ch * 512:(ch + 1) * 512], in1=ps[:, :], op=ALU.add),
                               reads=[ps_b, hres_b], writes=[hres_b])
        kb.barrier()
        if STOP == "T5c":
            return nc

        with ExitStack() as ph:
            gfb, gfb_b = sb("gfb", [128, D], F32, ph)
            dma("sp", gfb[:, :], g_fin_d.partition_broadcast(128), writes=[gfb_b])
            junk, junk_b = sb("djunk", [128, D], BF16, ph)
            ss, ss_b = sb("dss", [128, 16], F32, ph)
            rr, rr_b = sb("drr", [128, 16], F32, ph)
            yo = [sb("yo%d" % i, [128, D], F32, ph) for i in range(2)]
            out_b = Buf()
            for ti in range(NT):
                y_t, y_b = yo[ti % 2]
                op("act", lambda e: e.activation(out=junk[:, :], in_=hres[:, ti, :], func=AF.Square, accum_out=ss[:, ti:ti + 1]),
                   reads=[hres_b], writes=[junk_b, ss_b])
                op("dve", lambda e: e.tensor_scalar(out=rr[:, ti:ti + 1], in0=ss[:, ti:ti + 1], scalar1=1.0 / D, scalar2=EPS,
                                                    op0=ALU.mult, op1=ALU.add), reads=[ss_b], writes=[rr_b])
                op("act", lambda e: e.activation(out=rr[:, ti:ti + 1], in_=rr[:, ti:ti + 1], func=AF.Ln), reads=[rr_b], writes=[rr_b])
                op("act", lambda e: e.activation(out=rr[:, ti:ti + 1], in_=rr[:, ti:ti + 1], func=AF.Exp, scale=-0.5),
                   reads=[rr_b], writes=[rr_b])
                op("dve", lambda e: e.scalar_tensor_tensor(out=y_t[:, :], in0=hres[:, ti, :], scalar=rr[:, ti:ti + 1],
                                                           in1=gfb[:, :], op0=ALU.mult, op1=ALU.mult),
                   reads=[hres_b, rr_b, gfb_b], writes=[y_b])
                dma("sp", out_d[ti * 128:(ti + 1) * 128, :], y_t[:, :], reads=[y_b], writes=[out_b])
        kb.barrier(final=True)
        es_h.close()
    return nc


_CACHE = {}


def kernel(x, mem, g_mix, w_in, w_alpha_up, b_alpha, b_forget, g_gla_head, g_mem, w_mem_kv,
           w_gla_o, w_fox_o, w_mem_o, w_out, g_ffn, w_ff1, w_ff2, g_final):
    f = lambda a: np.ascontiguousarray(np.asarray(a, dtype=np.float32))
    if "nc" not in _CACHE:
        _CACHE["nc"] = build_nc()
        _CACHE["c"] = _consts()
    nc = _CACHE["nc"]
    cf, cb = _CACHE["c"]
    xs = f(x).reshape(S, D)
    shared = {
        "mem": f(mem).reshape(256, D), "g_mix": f(g_mix).reshape(1, D), "w_in": f(w_in).reshape(D, D_IN),
        "w_alpha_up": f(w_alpha_up).reshape(16, 256), "b_alpha": f(b_alpha).reshape(1, 256),
        "b_forget": f(b_forget).reshape(8, 1), "g_gla_head": f(g_gla_head).reshape(4, 128),
        "g_mem": f(g_mem).reshape(1, D), "w_mem_kv": f(w_mem_kv).reshape(D, 1024),
        "w_gla_o": f(w_gla_o).reshape(512, D), "w_fox_o": f(w_fox_o).reshape(512, D),
        "w_mem_o": f(w_mem_o).reshape(512, D), "w_out": f(w_out).reshape(D, D),
        "g_ffn": f(g_ffn).reshape(1, D), "w_ff1": f(w_ff1).reshape(D, 4096), "w_ff2": f(w_ff2).reshape(4096, D),
        "g_final": f(g_final).reshape(1, D), "cf": cf, "cb": cb,
    }
    big = ("mem", "w_in", "w_mem_kv", "w_gla_o", "w_fox_o", "w_mem_o", "w_out", "w_ff1", "w_ff2", "cf")
    in_maps = []
    for c in range(NCORE):
        m = dict(shared)
        for k in big:
            a = shared[k]
            m[k] = np.concatenate([a, np.full((1, a.shape[1]), float(c), a.dtype)], axis=0)
        m["x"] = xs[c * T:(c + 1) * T]
        m["cid"] = np.array([[c, c * T]], np.int32)
        m["cmask"] = np.tile((np.arange(8) < c).astype(np.float32)[None, :], (128, 1))
        in_maps.append(m)
    res = run_bass_kernel_spmd(nc, in_maps, core_ids=list(range(NCORE)))
    out = np.concatenate([np.asarray(r["out"], dtype=np.float32) for r in res.results], axis=0)
    return out.reshape(1, S, D)
```

```python
import numpy as np
import ml_dtypes
from contextlib import ExitStack
import concourse.bass as bass
import concourse.mybir as mybir
from concourse.bass_utils import run_bass_kernel_spmd

F32 = mybir.dt.float32
BF16 = mybir.dt.bfloat16
I32 = mybir.dt.int32
AF = mybir.ActivationFunctionType
ALU = mybir.AluOpType

NCORE = 8
D = 1024
S = 16384
T = S // NCORE
NT = T // 128
NQ = T // 512
EPS = 1e-6
D_IN = 6680
O_GQ, O_GK, O_GV, O_GG, O_GA = 0, 256, 512, 1024, 1536
O_FQ, O_FK, O_FV, O_FF, O_MQ, O_GATES = 1552, 2064, 2576, 3088, 3096, 3608
PAY_R = 193
NDS = 40

C_IDENT, C_TRI, C_SU, C_G4, C_TRIB, C_UPB, C_ONES, C_U2, C_L2, C_SU4 = (
    0, 128, 256, 384, 512, 640, 768, 896, 1408, 1920)
C_SEL = 1952
C_TOT = 2016
B_IDENT, B_ONES, B_MASK = 0, 128, 256
B_TOT = 256 + 4 * 512


def _consts():
    p = np.arange(128)
    cf = np.zeros((128, C_TOT), np.float32)
    cf[:, C_IDENT:C_IDENT + 128] = np.eye(128)
    cf[:, C_TRI:C_TRI + 128] = (p[:, None] <= p[None, :])
    cf[:, C_SU:C_SU + 128] = (p[:, None] < p[None, :])
    cf[:, C_G4:C_G4 + 128] = (p[:, None] < p[None, :]) & ((p[:, None] // 4) == (p[None, :] // 4))
    same = (p[:, None] // 64) == (p[None, :] // 64)
    cf[:, C_TRIB:C_TRIB + 128] = np.where(same & (p[:, None] <= p[None, :]), -1.0 / 16.0, 0.0)
    cf[:, C_UPB:C_UPB + 128] = np.where(same & (p[:, None] > p[None, :]), -1.0 / 16.0, 0.0)
    cf[:, C_ONES:C_ONES + 128] = 1.0
    u2 = (same & (p[None, :] >= p[:, None])).astype(np.float32)
    l2 = (same & (p[None, :] < p[:, None])).astype(np.float32)
    cf[:, C_U2:C_U2 + 512] = np.tile(u2, (1, 4))
    cf[:, C_L2:C_L2 + 512] = np.tile(l2, (1, 4))
    cf[:, C_SU4:C_SU4 + 32] = (p[:, None] < 4 * np.arange(32)[None, :])
    cf[64, C_SEL:C_SEL + 64] = 1.0
    cb = np.zeros((128, B_TOT), np.float32)
    cb[:, B_IDENT:B_IDENT + 128] = np.eye(128)
    cb[:, B_ONES:B_ONES + 128] = 1.0
    cq = np.arange(512)
    for j in range(4):
        cb[:, B_MASK + j * 512:B_MASK + (j + 1) * 512] = np.where(
            (128 * j + p[:, None]) > cq[None, :], -30000.0, 0.0)
    return cf, cb.astype(ml_dtypes.bfloat16)


class Buf:
    __slots__ = ("w", "r")

    def __init__(self):
        self.w = None
        self.r = {}


class _FirstIns:
    def __init__(self, eng, first):
        self._eng = eng
        self._first = first

    def __getattr__(self, name):
        f = getattr(self._eng, name)

        def call(*a, **k):
            r = f(*a, **k)
            if not self._first:
                self._first.append(r)
            return r
        return call


class KB:
    def __init__(self, nc, es):
        self.nc = nc
        self.eng = {"pe": nc.tensor, "act": nc.scalar, "dve": nc.vector, "pool": nc.gpsimd,
                    "sp": nc.sync}
        self.psem = {e: es.enter_context(nc.semaphore("p_" + e)) for e in ("pe", "act", "dve", "pool")}
        self.cnt = {e: 0 for e in self.psem}
        self.dsem = [es.enter_context(nc.semaphore("d%d" % i)) for i in range(NDS)]
        self.dval = [0] * NDS
        self.dnext = {"sp": 0, "pool": 0}
        self.drange = {"sp": (0, NDS - 8), "pool": (NDS - 8, NDS)}
        self.csems = [es.enter_context(nc.semaphore("cc%d" % i)) for i in range(4)]
        self.cidx = 0
        self.seen = {e: {} for e in self.eng}

    def _sem(self, key):
        if isinstance(key, tuple) and key[0] == "c":
            return self.csems[key[1]]
        return self.psem[key] if isinstance(key, str) else self.dsem[key[1]]

    def wait(self, e, tok):
        if tok is None:
            return
        key, val = tok
        if self.seen[e].get(key, 0) >= val:
            return
        self.eng[e].wait_ge(self._sem(key), val)
        self.seen[e][key] = val

    def _deps(self, e, reads, writes):
        for b in reads:
            self.wait(e, b.w)
        for b in writes:
            self.wait(e, b.w)
            for k, v in b.r.items():
                self.wait(e, (k, v))

    def _commit(self, tok, reads, writes):
        for b in writes:
            b.w = tok
            b.r = {}
        for b in reads:
            if b.r.get(tok[0], 0) < tok[1]:
                b.r[tok[0]] = tok[1]

    def _need(self, e, reads, writes):
        need = {}

        def add(tok):
            if tok is None:
                return
            key, val = tok
            if self.seen[e].get(key, 0) >= val or need.get(key, 0) >= val:
                return
            need[key] = val
        for b in reads:
            add(b.w)
        for b in writes:
            add(b.w)
            for k, v in b.r.items():
                add((k, v))
        return list(need.items())

    def op(self, e, fn, reads=(), writes=()):
        need = self._need(e, reads, writes)
        for tok in need[:-1]:
            self.wait(e, tok)
        first = []
        ins = fn(_FirstIns(self.eng[e], first))
        if need:
            key, val = need[-1]
            first[0]._wait_ge(self._sem(key), val)
            self.seen[e][key] = val
        self.cnt[e] += 1
        ins.then_inc(self.psem[e], 1)
        tok = (e, self.cnt[e])
        self._commit(tok, reads, writes)
        return tok

    def dma(self, q, out, in_, reads=(), writes=(), **kw):
        self._deps(q, reads, writes)
        lo, hi = self.drange[q]
        i = lo + self.dnext[q]
        self.dnext[q] = (self.dnext[q] + 1) % (hi - lo)
        if self.dval[i] > 0:
            self.wait(q, (("d", i), self.dval[i]))
        ins = self.eng[q].dma_start(out=out, in_=in_, **kw)
        self.dval[i] += 16
        ins.then_inc(self.dsem[i], 16)
        tok = (("d", i), self.dval[i])
        self._commit(tok, reads, writes)
        return tok

    def collective(self, in_t, out_t, reads=(), writes=()):
        self._deps("pool", reads, writes)
        ins = self.nc.gpsimd.collective_compute(
            "AllGather", ALU.bypass, replica_groups=[list(range(NCORE))],
            ins=[in_t.ap().opt()], outs=[out_t.ap().opt()])
        ins.then_inc(self.csems[self.cidx])
        tok = (("c", self.cidx), 1)
        self.cidx += 1
        self._commit(tok, reads, writes)
        return tok

    def barrier(self, final=False):
        for e in self.eng:
            for e2 in self.psem:
                if e2 != e and self.cnt[e2] > 0:
                    self.wait(e, (e2, self.cnt[e2]))
            for i in range(NDS):
                if self.dval[i] > 0:
                    self.wait(e, (("d", i), self.dval[i]))
            if final:
                for i in range(self.cidx):
                    self.wait(e, (("c", i), 1))


def build_nc():
    import os
    STOP = os.environ.get("KSTOP", "")
    nc = bass.Bass("TRN2", target_bir_lowering=False)
    dt = nc.dram_tensor
    x_d = dt("x", [T, D], F32, kind="ExternalInput").ap()
    mem_d = dt("mem", [256 + 1, D], F32, kind="ExternalInput").ap()
    g_mix_d = dt("g_mix", [1, D], F32, kind="ExternalInput").ap()
    w_in_d = dt("w_in", [D + 1, D_IN], F32, kind="ExternalInput").ap()
    w_up_d = dt("w_alpha_up", [16, 256], F32, kind="ExternalInput").ap()
    b_alpha_d = dt("b_alpha", [1, 256], F32, kind="ExternalInput").ap()
    b_forget_d = dt("b_forget", [8, 1], F32, kind="ExternalInput").ap()
    g_gla_d = dt("g_gla_head", [4, 128], F32, kind="ExternalInput").ap()
    g_mem_d = dt("g_mem", [1, D], F32, kind="ExternalInput").ap()
    w_mkv_d = dt("w_mem_kv", [D + 1, 1024], F32, kind="ExternalInput").ap()
    w_o_d = [dt(n, [512 + 1, D], F32, kind="ExternalInput").ap() for n in ("w_gla_o", "w_fox_o", "w_mem_o")]
    w_out_d = dt("w_out", [D + 1, D], F32, kind="ExternalInput").ap()
    g_ffn_d = dt("g_ffn", [1, D], F32, kind="ExternalInput").ap()
    w_ff1_d = dt("w_ff1", [D + 1, 4096], F32, kind="ExternalInput").ap()
    w_ff2_d = dt("w_ff2", [4096 + 1, D], F32, kind="ExternalInput").ap()
    g_fin_d = dt("g_final", [1, D], F32, kind="ExternalInput").ap()
    cf_d = dt("cf", [128 + 1, C_TOT], F32, kind="ExternalInput").ap()
    cb_d = dt("cb", [128, B_TOT], BF16, kind="ExternalInput").ap()
    cid_d = dt("cid", [1, 2], I32, kind="ExternalInput").ap()
    cmask_d = dt("cmask", [128, 8], F32, kind="ExternalInput").ap()
    out_d = dt("out", [T, D], F32, kind="ExternalOutput").ap()

    pay_t = dt("pay", [8 * PAY_R, T], BF16)
    gat_t = dt("gat", [NCORE * 8 * PAY_R, T], BF16)
    payf_t = dt("payf", [8, T], F32)
    gatf_t = dt("gatf", [NCORE * 8, T], F32)
    gs_t = dt("gs", [128, 514], F32)
    gsg_t = dt("gsg", [NCORE * 128, 514], F32)
    pay2_t = dt("pay2", [64, S], BF16)
    gat2_t = dt("gat2", [NCORE * 64, S], BF16)
    cd_t = dt("cd", [128, 128], BF16)
    pay, gat, payf, gatf = pay_t.ap(), gat_t.ap(), payf_t.ap(), gatf_t.ap()
    gs, gsg, pay2, gat2, cd = gs_t.ap(), gsg_t.ap(), pay2_t.ap(), gat2_t.ap(), cd_t.ap()

    es = ExitStack()
    with es:
        kb = KB(nc, es)
        op, dma = kb.op, kb.dma

        def sb(name, shape, dtype, stack=es, side=None):
            return stack.enter_context(nc.sbuf_tensor("s_" + name, shape, dtype, side=side)), Buf()

        PS = []
        for i in range(7):
            PS.append((es.enter_context(nc.psum_tensor("ps%d" % i, [128, 512], F32)), Buf()))
        PST = (es.enter_context(nc.psum_tensor("pst", [128, 1024], BF16)), Buf())

        cf, cf_b = sb("cf_sb", [128, C_TOT], F32)
        cb, cb_b = sb("cb_sb", [128, B_TOT], BF16)
        uT, uT_b = sb("uT", [128, 8, T], BF16)
        es_r = ExitStack()
        omemT, omemT_b = sb("omemT", [128, 4, T], BF16, es_r, "right")
        oglaT, oglaT_b = sb("oglaT", [128, 4, T], BF16, es_r, "right")
        dma("sp", cf[:, :], cf_d[0:128, :], writes=[cf_b])
        dma("sp", cb[:, :], cb_d[:, :], writes=[cb_b])
        ident_f = cf[:, C_IDENT:C_IDENT + 128]
        ones_f = cf[:, C_ONES:C_ONES + 128]
        ident_b = cb[:, B_IDENT:B_IDENT + 128]
        ones_b = cb[:, B_ONES:B_ONES + 128]

        r_cid = nc.sync.alloc_register("r_cid")
        r_off = nc.sync.alloc_register("r_off")
        nc.sync.reg_load(r_cid, cid_d[0:1, 0:1])
        nc.sync.reg_load(r_off, cid_d[0:1, 1:2])
        v_cid = nc.sync.snap(r_cid, min_val=0, max_val=NCORE - 1)
        v_off = nc.sync.snap(r_off, min_val=0, max_val=S - T)

        def load_w(dst, src, dbuf):
            return dma("pool", dst, src, writes=[dbuf])

        def rmsnorm_T(src_rows, ntiles, gvec_d, dstT, dstT_b, ph, tag):
            gm, gm_b = sb(tag + "gm", [128, 8], F32, ph)
            with nc.allow_non_contiguous_dma(reason="tiny gain vector transpose"):
                dma("sp", gm[:, :], gvec_d.rearrange("o (dc p) -> p (o dc)", p=128), writes=[gm_b])
            xt = [sb(tag + "xt%d" % i, [128, D], F32, ph) for i in range(2)]
            xn = [sb(tag + "xn%d" % i, [128, D], BF16, ph) for i in range(2)]
            junk, junk_b = sb(tag + "junk", [128, D], BF16, ph)
            ss, ss_b = sb(tag + "ss", [128, 16], F32, ph)
            rr, rr_b = sb(tag + "rr", [128, 16], F32, ph)
            for ti in range(ntiles):
                xt_t, xt_b = xt[ti % 2]
                xn_t, xn_b = xn[ti % 2]
                dma("sp", xt_t[:, :], src_rows[ti * 128:(ti + 1) * 128, :], writes=[xt_b])
                op("act", lambda e: e.activation(out=junk[:, :], in_=xt_t[:, :], func=AF.Square,
                                                 accum_out=ss[:, ti:ti + 1]),
                   reads=[xt_b], writes=[junk_b, ss_b])
                op("dve", lambda e: e.tensor_scalar(out=rr[:, ti:ti + 1], in0=ss[:, ti:ti + 1],
                                                    scalar1=1.0 / D, scalar2=EPS, op0=ALU.mult, op1=ALU.add),
                   reads=[ss_b], writes=[rr_b])
                op("act", lambda e: e.activation(out=rr[:, ti:ti + 1], in_=rr[:, ti:ti + 1], func=AF.Ln), reads=[rr_b], writes=[rr_b])
                op("act", lambda e: e.activation(out=rr[:, ti:ti + 1], in_=rr[:, ti:ti + 1], func=AF.Exp, scale=-0.5),
                   reads=[rr_b], writes=[rr_b])
                op("dve", lambda e: e.tensor_scalar_mul(out=xn_t[:, :], in0=xt_t[:, :], scalar1=rr[:, ti:ti + 1]),
                   reads=[xt_b, rr_b], writes=[xn_b])

                def tr(e):
                    ins = None
                    for dc in range(8):
                        ins = e.transpose(out=PST[0][:, dc * 128:(dc + 1) * 128],
                                          in_=xn_t[:, dc * 128:(dc + 1) * 128], identity=ident_b)
                    return ins
                op("pe", tr, reads=[xn_b, cb_b], writes=[PST[1]])
                for dc in range(8):
                    if dc % 2 == 0:
                        op("dve", lambda e: e.tensor_scalar_mul(
                            out=dstT[:, dc, ti * 128:(ti + 1) * 128], in0=PST[0][:, dc * 128:(dc + 1) * 128],
                            scalar1=gm[:, dc:dc + 1]), reads=[PST[1], gm_b], writes=[dstT_b])
                    else:
                        op("act", lambda e: e.activation(
                            out=dstT[:, dc, ti * 128:(ti + 1) * 128], in_=PST[0][:, dc * 128:(dc + 1) * 128],
                            func=AF.Copy, scale=gm[:, dc:dc + 1]), reads=[PST[1], gm_b], writes=[dstT_b])

        def proj_T(ps, W, c0, m, rhs_fn, nk=8):
            def f(e):
                ins = None
                for k in range(nk):
                    ins = e.matmul(ps, lhsT=W[:, k, c0:c0 + m], rhs=rhs_fn(k), start=(k == 0), stop=(k == nk - 1))
                return ins
            return f

        with ExitStack() as ph:
            rmsnorm_T(x_d, NT, g_mix_d, uT, uT_b, ph, "a")
        kb.barrier()
        if STOP == "T1":
            return nc

        es_w = ExitStack()
        Wg, Wg_b = sb("Wg", [128, 8, 1552], BF16, es_w)
        wup, wup_b = sb("wup", [128, 256], BF16, es_w)
        es_w2 = ExitStack()
        Wkv, Wkv_b = sb("Wkv", [128, 8, 1024], BF16, es_w2)
        Wmq, Wmq_b = sb("Wmq", [128, 8, 512], BF16, es_w2)
        with ExitStack() as ph:
            Wf, Wf_b = sb("Wf", [128, 8, 1544], BF16, ph)
            for dc in range(8):
                load_w(Wf[:, dc, 0:1536], w_in_d[dc * 128:(dc + 1) * 128, O_FQ:O_FQ + 1536], Wf_b)
                load_w(Wf[:, dc, 1536:1544], w_in_d[dc * 128:(dc + 1) * 128, O_FF:O_FF + 8], Wf_b)
            for dc in range(8):
                load_w(Wkv[:, dc, :], w_mkv_d[dc * 128:(dc + 1) * 128, :], Wkv_b)
                load_w(Wmq[:, dc, :], w_in_d[dc * 128:(dc + 1) * 128, O_MQ:O_MQ + 512], Wmq_b)
            for dc in range(8):
                load_w(Wg[:, dc, :], w_in_d[dc * 128:(dc + 1) * 128, 0:1552], Wg_b)
            op("dve", lambda e: e.memset(wup[:, :], 0.0), writes=[wup_b])
            load_w(wup[0:16, :], w_up_d[:, :], wup_b)
            nbf, nbf_b = sb("nbf", [8, 1], F32, ph)
            dma("sp", nbf[:, :], b_forget_d[:, :], writes=[nbf_b])
            op("dve", lambda e: e.tensor_scalar_mul(out=nbf[:, :], in0=nbf[:, :], scalar1=-1.0),
               reads=[nbf_b], writes=[nbf_b])
            stg = [sb("stg%d" % i, [128, 512], BF16, ph) for i in range(3)]
            lfT, lfT_b = sb("lfT", [8, T], F32, ph)
            etmp, etmp_b = sb("etmp", [8, 512], F32, ph)
            si = 0
            pi = 0
            pay3 = pay.rearrange("(h r) c -> h r c", r=PAY_R)
            payv = pay3[:, 128:193, :].rearrange("h r c -> h (r c)").rearrange("h (p k c) -> p h k c", p=128, k=16)
            stgv = [sb("stgv%d" % i, [128, 8, 65], BF16, ph) for i in range(2)]
            for sv, sv_b in stgv:
                op("dve", lambda e: e.memset(sv[:, :, :], 1.0), writes=[sv_b])
            pay_b = Buf()
            payf_b = Buf()
            for qt in range(NQ):
                ts = slice(qt * 512, (qt + 1) * 512)
                for which in range(2):
                    for hp in range(4):
                        ps, ps_b = PS[pi % 4]
                        pi += 1
                        st, st_b = stg[si % 3]
                        si += 1
                        op("pe", proj_T(ps[:, :], Wf, which * 512 + hp * 128, 128, lambda k: uT[:, k, ts]),
                           reads=[Wf_b, uT_b], writes=[ps_b])
                        op("act", lambda e: e.activation(out=st[:, :], in_=ps[:, :], func=AF.Copy,
                                                         scale=(0.125 if which == 0 else 1.0)),
                           reads=[ps_b], writes=[st_b])
                        for a in range(2):
                            h = 2 * hp + a
                            dma("sp", pay3[h, which * 64:(which + 1) * 64, ts], st[a * 64:(a + 1) * 64, :],
                                reads=[st_b], writes=[pay_b])
                for tt in range(4):
                    tk = slice(qt * 512 + tt * 128, qt * 512 + (tt + 1) * 128)
                    ps, ps_b = PS[pi % 4]
                    pi += 1
                    st, st_b = stgv[tt % 2]

                    def fv(e):
                        ins = None
                        for k in range(8):
                            ins = e.matmul(ps[:, :], lhsT=uT[:, k, tk], rhs=Wf[:, k, 1024:1536],
                                           start=(k == 0), stop=(k == 7))
                        return ins
                    op("pe", fv, reads=[Wf_b, uT_b], writes=[ps_b])
                    op("dve", lambda e: e.tensor_copy(out=st[:, :, 0:64], in_=ps[:, :].rearrange("p (h b) -> p h b", b=64)),
                       reads=[ps_b], writes=[st_b])
                    dma("sp", payv[:, :, qt * 4 + tt, :], st[:, :, :], reads=[st_b], writes=[pay_b])
                ps, ps_b = PS[pi % 4]
                pi += 1
                op("pe", proj_T(ps[0:8, :], Wf, 1536, 8, lambda k: uT[:, k, ts]), reads=[Wf_b, uT_b], writes=[ps_b])
                op("act", lambda e: e.activation(out=etmp[:, :], in_=ps[0:8, :], func=AF.Exp, bias=nbf[:, 0:1], scale=-1.0),
                   reads=[ps_b, nbf_b], writes=[etmp_b])
                op("act", lambda e: e.activation(out=etmp[:, :], in_=etmp[:, :], func=AF.Ln, bias=1.0, scale=1.0),
                   reads=[etmp_b], writes=[etmp_b])
                op("dve", lambda e: e.tensor_scalar_mul(out=lfT[:, ts], in0=etmp[:, :], scalar1=-1.0),
                   reads=[etmp_b], writes=[lfT_b])
            dma("sp", payf[:, :], lfT[:, :], reads=[lfT_b], writes=[payf_b])
            gat_b = Buf()
            gatf_b = Buf()
            kb.collective(pay_t, gat_t, reads=[pay_b], writes=[gat_b])
            kb.collective(payf_t, gatf_t, reads=[payf_b], writes=[gatf_b])
        kb.barrier()
        if STOP == "T2":
            return nc

        with ExitStack() as pm:
            mnT, mnT_b = sb("mnT", [128, 8, 256], BF16, pm)
            rmsnorm_T(mem_d, 2, g_mem_d, mnT, mnT_b, pm, "m")
            mkT, mkT_b = sb("mkT", [128, 4, 256], BF16, pm)
            mv, mv_b = sb("mv", [128, 2, 512], BF16, pm)
            mq, mq_b = sb("mq", [128, 512], BF16, pm)
            pT = [sb("pT%d" % i, [128, 512], BF16, pm) for i in range(2)]
            rd, rd_b = sb("rd", [128, 512], F32, pm)
            for h in range(4):
                ps, ps_b = PS[h % 4]
                op("pe", proj_T(ps[:, 0:256], Wkv, h * 128, 128, lambda k: mnT[:, k, :]),
                   reads=[Wkv_b, mnT_b], writes=[ps_b])
                op("act", lambda e: e.activation(out=mkT[:, h, :], in_=ps[:, 0:256], func=AF.Copy),
                   reads=[ps_b], writes=[mkT_b])
            for mt in range(2):
                ps, ps_b = PS[mt]

                def fmv(e):
                    ins = None
                    for k in range(8):
                        ins = e.matmul(ps[:, :], lhsT=mnT[:, k, mt * 128:(mt + 1) * 128], rhs=Wkv[:, k, 512:1024],
                                       start=(k == 0), stop=(k == 7))
                    return ins
                op("pe", fmv, reads=[Wkv_b, mnT_b], writes=[ps_b])
                op("act", lambda e: e.activation(out=mv[:, mt, :], in_=ps[:, :], func=AF.Copy),
                   reads=[ps_b], writes=[mv_b])
            for qt in range(NQ):
                ts = slice(qt * 512, (qt + 1) * 512)
                for h in range(4):
                    ps, ps_b = PS[0]
                    op("pe", proj_T(ps[:, :], Wmq, h * 128, 128, lambda k: uT[:, k, ts]),
                       reads=[Wmq_b, uT_b], writes=[ps_b])
                    op("dve", lambda e: e.tensor_scalar_mul(out=mq[:, :], in0=ps[:, :], scalar1=float(128 ** -0.5)),
                       reads=[ps_b], writes=[mq_b])
                    for mt in range(2):
                        pss, pss_b = PS[1 + mt]
                        p_t, p_b = pT[mt]
                        op("pe", lambda e: e.matmul(pss[:, :], lhsT=mkT[:, h, mt * 128:(mt + 1) * 128], rhs=mq[:, :],
                                                    start=True, stop=True), reads=[mkT_b, mq_b], writes=[pss_b])
                        op("act", lambda e: e.activation(out=p_t[:, :], in_=pss[:, :], func=AF.Exp),
                           reads=[pss_b], writes=[p_b])
                    pso, pso_b = PS[3]
                    psd, psd_b = PS[4]

                    def fo(e):
                        ins = None
                        for mt in range(2):
                            e.matmul(pso[:, :], lhsT=mv[:, mt, h * 128:(h + 1) * 128], rhs=pT[mt][0][:, :],
                                     start=(mt == 0), stop=(mt == 1))
                        for mt in range(2):
                            ins = e.matmul(psd[:, :], lhsT=ones_b, rhs=pT[mt][0][:, :], start=(mt == 0), stop=(mt == 1))
                        return ins
                    op("pe", fo, reads=[mv_b, pT[0][1], pT[1][1], cb_b], writes=[pso_b, psd_b])
                    op("dve", lambda e: e.reciprocal(out=rd[:, :], in_=psd[:, :]), reads=[psd_b], writes=[rd_b])
                    op("dve", lambda e: e.tensor_tensor(out=omemT[:, h, ts], in0=pso[:, :], in1=rd[:, :], op=ALU.mult),
                       reads=[pso_b, rd_b], writes=[omemT_b])
        kb.barrier()
        if STOP == "Tmem":
            return nc
        es_w2.close()

        gsg_b = Buf()
        with ExitStack() as ph:
            balp, balp_b = sb("balp", [128, 256], F32, ph)
            dma("sp", balp[:, :], b_alpha_d.partition_broadcast(128), writes=[balp_b])
            ggl, ggl_b = sb("ggl", [128, 4], F32, ph)
            with nc.allow_non_contiguous_dma(reason="tiny gain transpose"):
                dma("sp", ggl[:, :], g_gla_d.rearrange("h p -> p h"), writes=[ggl_b])
            cmask, cmask_b = sb("cmask", [128, 8], F32, ph)
            dma("sp", cmask[:, :], cmask_d[:, :], writes=[cmask_b])

            qpT, qpT_b = sb("qpT", [128, 2, T], BF16, ph)
            attn, attn_b = sb("attn", [128, NT, 512], BF16, ph)
            kdec, kdec_b = sb("kdec", [128, NT, 2, 256], BF16, ph)
            op("dve", lambda e: e.memset(kdec[:, :, :, :], 0.0), writes=[kdec_b])
            vtok, vtok_b = sb("vtok", [128, NT, 512], BF16, ph)
            decs, decs_b = sb("decs", [128, 2, 32], F32, ph)
            qT, qT_b = sb("qT", [128, 2, 512], F32, ph)
            kT, kT_b = sb("kT", [128, 2, 512], F32, ph)
            gaT, gaT_b = sb("gaT", [128, 512], BF16, ph)
            op("dve", lambda e: e.memset(gaT[:, :], 0.0), writes=[gaT_b])
            lsb, lsb_b = sb("lsb", [128, 256], F32, ph)
            Ep, Ep_b = sb("Ep", [128, 256], F32, ph)
            En, En_b = sb("En", [128, 256], F32, ph)
            edl, edl_b = sb("edl", [128, 256], F32, ph)
            qn, qn_b = sb("qn", [128, 256], BF16, ph)
            kp, kp_b = sb("kp", [128, 2, 2, 128], BF16, ph)
            kn, kn_b = sb("kn", [128, 2, 2, 128], BF16, ph)
            op("dve", lambda e: e.memset(kp[:, :, :, :], 0.0), writes=[kp_b])
            op("dve", lambda e: e.memset(kn[:, :, :, :], 0.0), writes=[kn_b])
            t1, t1_b = sb("t1", [128, 512], F32, ph)
            t2, t2_b = sb("t2", [128, 512], F32, ph)
            pi = 0
            for qt in range(NQ):
                ts = slice(qt * 512, (qt + 1) * 512)
                for fc in range(2):
                    ps, ps_b = PS[pi % 4]
                    pi += 1
                    op("pe", proj_T(ps[:, :], Wg, O_GQ + fc * 128, 128, lambda k: uT[:, k, ts]),
                       reads=[Wg_b, uT_b], writes=[ps_b])
                    op("act", lambda e: e.activation(out=qT[:, fc, :], in_=ps[:, :], func=AF.Copy, scale=0.125),
                       reads=[ps_b], writes=[qT_b])
                    ps, ps_b = PS[pi % 4]
                    pi += 1
                    op("pe", proj_T(ps[:, :], Wg, O_GK + fc * 128, 128, lambda k: uT[:, k, ts]),
                       reads=[Wg_b, uT_b], writes=[ps_b])
                    op("dve", lambda e: e.tensor_copy(out=kT[:, fc, :], in_=ps[:, :]), reads=[ps_b], writes=[kT_b])
                ps, ps_b = PS[pi % 4]
                pi += 1
                op("pe", proj_T(ps[0:16, :], Wg, O_GA, 16, lambda k: uT[:, k, ts]), reads=[Wg_b, uT_b], writes=[ps_b])
                op("dve", lambda e: e.tensor_copy(out=gaT[0:16, :], in_=ps[0:16, :]), reads=[ps_b], writes=[gaT_b])
                for tt in range(4):
                    ti = qt * 4 + tt
                    tk = slice(ti * 128, (ti + 1) * 128)
                    lt = slice(tt * 128, (tt + 1) * 128)
                    ps, ps_b = PS[pi % 4]
                    pi += 1
                    op("pe", lambda e: e.matmul(ps[:, 0:256], lhsT=gaT[:, lt], rhs=wup[:, :], start=True, stop=True),
                       reads=[gaT_b, wup_b], writes=[ps_b])
                    op("dve", lambda e: e.tensor_tensor(out=lsb[:, :], in0=ps[:, 0:256], in1=balp[:, :], op=ALU.add),
                       reads=[ps_b, balp_b], writes=[lsb_b])
                    op("act", lambda e: e.activation(out=lsb[:, :], in_=lsb[:, :], func=AF.Exp, scale=-1.0),
                       reads=[lsb_b], writes=[lsb_b])
                    op("act", lambda e: e.activation(out=lsb[:, :], in_=lsb[:, :], func=AF.Ln, bias=1.0, scale=1.0),
                       reads=[lsb_b], writes=[lsb_b])
                    ps, ps_b = PS[pi % 4]
                    pi += 1

                    def gv(e):
                        ins = None
                        for k in range(8):
                            ins = e.matmul(ps[:, :], lhsT=uT[:, k, tk], rhs=Wg[:, k, O_GV:O_GV + 512],
                                           start=(k == 0), stop=(k == 7))
                        return ins
                    op("pe", gv, reads=[Wg_b, uT_b], writes=[ps_b])
                    op("act", lambda e: e.activation(out=vtok[:, ti, :], in_=ps[:, :], func=AF.Copy),
                       reads=[ps_b], writes=[vtok_b])
                    psb, psb_b = PS[pi % 4]
                    pi += 1

                    def fb(e):
                        ins = None
                        for fc in range(2):
                            ins = e.matmul(psb[:, fc * 128:(fc + 1) * 128], lhsT=lsb[:, fc * 128:(fc + 1) * 128],
                                           rhs=cf[:, C_TRIB:C_TRIB + 128], start=True, stop=True)
                        return ins
                    op("pe", fb, reads=[lsb_b, cf_b], writes=[psb_b])
                    op("act", lambda e: e.activation(out=Ep[:, :], in_=psb[:, 0:256], func=AF.Exp),
                       reads=[psb_b], writes=[Ep_b])
                    op("act", lambda e: e.activation(out=En[:, :], in_=psb[:, 0:256], func=AF.Exp, scale=-1.0),
                       reads=[psb_b], writes=[En_b])
                    Ep3 = Ep[:, :].rearrange("p (f t) -> p f t", t=128)
                    En3 = En[:, :].rearrange("p (f t) -> p f t", t=128)
                    op("dve", lambda e: e.tensor_tensor(out=qpT[:, :, tk], in0=qT[:, :, lt], in1=Ep3, op=ALU.mult),
                       reads=[qT_b, Ep_b], writes=[qpT_b])
                    op("dve", lambda e: e.tensor_tensor(out=qn[:, :].rearrange("p (f t) -> p f t", t=128),
                                                        in0=qT[:, :, lt], in1=En3, op=ALU.mult),
                       reads=[qT_b, En_b], writes=[qn_b])
                    for a in range(2):
                        pr = slice(a * 64, (a + 1) * 64)
                        op("dve", lambda e: e.tensor_tensor(out=kp[pr, :, a, :], in0=kT[pr, :, lt], in1=Ep3[pr], op=ALU.mult),
                           reads=[kT_b, Ep_b], writes=[kp_b])
                        op("dve", lambda e: e.tensor_tensor(out=kn[pr, :, a, :], in0=kT[pr, :, lt], in1=En3[pr], op=ALU.mult),
                           reads=[kT_b, En_b], writes=[kn_b])
                    for fc in range(2):
                        op("dve", lambda e: e.tensor_copy(
                            out=decs[:, fc, 2 * ti:2 * ti + 2],
                            in_=Ep[:, fc * 128:(fc + 1) * 128].rearrange("p (j t) -> p j t", t=64)[:, :, 63]),
                           reads=[Ep_b], writes=[decs_b])
                    psd, psd_b = PS[pi % 4]
                    pi += 1
                    op("pe", lambda e: e.matmul(psd[:, 0:256], lhsT=cf[:, C_UPB:C_UPB + 128], rhs=lsb[:, :],
                                                start=True, stop=True), reads=[lsb_b, cf_b], writes=[psd_b])
                    op("act", lambda e: e.activation(out=edl[:, :], in_=psd[:, 0:256], func=AF.Exp),
                       reads=[psd_b], writes=[edl_b])
                    psk, psk_b = PS[pi % 4]
                    pi += 1

                    def gk(e):
                        ins = None
                        for k in range(8):
                            ins = e.matmul(psk[:, 0:256], lhsT=uT[:, k, tk], rhs=Wg[:, k, O_GK:O_GK + 256],
                                           start=(k == 0), stop=(k == 7))
                        return ins
                    op("pe", gk, reads=[Wg_b, uT_b], writes=[psk_b])
                    for j in range(2):
                        pr = slice(j * 64, (j + 1) * 64)
                        op("dve", lambda e: e.tensor_tensor(out=kdec[pr, ti, j, :], in0=psk[pr, 0:256], in1=edl[pr, :], op=ALU.mult),
                           reads=[psk_b, edl_b], writes=[kdec_b])
                    pac, pac_b = PS[4]
                    paa, paa_b = PS[5]

                    def fa(e):
                        ins = None
                        for h in range(4):
                            fc, a = h // 2, h % 2
                            e.matmul(pac[:, h * 128:(h + 1) * 128], lhsT=kn[:, fc, a, :],
                                     rhs=qpT[:, fc, tk], start=True, stop=True)
                            ins = e.matmul(paa[:, h * 128:(h + 1) * 128], lhsT=kp[:, fc, a, :],
                                           rhs=qn[:, fc * 128:(fc + 1) * 128], start=True, stop=True)
                        return ins
                    op("pe", fa, reads=[kn_b, kp_b, qn_b, qpT_b], writes=[pac_b, paa_b])
                    op("dve", lambda e: e.tensor_tensor(out=t1[:, :], in0=pac[:, :], in1=cf[:, C_U2:C_U2 + 512], op=ALU.mult),
                       reads=[pac_b, cf_b], writes=[t1_b])
                    op("dve", lambda e: e.tensor_tensor(out=t2[:, :], in0=paa[:, :], in1=cf[:, C_L2:C_L2 + 512], op=ALU.mult),
                       reads=[paa_b, cf_b], writes=[t2_b])
                    op("dve", lambda e: e.tensor_tensor(out=attn[:, ti, :], in0=t1[:, :], in1=t2[:, :], op=ALU.add),
                       reads=[t1_b, t2_b], writes=[attn_b])

            if STOP == "T3a":
                kb.barrier()
                return nc
            Sst, Sst_b = sb("Sst", [128, 2, 256], F32, ph)
            Pst, Pst_b = sb("Pst", [128, 2], F32, ph)
            Sbf, Sbf_b = sb("Sbf", [128, 2, 2, 128], BF16, ph)
            op("dve", lambda e: e.memset(Sbf[:, :, :, :], 0.0), writes=[Sbf_b])

            def kv_update(n):
                ti, j = n // 2, n % 2
                for fc in range(2):
                    ps, ps_b = PS[(2 * n + fc) % 4]
                    op("pe", lambda e: e.matmul(ps[:, 0:256], lhsT=kdec[:, ti, j, fc * 128:(fc + 1) * 128],
                                                rhs=vtok[:, ti, fc * 256:(fc + 1) * 256],
                                                start=True, stop=True),
                       reads=[kdec_b, vtok_b], writes=[ps_b])
                    op("dve", lambda e: e.scalar_tensor_tensor(out=Sst[:, fc, :], in0=Sst[:, fc, :],
                                                               scalar=decs[:, fc, n:n + 1], in1=ps[:, 0:256],
                                                               op0=ALU.mult, op1=ALU.add),
                       reads=[ps_b, decs_b, Sst_b], writes=[Sst_b])

            op("dve", lambda e: e.memset(Sst[:, :, :], 0.0), writes=[Sst_b])
            op("dve", lambda e: e.memset(Pst[:, :], 1.0), writes=[Pst_b])
            for n in range(32):
                kv_update(n)
                op("dve", lambda e: e.tensor_tensor(out=Pst[:, :], in0=Pst[:, :], in1=decs[:, :, n], op=ALU.mult),
                   reads=[decs_b, Pst_b], writes=[Pst_b])
            gs_b = Buf()
            dma("sp", gs[:, 0:512], Sst[:, :, :].rearrange("p f v -> p (f v)"), reads=[Sst_b], writes=[gs_b])
            dma("sp", gs[:, 512:514], Pst[:, :], reads=[Pst_b], writes=[gs_b])
            kb.collective(gs_t, gsg_t, reads=[gs_b], writes=[gsg_b])

            kb.barrier()

            if STOP == "T3b":
                kb.barrier()
                return nc
            Ain, Ain_b = sb("Ain", [128, 514], F32, ph)
            Pp, Pp_b = sb("Pp", [128, 2], F32, ph)
            op("dve", lambda e: e.memset(Sst[:, :, :], 0.0), writes=[Sst_b])
            for c2 in range(NCORE - 1):
                dma("sp", Ain[:, :], gsg[c2 * 128:(c2 + 1) * 128, :], reads=[gsg_b], writes=[Ain_b])
                op("dve", lambda e: e.tensor_scalar(out=Pp[:, :], in0=Ain[:, 512:514], scalar1=-1.0,
                                                    scalar2=cmask[:, c2:c2 + 1], op0=ALU.add, op1=ALU.mult),
                   reads=[Ain_b, cmask_b], writes=[Pp_b])
                op("dve", lambda e: e.tensor_scalar_add(out=Pp[:, :], in0=Pp[:, :], scalar1=1.0),
                   reads=[Pp_b], writes=[Pp_b])
                op("dve", lambda e: e.tensor_scalar_mul(out=Ain[:, 0:512], in0=Ain[:, 0:512], scalar1=cmask[:, c2:c2 + 1]),
                   reads=[Ain_b, cmask_b], writes=[Ain_b])
                for fc in range(2):
                    op("dve", lambda e: e.scalar_tensor_tensor(out=Sst[:, fc, :], in0=Sst[:, fc, :],
                                                               scalar=Pp[:, fc:fc + 1], in1=Ain[:, fc * 256:(fc + 1) * 256],
                                                               op0=ALU.mult, op1=ALU.add),
                       reads=[Pp_b, Ain_b, Sst_b], writes=[Sst_b])

            if STOP == "T3c":
                kb.barrier()
                return nc
            og, og_b = sb("og", [128, 512], F32, ph)
            osq, osq_b = sb("osq", [128, 512], BF16, ph)
            rstd, rstd_b = sb("rstd", [128, 512], F32, ph)
            sg, sg_b = sb("sg", [128, 512], F32, ph)
            for ti in range(NT):
                tk = slice(ti * 128, (ti + 1) * 128)
                pog, pog_b = PS[4]
                for j in range(2):
                    n = 2 * ti + j
                    for a in range(2):
                        pr = slice(a * 64, (a + 1) * 64)
                        op("act", lambda e: e.activation(out=Sbf[pr, :, a, :], in_=Sst[pr, :, a * 128:(a + 1) * 128], func=AF.Copy),
                           reads=[Sst_b], writes=[Sbf_b])

                    def fo2(e):
                        ins = None
                        for h in range(4):
                            fc, a = h // 2, h % 2
                            cs = slice(h * 128 + j * 64, h * 128 + j * 64 + 64)
                            e.matmul(pog[:, cs], lhsT=vtok[:, ti, h * 128:(h + 1) * 128],
                                     rhs=attn[:, ti, h * 128 + j * 64:h * 128 + j * 64 + 64], start=True, stop=False)
                            ins = e.matmul(pog[:, cs], lhsT=Sbf[:, fc, a, :],
                                           rhs=qpT[:, fc, ti * 128 + j * 64:ti * 128 + j * 64 + 64],
                                           start=False, stop=True)
                        return ins
                    op("pe", fo2, reads=[vtok_b, attn_b, Sbf_b, qpT_b], writes=[pog_b])
                    kv_update(n)
                op("act", lambda e: e.activation(out=og[:, :], in_=pog[:, :], func=AF.Copy), reads=[pog_b], writes=[og_b])
                op("dve", lambda e: e.tensor_tensor(out=osq[:, :], in0=og[:, :], in1=og[:, :], op=ALU.mult),
                   reads=[og_b], writes=[osq_b])
                pss, pss_b = PS[5]
                op("pe", lambda e: e.matmul(pss[:, :], lhsT=ones_b, rhs=osq[:, :], start=True, stop=True),
                   reads=[osq_b, cb_b], writes=[pss_b])
                op("dve", lambda e: e.tensor_scalar(out=rstd[:, :], in0=pss[:, :], scalar1=1.0 / 128.0, scalar2=EPS,
                                                    op0=ALU.mult, op1=ALU.add), reads=[pss_b], writes=[rstd_b])
                op("act", lambda e: e.activation(out=rstd[:, :], in_=rstd[:, :], func=AF.Ln), reads=[rstd_b], writes=[rstd_b])
                op("act", lambda e: e.activation(out=rstd[:, :], in_=rstd[:, :], func=AF.Exp, scale=-0.5),
                   reads=[rstd_b], writes=[rstd_b])
                op("dve", lambda e: e.tensor_tensor(out=og[:, :], in0=og[:, :], in1=rstd[:, :], op=ALU.mult),
                   reads=[og_b, rstd_b], writes=[og_b])
                psg, psg_b = PS[6]

                def fgg(e):
                    ins = None
                    for h in range(4):
                        for k in range(8):
                            ins = e.matmul(psg[:, h * 128:(h + 1) * 128], lhsT=Wg[:, k, O_GG + h * 128:O_GG + (h + 1) * 128],
                                           rhs=uT[:, k, tk], start=(k == 0), stop=(k == 7))
                    return ins
                op("pe", fgg, reads=[Wg_b, uT_b], writes=[psg_b])
                op("act", lambda e: e.activation(out=sg[:, :], in_=psg[:, :], func=AF.Exp, scale=-1.0),
                   reads=[psg_b], writes=[sg_b])
                op("dve", lambda e: e.tensor_scalar_add(out=sg[:, :], in0=sg[:, :], scalar1=1.0), reads=[sg_b], writes=[sg_b])
                op("dve", lambda e: e.reciprocal(out=sg[:, :], in_=sg[:, :]), reads=[sg_b], writes=[sg_b])
                op("dve", lambda e: e.tensor_tensor(out=sg[:, :], in0=psg[:, :], in1=sg[:, :], op=ALU.mult),
                   reads=[psg_b, sg_b], writes=[sg_b])
                op("dve", lambda e: e.tensor_tensor(out=og[:, :], in0=og[:, :], in1=sg[:, :], op=ALU.mult),
                   reads=[og_b, sg_b], writes=[og_b])
                for h in range(4):
                    op("dve", lambda e: e.tensor_scalar_mul(out=oglaT[:, h, tk], in0=og[:, h * 128:(h + 1) * 128],
                                                            scalar1=ggl[:, h:h + 1]),
                       reads=[og_b, ggl_b], writes=[oglaT_b])
        kb.barrier()
        if STOP == "T3":
            return nc
        es_w.close()

        pay2_b = Buf()
        gat2_b = Buf()
        with ExitStack() as ph:
            Qp, Qp_b = sb("Qp", [128, S], BF16, ph)
            Kp, Kp_b = sb("Kp", [128, S], BF16, ph)
            op("dve", lambda e: e.memset(Qp[:, :], 0.0), writes=[Qp_b])
            op("pool", lambda e: e.memset(Kp[:, :], 0.0), writes=[Kp_b])
            Va, Va_b = sb("Va", [128, 128, 65], BF16, ph)
            L, L_b = sb("L", [128, 128], F32, ph)
            gatq = gat.rearrange("(r h q) c -> h q r c", r=NCORE, h=8)[v_cid]
            dma("sp", Qp[0:64, :].rearrange("q (r c) -> q r c", r=NCORE), gatq[0:64], reads=[gat_b], writes=[Qp_b])
            dma("sp", Kp[0:64, :].rearrange("q (r c) -> q r c", r=NCORE), gatq[64:128], reads=[gat_b], writes=[Kp_b])
            gatv = gat.rearrange("(r h q) c -> h r q c", r=NCORE, h=8)[v_cid][:, 128:193, :]
            gatv = gatv.rearrange("r q c -> r (q c)").rearrange("r (p kc) -> p r kc", p=128)
            dma("sp", Va[:, :, :].rearrange("p (r k) c -> p r (k c)", r=NCORE), gatv, reads=[gat_b], writes=[Va_b])
            dma("sp", L[:, :], gatf.rearrange("(r h) (i p) -> h r i p", h=8, p=128)[v_cid], reads=[gatf_b], writes=[L_b])
            op("dve", lambda e: e.memset(Kp[64:65, :], 1.0), writes=[Kp_b])
            LT, LT_b = sb("LT", [128, 128], F32, ph)
            Xr, Xr_b = sb("Xr", [128, 128], F32, ph)
            Fk, Fk_b = sb("Fk", [128, 128], F32, ph)
            Rrep, Rrep_b = sb("Rrep", [128, 32], F32, ph)
            Cb, Cb_b = sb("Cb", [128, 128], BF16, ph)
            ps, ps_b = PS[0]
            op("pe", lambda e: e.matmul(ps[:, 0:128], lhsT=L[:, :], rhs=ident_f, start=True, stop=True),
               reads=[L_b, cf_b], writes=[ps_b])
            op("dve", lambda e: e.tensor_copy(out=LT[:, :], in_=ps[:, 0:128]), reads=[ps_b], writes=[LT_b])
            ps, ps_b = PS[1]
            op("pe", lambda e: e.matmul(ps[:, 0:128], lhsT=LT[:, :], rhs=ones_f, start=True, stop=True),
               reads=[LT_b, cf_b], writes=[ps_b])
            op("dve", lambda e: e.tensor_copy(out=Xr[:, :], in_=ps[:, 0:128]), reads=[ps_b], writes=[Xr_b])
            ps, ps_b = PS[2]

            def fF(e):
                e.matmul(ps[:, 0:128], lhsT=cf[:, C_TRI:C_TRI + 128], rhs=LT[:, :], start=True, stop=False)
                return e.matmul(ps[:, 0:128], lhsT=Xr[:, :], rhs=cf[:, C_SU:C_SU + 128], start=False, stop=True)
            op("pe", fF, reads=[LT_b, Xr_b, cf_b], writes=[ps_b])
            op("dve", lambda e: e.tensor_copy(out=Fk[:, :], in_=ps[:, 0:128]), reads=[ps_b], writes=[Fk_b])
            ps, ps_b = PS[3]
            op("pe", lambda e: e.matmul(ps[:, 0:32], lhsT=Xr[:, :], rhs=cf[:, C_SU4:C_SU4 + 32], start=True, stop=True),
               reads=[Xr_b, cf_b], writes=[ps_b])
            op("dve", lambda e: e.tensor_copy(out=Rrep[:, :], in_=ps[:, 0:32]), reads=[ps_b], writes=[Rrep_b])
            ps, ps_b = PS[0]

            def fC(e):
                e.matmul(ps[:, 0:128], lhsT=LT[:, :], rhs=cf[:, C_TRI:C_TRI + 128], start=True, stop=False)
                return e.matmul(ps[:, 0:128], lhsT=cf[:, C_G4:C_G4 + 128], rhs=Xr[:, :], start=False, stop=True)
            op("pe", fC, reads=[LT_b, Xr_b, cf_b], writes=[ps_b])
            op("dve", lambda e: e.tensor_copy(out=Cb[:, :], in_=ps[:, 0:128]), reads=[ps_b], writes=[Cb_b])
            cd_b = Buf()
            dma("sp", cd[:, :], Cb[:, :], reads=[Cb_b], writes=[cd_b])
            dma("sp", Qp[64:65, :], cd.rearrange("(o k) p -> o (k p)", o=1), reads=[cd_b], writes=[Qp_b])

            biasq = [sb("biasq%d" % i, [128, 128], F32, ph) for i in range(2)]
            PT = [sb("PT%d" % i, [128, 512], BF16, ph) for i in range(4)]
            rec, rec_b = sb("rec", [128, 512], F32, ph)
            op("dve", lambda e: e.memset(rec[:, :], 0.0), writes=[rec_b])
            bcs, bcs_b = sb("bcs", [64, 512], F32, ph)
            ost = [sb("ost%d" % i, [64, 512], BF16, ph) for i in range(2)]
            steps = [(qi, kt) for qi in range(S // 512) for kt in range(4 * qi + 4)]
            SB = [PS[0], PS[1], PS[2], PS[6]]
            LOOK = 3

            def emit_s(sidx):
                qi, kt = steps[sidx]
                nkt = 4 * qi + 4
                qs = slice(qi * 512, (qi + 1) * 512)
                bq, bq_b = biasq[qi % 2]
                if kt == 0:
                    op("dve", lambda e: e.tensor_scalar(out=bq[:, 0:nkt], in0=Fk[:, 0:nkt], scalar1=Rrep[:, qi:qi + 1],
                                                        scalar2=-1.0, op0=ALU.subtract, op1=ALU.mult),
                       reads=[Fk_b, Rrep_b], writes=[bq_b])
                pss, pss_b = SB[sidx % 4]
                j = kt - 4 * qi

                def fs(e):
                    ins = e.matmul(pss[:, :], lhsT=Kp[:, kt * 128:(kt + 1) * 128], rhs=Qp[:, qs],
                                   start=True, stop=(j < 0))
                    if j >= 0:
                        ins = e.matmul(pss[:, :], lhsT=ident_b, rhs=cb[:, B_MASK + j * 512:B_MASK + (j + 1) * 512],
                                       start=False, stop=True)
                    return ins
                op("pe", fs, reads=[Kp_b, Qp_b, cb_b], writes=[pss_b])

            def emit_rest(sidx):
                qi, kt = steps[sidx]
                nkt = 4 * qi + 4
                qs = slice(qi * 512, (qi + 1) * 512)
                bq, bq_b = biasq[qi % 2]
                pss, pss_b = SB[sidx % 4]
                p_t, p_b = PT[sidx % 4]
                pso, pso_b = PS[3 + (qi % 2)]
                op("act", lambda e: e.activation(out=p_t[:, :], in_=pss[:, :], func=AF.Exp, bias=bq[:, kt:kt + 1], scale=1.0),
                   reads=[pss_b, bq_b], writes=[p_b])
                op("pe", lambda e: e.matmul(pso[0:65, :], lhsT=Va[:, kt, 0:65], rhs=p_t[:, :],
                                            start=(kt == 0), stop=(kt == nkt - 1)),
                   reads=[Va_b, p_b], writes=([pso_b] if kt in (0, nkt - 1) else []))
                if kt != nkt - 1:
                    return
                op("dve", lambda e: e.reciprocal(out=rec[64:65, :], in_=pso[64:65, :]), reads=[pso_b], writes=[rec_b])
                pbc, pbc_b = PS[5]
                op("pe", lambda e: e.matmul(pbc[0:64, :], lhsT=cf[:, C_SEL:C_SEL + 64], rhs=rec[:, :],
                                            start=True, stop=True), reads=[rec_b, cf_b], writes=[pbc_b])
                op("act", lambda e: e.activation(out=bcs[:, :], in_=pbc[0:64, :], func=AF.Copy), reads=[pbc_b], writes=[bcs_b])
                o_t, o_b = ost[qi % 2]
                op("dve", lambda e: e.tensor_tensor(out=o_t[:, :], in0=pso[0:64, :], in1=bcs[:, :], op=ALU.mult),
                   reads=[pso_b, bcs_b], writes=[o_b])
                dma("sp", pay2[:, qs], o_t[:, :], reads=[o_b], writes=[pay2_b])

            for sidx in range(LOOK):
                emit_s(sidx)
            for sidx in range(len(steps)):
                if sidx + LOOK < len(steps):
                    emit_s(sidx + LOOK)
                emit_rest(sidx)
            kb.collective(pay2_t, gat2_t, reads=[pay2_b], writes=[gat2_b])
        kb.barrier()
        if STOP == "FOX":
            return nc

        es_m = ExitStack()
        mT, mT_b = sb("mT", [128, 8, T], BF16, es_m)
        with ExitStack() as ph:
            ofoxT, ofoxT_b = sb("ofoxT", [128, 4, T], BF16, ph)
            gat2v = gat2.rearrange("(kc a d) s -> a d kc s", a=2, d=64)
            for a in range(2):
                dma("sp", ofoxT[a * 64:(a + 1) * 64, :, :], gat2v[a][:, :, bass.ds(v_off, T)],
                    reads=[gat2_b], writes=[ofoxT_b])
            oT = [(oglaT, oglaT_b), (ofoxT, ofoxT_b), (omemT, omemT_b)]
            Wgt = [sb("Wgt%d" % i, [128, 8, 1024], BF16, ph) for i in range(2)]
            Wo = [sb("Wo%d" % i, [128, 4, 1024], BF16, ph) for i in range(2)]
            sgm, sgm_b = sb("sgm", [128, 512], BF16, ph)
            tmp, tmp_b = sb("tmpm", [128, 512], BF16, ph)
            pi = 0
            for b in range(3):
                wg_t, wg_b = Wgt[b % 2]
                wo_t, wo_b = Wo[b % 2]
                for dc in range(8):
                    load_w(wg_t[:, dc, :], w_in_d[dc * 128:(dc + 1) * 128, O_GATES + b * 1024:O_GATES + (b + 1) * 1024], wg_b)
                for kc in range(4):
                    load_w(wo_t[:, kc, :], w_o_d[b][kc * 128:(kc + 1) * 128, :], wo_b)
                o_t, o_b = oT[b]
                for qt in range(NQ):
                    ts = slice(qt * 512, (qt + 1) * 512)
                    for fc in range(8):
                        psg, psg_b = PS[pi % 3]
                        psy, psy_b = PS[3 + pi % 3]
                        pi += 1
                        op("pe", proj_T(psg[:, :], wg_t, fc * 128, 128, lambda k: uT[:, k, ts]), reads=[wg_b, uT_b], writes=[psg_b])
                        op("pe", proj_T(psy[:, :], wo_t, fc * 128, 128, lambda k: o_t[:, k, ts], nk=4), reads=[wo_b, o_b], writes=[psy_b])
                        op("act", lambda e: e.activation(out=sgm[:, :], in_=psg[:, :], func=AF.Sigmoid), reads=[psg_b], writes=[sgm_b])
                        if b == 0:
                            op("dve", lambda e: e.tensor_tensor(out=mT[:, fc, ts], in0=psy[:, :], in1=sgm[:, :], op=ALU.mult),
                               reads=[psy_b, sgm_b], writes=[mT_b])
                        else:
                            op("dve", lambda e: e.tensor_tensor(out=tmp[:, :], in0=psy[:, :], in1=sgm[:, :], op=ALU.mult),
                               reads=[psy_b, sgm_b], writes=[tmp_b])
                            op("dve", lambda e: e.tensor_tensor(out=mT[:, fc, ts], in0=mT[:, fc, ts], in1=tmp[:, :], op=ALU.add),
                               reads=[tmp_b, mT_b], writes=[mT_b])
        kb.barrier()
        if STOP == "T5a":
            return nc
        es_r.close()

        es_h = ExitStack()
        hres, hres_b = sb("hres", [128, NT, D], F32, es_h, "right")
        with ExitStack() as ph:
            Wout, Wout_b = sb("Wout", [128, 8, 1024], BF16, ph)
            for kc in range(8):
                load_w(Wout[:, kc, :], w_out_d[kc * 128:(kc + 1) * 128, :], Wout_b)
            gm2, gm2_b = sb("gm2", [128, 8], F32, ph)
            with nc.allow_non_contiguous_dma(reason="tiny gain vector transpose"):
                dma("sp", gm2[:, :], g_ffn_d.rearrange("o (dc p) -> p (o dc)", p=128), writes=[gm2_b])
            xt = [sb("bxt%d" % i, [128, D], F32, ph) for i in range(2)]
            xn = [sb("bxn%d" % i, [128, D], BF16, ph) for i in range(2)]
            junk, junk_b = sb("bjunk", [128, D], BF16, ph)
            ss, ss_b = sb("bss", [128, 16], F32, ph)
            rr, rr_b = sb("brr", [128, 16], F32, ph)
            for ti in range(NT):
                tk = slice(ti * 128, (ti + 1) * 128)
                xt_t, xt_b = xt[ti % 2]
                xn_t, xn_b = xn[ti % 2]
                dma("sp", xt_t[:, :], x_d[tk, :], writes=[xt_b])
                for ch in range(2):
                    ps, ps_b = PS[(2 * ti + ch) % 4]

                    def fh(e):
                        ins = None
                        for k in range(8):
                            ins = e.matmul(ps[:, :], lhsT=mT[:, k, tk], rhs=Wout[:, k, ch * 512:(ch + 1) * 512],
                                           start=(k == 0), stop=(k == 7))
                        return ins
                    op("pe", fh, reads=[mT_b, Wout_b], writes=[ps_b])
                    op("dve", lambda e: e.tensor_tensor(out=hres[:, ti, ch * 512:(ch + 1) * 512], in0=ps[:, :],
                                                        in1=xt_t[:, ch * 512:(ch + 1) * 512], op=ALU.add),
                       reads=[ps_b, xt_b], writes=[hres_b])
                op("act", lambda e: e.activation(out=junk[:, :], in_=hres[:, ti, :], func=AF.Square, accum_out=ss[:, ti:ti + 1]),
                   reads=[hres_b], writes=[junk_b, ss_b])
                op("dve", lambda e: e.tensor_scalar(out=rr[:, ti:ti + 1], in0=ss[:, ti:ti + 1], scalar1=1.0 / D, scalar2=EPS,
                                                    op0=ALU.mult, op1=ALU.add), reads=[ss_b], writes=[rr_b])
                op("act", lambda e: e.activation(out=rr[:, ti:ti + 1], in_=rr[:, ti:ti + 1], func=AF.Ln), reads=[rr_b], writes=[rr_b])
                op("act", lambda e: e.activation(out=rr[:, ti:ti + 1], in_=rr[:, ti:ti + 1], func=AF.Exp, scale=-0.5),
                   reads=[rr_b], writes=[rr_b])
                op("dve", lambda e: e.tensor_scalar_mul(out=xn_t[:, :], in0=hres[:, ti, :], scalar1=rr[:, ti:ti + 1]),
                   reads=[hres_b, rr_b], writes=[xn_b])

                def tr2(e):
                    ins = None
                    for dc in range(8):
                        ins = e.transpose(out=PST[0][:, dc * 128:(dc + 1) * 128], in_=xn_t[:, dc * 128:(dc + 1) * 128],
                                          identity=ident_b)
                    return ins
                op("pe", tr2, reads=[xn_b, cb_b], writes=[PST[1]])
                for dc in range(8):
                    if dc % 2 == 0:
                        op("dve", lambda e: e.tensor_scalar_mul(out=uT[:, dc, tk], in0=PST[0][:, dc * 128:(dc + 1) * 128],
                                                                scalar1=gm2[:, dc:dc + 1]),
                           reads=[PST[1], gm2_b], writes=[uT_b])
                    else:
                        op("act", lambda e: e.activation(out=uT[:, dc, tk], in_=PST[0][:, dc * 128:(dc + 1) * 128],
                                                         func=AF.Copy, scale=gm2[:, dc:dc + 1]),
                           reads=[PST[1], gm2_b], writes=[uT_b])
        kb.barrier()
        if STOP == "T5b":
            return nc
        es_m.close()

        with ExitStack() as ph:
            W1 = [sb("W1_%d" % i, [128, 8, 1024], BF16, ph) for i in range(2)]
            W2 = [sb("W2_%d" % i, [128, 8, 1024], BF16, ph) for i in range(2)]
            aT = [sb("aT%d" % i, [128, 8, 512], BF16, ph) for i in range(2)]
            pi = 0
            ai = 0
            for qf in range(4):
                w1_t, w1_b = W1[qf % 2]
                w2_t, w2_b = W2[qf % 2]
                for dc in range(8):
                    load_w(w1_t[:, dc, :], w_ff1_d[dc * 128:(dc + 1) * 128, qf * 1024:(qf + 1) * 1024], w1_b)
                for fc in range(8):
                    load_w(w2_t[:, fc, :], w_ff2_d[qf * 1024 + fc * 128:qf * 1024 + (fc + 1) * 128, :], w2_b)
                for qt in range(NQ):
                    ts = slice(qt * 512, (qt + 1) * 512)
                    a_t, a_b = aT[ai % 2]
                    ai += 1
                    for fc in range(8):
                        ps, ps_b = PS[pi % 6]
                        pi += 1
                        op("pe", proj_T(ps[:, :], w1_t, fc * 128, 128, lambda k: uT[:, k, ts]), reads=[w1_b, uT_b], writes=[ps_b])
                        op("act", lambda e: e.activation(out=a_t[:, fc, :], in_=ps[:, :], func=AF.Relu), reads=[ps_b], writes=[a_b])
                        op("dve",
                           lambda e: e.tensor_tensor(out=a_t[:, fc, :], in0=a_t[:, fc, :], in1=a_t[:, fc, :], op=ALU.mult),
                           reads=[a_b], writes=[a_b])
                    for tt in range(4):
                        ti = qt * 4 + tt
                        for ch in range(2):
                            ps, ps_b = PS[pi % 6]
                            pi += 1

                            def f2(e):
                                ins = None
                                for fc in range(8):
                                    ins = e.matmul(ps[:, :], lhsT=a_t[:, fc, tt * 128:(tt + 1) * 128],
                                                   rhs=w2_t[:, fc, ch * 512:(ch + 1) * 512], start=(fc == 0), stop=(fc == 7))
                                return ins
                            op("pe", f2, reads=[a_b, w2_b], writes=[ps_b])
                            op("dve", lambda e: e.tensor_tensor(out=hres[:, ti, ch * 512:(ch + 1) * 512],
                                                                in0=hres[:, ti, ch * 512:(ch + 1) * 512], in1=ps[:, :], op=ALU.add),
                               reads=[ps_b, hres_b], writes=[hres_b])
        kb.barrier()
        if STOP == "T5c":
            return nc

        with ExitStack() as ph:
            gfb, gfb_b = sb("gfb", [128, D], F32, ph)
            dma("sp", gfb[:, :], g_fin_d.partition_broadcast(128), writes=[gfb_b])
            junk, junk_b = sb("djunk", [128, D], BF16, ph)
            ss, ss_b = sb("dss", [128, 16], F32, ph)
            rr, rr_b = sb("drr", [128, 16], F32, ph)
            yo = [sb("yo%d" % i, [128, D], F32, ph) for i in range(2)]
            out_b = Buf()
            for ti in range(NT):
                y_t, y_b = yo[ti % 2]
                op("act", lambda e: e.activation(out=junk[:, :], in_=hres[:, ti, :], func=AF.Square, accum_out=ss[:, ti:ti + 1]),
                   reads=[hres_b], writes=[junk_b, ss_b])
                op("dve", lambda e: e.tensor_scalar(out=rr[:, ti:ti + 1], in0=ss[:, ti:ti + 1], scalar1=1.0 / D, scalar2=EPS,
                                                    op0=ALU.mult, op1=ALU.add), reads=[ss_b], writes=[rr_b])
                op("act", lambda e: e.activation(out=rr[:, ti:ti + 1], in_=rr[:, ti:ti + 1], func=AF.Ln), reads=[rr_b], writes=[rr_b])
                op("act", lambda e: e.activation(out=rr[:, ti:ti + 1], in_=rr[:, ti:ti + 1], func=AF.Exp, scale=-0.5),
                   reads=[rr_b], writes=[rr_b])
                op("dve", lambda e: e.scalar_tensor_tensor(out=y_t[:, :], in0=hres[:, ti, :], scalar=rr[:, ti:ti + 1],
                                                           in1=gfb[:, :], op0=ALU.mult, op1=ALU.mult),
                   reads=[hres_b, rr_b, gfb_b], writes=[y_b])
                dma("sp", out_d[ti * 128:(ti + 1) * 128, :], y_t[:, :], reads=[y_b], writes=[out_b])
        kb.barrier(final=True)
        es_h.close()
    return nc


_CACHE = {}


def kernel(x, mem, g_mix, w_in, w_alpha_up, b_alpha, b_forget, g_gla_head, g_mem, w_mem_kv,
           w_gla_o, w_fox_o, w_mem_o, w_out, g_ffn, w_ff1, w_ff2, g_final):
    f = lambda a: np.ascontiguousarray(np.asarray(a, dtype=np.float32))
    if "nc" not in _CACHE:
        _CACHE["nc"] = build_nc()
        _CACHE["c"] = _consts()
    nc = _CACHE["nc"]
    cf, cb = _CACHE["c"]
    xs = f(x).reshape(S, D)
    shared = {
        "mem": f(mem).reshape(256, D), "g_mix": f(g_mix).reshape(1, D), "w_in": f(w_in).reshape(D, D_IN),
        "w_alpha_up": f(w_alpha_up).reshape(16, 256), "b_alpha": f(b_alpha).reshape(1, 256),
        "b_forget": f(b_forget).reshape(8, 1), "g_gla_head": f(g_gla_head).reshape(4, 128),
        "g_mem": f(g_mem).reshape(1, D), "w_mem_kv": f(w_mem_kv).reshape(D, 1024),
        "w_gla_o": f(w_gla_o).reshape(512, D), "w_fox_o": f(w_fox_o).reshape(512, D),
        "w_mem_o": f(w_mem_o).reshape(512, D), "w_out": f(w_out).reshape(D, D),
        "g_ffn": f(g_ffn).reshape(1, D), "w_ff1": f(w_ff1).reshape(D, 4096), "w_ff2": f(w_ff2).reshape(4096, D),
        "g_final": f(g_final).reshape(1, D), "cf": cf, "cb": cb,
    }
    big = ("mem", "w_in", "w_mem_kv", "w_gla_o", "w_fox_o", "w_mem_o", "w_out", "w_ff1", "w_ff2", "cf")
    in_maps = []
    for c in range(NCORE):
        m = dict(shared)
        for k in big:
            a = shared[k]
            m[k] = np.concatenate([a, np.full((1, a.shape[1]), float(c), a.dtype)], axis=0)
        m["x"] = xs[c * T:(c + 1) * T]
        m["cid"] = np.array([[c, c * T]], np.int32)
        m["cmask"] = np.tile((np.arange(8) < c).astype(np.float32)[None, :], (128, 1))
        in_maps.append(m)
    res = run_bass_kernel_spmd(nc, in_maps, core_ids=list(range(NCORE)))
    out = np.concatenate([np.asarray(r["out"], dtype=np.float32) for r in res.results], axis=0)
    return out.reshape(1, S, D)
```
